# Optimizing a Trainium2 kernel written in Bass

```python
import math
import jax
import jax.numpy as jnp
from jax import lax
import numpy as np

D_MODEL = 1024
BATCH = 2
SEQ = 8192
DEPTH = 2

EPS = 1e-6
GM_WIDTH = 512
GM_GROUPS = 4
GM_GROUP_DIM = GM_WIDTH // GM_GROUPS
GM_CHUNK = 128
SSM_INNER = 1024
SSM_HEAD_DIM = 64
SSM_HEADS = SSM_INNER // SSM_HEAD_DIM
SSM_GROUPS = 2
SSM_STATE = 128
SSM_CONV = 4
SSM_CONV_DIM = SSM_INNER + 2 * SSM_GROUPS * SSM_STATE
SSM_CHUNK = 128
DA_HEADS = 4
DA_HEAD_DIM = 64
DA_V_DIM = 2 * DA_HEAD_DIM
DA_QK_WIDTH = DA_HEADS * 2 * DA_HEAD_DIM
DA_WIDTH = DA_HEADS * DA_V_DIM
ROPE_DIM = DA_HEAD_DIM // 4
ROPE_THETA = 500000.0
Q_BLOCK = 128
N_BRANCH = 3
IN_SIZES = (GM_WIDTH, GM_WIDTH, SSM_INNER, SSM_CONV_DIM, SSM_HEADS, DA_QK_WIDTH, DA_QK_WIDTH, DA_WIDTH, N_BRANCH * D_MODEL)
IN_DIM = sum(IN_SIZES)
FFN_DIM = 2816
FFN_CONV = 3

kernel_name = 'hybrid_gmlp_ssd_diffattn_block'


def rmsnorm(x, g):
    xf = x.astype(jnp.float32)
    y = xf * lax.rsqrt(jnp.mean(xf * xf, axis=-1, keepdims=True) + EPS)
    return (y * g.astype(jnp.float32)).astype(x.dtype)


def causal_dwconv(x, w, b):
    k = w.shape[1]
    rhs = jnp.transpose(w)[:, None, :].astype(x.dtype)
    y = lax.conv_general_dilated(x, rhs, window_strides=(1,), padding=[(k - 1, 0)],
                                 dimension_numbers=('NWC', 'WIO', 'NWC'),
                                 feature_group_count=x.shape[-1])
    return y + b.astype(x.dtype)


def split_cols(t, sizes):
    idx = [int(i) for i in np.cumsum(sizes)[:-1]]
    return jnp.split(t, idx, axis=-1)


def chunked_spatial_gating(u, v, v_gain, w_s, b_s):
    bsz, s, _ = v.shape
    n = s // GM_CHUNK
    vn = rmsnorm(v, v_gain).reshape(bsz, n, GM_CHUNK, GM_GROUPS, GM_GROUP_DIM)
    causal = jnp.tril(jnp.ones((GM_CHUNK, GM_CHUNK), dtype=bool))
    w = jnp.where(causal[None], w_s, jnp.zeros_like(w_s))
    mixed = jnp.einsum('gts,bnsgc->bntgc', w, vn) + jnp.transpose(b_s)[None, None, :, :, None]
    return u * mixed.reshape(bsz, s, GM_WIDTH)


def ssd_scan(xs, dt, a_log, bm, cm, d_skip):
    bsz, s = xs.shape[:2]
    nc, q = s // SSM_CHUNK, SSM_CHUNK
    hpg = SSM_HEADS // SSM_GROUPS
    f32 = jnp.float32
    x = xs.astype(f32).reshape(bsz, nc, q, SSM_GROUPS, hpg, SSM_HEAD_DIM)
    dtc = dt.astype(f32).reshape(bsz, nc, q, SSM_GROUPS, hpg)
    bc = bm.astype(f32).reshape(bsz, nc, q, SSM_GROUPS, SSM_STATE)
    cc = cm.astype(f32).reshape(bsz, nc, q, SSM_GROUPS, SSM_STATE)
    a = -jnp.exp(a_log.astype(f32)).reshape(SSM_GROUPS, hpg)
    a_cum = jnp.cumsum(dtc * a, axis=2)
    xdt = x * dtc[..., None]
    causal = jnp.tril(jnp.ones((q, q), dtype=bool))[:, :, None, None]
    seg = a_cum[:, :, :, None] - a_cum[:, :, None, :]
    decay = jnp.exp(jnp.where(causal, seg, -jnp.inf))
    cb = jnp.einsum('bctgn,bcsgn->bctsg', cc, bc)
    y_diag = jnp.einsum('bctsgh,bcsghp->bctghp', cb[..., None] * decay, xdt)
    decay_end = jnp.exp(a_cum[:, :, -1:] - a_cum)
    states = jnp.einsum('bcsgn,bcsgh,bcsghp->bcghpn', bc, decay_end, xdt)
    chunk_decay = jnp.exp(a_cum[:, :, -1])

    def step(h, inp):
        st, dec = inp
        return h * dec[..., None, None] + st, h

    h0 = jnp.zeros((bsz, SSM_GROUPS, hpg, SSM_HEAD_DIM, SSM_STATE), f32)
    _, prev = lax.scan(step, h0, (jnp.moveaxis(states, 1, 0), jnp.moveaxis(chunk_decay, 1, 0)))
    prev = jnp.moveaxis(prev, 0, 1)
    y_off = jnp.einsum('bctgn,bcghpn,bctgh->bctghp', cc, prev, jnp.exp(a_cum))
    y = y_diag + y_off + d_skip.astype(f32).reshape(SSM_GROUPS, hpg)[..., None] * x
    return y.reshape(bsz, s, SSM_INNER).astype(xs.dtype)


def mamba2_mixer(z, xbc, dt_raw, conv_w, conv_b, dt_bias, a_log, d_skip, norm_g):
    bsz, s = z.shape[:2]
    xbc = jax.nn.silu(causal_dwconv(xbc, conv_w, conv_b))
    xs, bm, cm = split_cols(xbc, (SSM_INNER, SSM_GROUPS * SSM_STATE, SSM_GROUPS * SSM_STATE))
    dt = jax.nn.softplus(dt_raw.astype(jnp.float32) + dt_bias.astype(jnp.float32))
    y = ssd_scan(xs.reshape(bsz, s, SSM_HEADS, SSM_HEAD_DIM), dt, a_log,
                 bm.reshape(bsz, s, SSM_GROUPS, SSM_STATE),
                 cm.reshape(bsz, s, SSM_GROUPS, SSM_STATE), d_skip)
    return rmsnorm(y * jax.nn.silu(z), norm_g)


def rope_tables(s):
    pos = jnp.arange(s, dtype=jnp.float32)
    inv_freq = 1.0 / (ROPE_THETA ** (jnp.arange(0, ROPE_DIM, 2, dtype=jnp.float32) / ROPE_DIM))
    ang = pos[:, None] * inv_freq[None, :]
    return jnp.cos(ang), jnp.sin(ang)


def partial_rope(t, cos, sin):
    half = ROPE_DIM // 2
    c = cos[None, :, None, None, :]
    sn = sin[None, :, None, None, :]
    t1 = t[..., :half]
    t2 = t[..., half:ROPE_DIM]
    return jnp.concatenate([t1 * c - t2 * sn, t2 * c + t1 * sn, t[..., ROPE_DIM:]], axis=-1)


def diff_attention(q, k, v, q_g, k_g, lam_vecs, subln_g, lam_init, cos, sin):
    bsz, s = q.shape[:2]
    f32 = jnp.float32
    qh = rmsnorm(q.reshape(bsz, s, DA_HEADS, 2, DA_HEAD_DIM), q_g).astype(f32)
    kh = rmsnorm(k.reshape(bsz, s, DA_HEADS, 2, DA_HEAD_DIM), k_g).astype(f32)
    qh = partial_rope(qh, cos, sin) * (DA_HEAD_DIM ** -0.5)
    kh = partial_rope(kh, cos, sin)
    vf = v.reshape(bsz, s, DA_HEADS, DA_V_DIM).astype(f32)
    lv = lam_vecs.astype(f32)
    lam = jnp.exp(jnp.sum(lv[0] * lv[1])) - jnp.exp(jnp.sum(lv[2] * lv[3])) + lam_init
    nb = s // Q_BLOCK
    q_blocks = jnp.moveaxis(qh.reshape(bsz, nb, Q_BLOCK, DA_HEADS, 2, DA_HEAD_DIM), 1, 0)
    k_pos = jnp.arange(s)

    def block(args):
        qb, bi = args
        scores = jnp.einsum('bqhmd,bkhmd->bhmqk', qb, kh)
        q_pos = bi * Q_BLOCK + jnp.arange(Q_BLOCK)
        mask = k_pos[None, :] <= q_pos[:, None]
        p = jax.nn.softmax(jnp.where(mask, scores, -jnp.inf), axis=-1)
        w = p[:, :, 0] - lam * p[:, :, 1]
        return jnp.einsum('bhqk,bkhd->bqhd', w, vf)

    out = lax.map(block, (q_blocks, jnp.arange(nb)))
    out = jnp.moveaxis(out, 0, 1).reshape(bsz, s, DA_HEADS, DA_V_DIM)
    out = rmsnorm(out, subln_g) * (1.0 - lam_init)
    return out.reshape(bsz, s, DA_WIDTH).astype(v.dtype)


def setup_inputs(seed: int = 0) -> dict:
    key = jax.random.key(seed)
    ks = jax.random.split(key, 32)
    f32 = jnp.float32
    L = DEPTH

    def nrm(k, shape, scale):
        return jax.random.normal(k, shape, f32) * scale

    def gain(k, shape):
        return 1.0 + 0.05 * jax.random.normal(k, shape, f32)

    dt_init = jnp.exp(jax.random.uniform(ks[9], (L, SSM_HEADS), f32, math.log(1e-3), math.log(1e-1)))
    return {
        'x': nrm(ks[0], (BATCH, SEQ, D_MODEL), 1.0),
        'attn_norm_g': gain(ks[1], (L, D_MODEL)),
        'w_in': nrm(ks[2], (L, D_MODEL, IN_DIM), D_MODEL ** -0.5),
        'gm_v_norm_g': gain(ks[3], (L, GM_WIDTH)),
        'gm_w_s': nrm(ks[4], (L, GM_GROUPS, GM_CHUNK, GM_CHUNK), GM_CHUNK ** -0.5),
        'gm_b_s': 1.0 + 0.1 * jax.random.normal(ks[5], (L, GM_GROUPS, GM_CHUNK), f32),
        'ssm_conv_w': nrm(ks[6], (L, SSM_CONV_DIM, SSM_CONV), SSM_CONV ** -0.5),
        'ssm_conv_b': nrm(ks[7], (L, SSM_CONV_DIM), 0.02),
        'ssm_dt_bias': dt_init + jnp.log(-jnp.expm1(-dt_init)),
        'ssm_a_log': jnp.log(jax.random.uniform(ks[10], (L, SSM_HEADS), f32, 1.0, 16.0)),
        'ssm_d': gain(ks[11], (L, SSM_HEADS)),
        'ssm_norm_g': gain(ks[12], (L, SSM_INNER)),
        'da_q_norm_g': gain(ks[13], (L, DA_HEAD_DIM)),
        'da_k_norm_g': gain(ks[14], (L, DA_HEAD_DIM)),
        'da_lambda': nrm(ks[15], (L, 4, DA_HEAD_DIM), 0.1),
        'da_subln_g': gain(ks[16], (L, DA_V_DIM)),
        'w_branch_a': nrm(ks[17], (L, GM_WIDTH, D_MODEL), GM_WIDTH ** -0.5),
        'w_branch_b': nrm(ks[18], (L, SSM_INNER, D_MODEL), SSM_INNER ** -0.5),
        'w_branch_c': nrm(ks[19], (L, DA_WIDTH, D_MODEL), DA_WIDTH ** -0.5),
        'w_out': nrm(ks[20], (L, D_MODEL, D_MODEL), D_MODEL ** -0.5),
        'ffn_norm_g': gain(ks[21], (L, D_MODEL)),
        'ffn_w_up': nrm(ks[22], (L, D_MODEL, 2 * FFN_DIM), D_MODEL ** -0.5),
        'ffn_conv_w': nrm(ks[23], (L, 2 * FFN_DIM, FFN_CONV), FFN_CONV ** -0.5),
        'ffn_conv_b': nrm(ks[24], (L, 2 * FFN_DIM), 0.02),
        'ffn_w_down': nrm(ks[25], (L, FFN_DIM, D_MODEL), FFN_DIM ** -0.5),
    }


def reference(x, attn_norm_g, w_in, gm_v_norm_g, gm_w_s, gm_b_s, ssm_conv_w, ssm_conv_b,
              ssm_dt_bias, ssm_a_log, ssm_d, ssm_norm_g, da_q_norm_g, da_k_norm_g, da_lambda,
              da_subln_g, w_branch_a, w_branch_b, w_branch_c, w_out, ffn_norm_g, ffn_w_up,
              ffn_conv_w, ffn_conv_b, ffn_w_down):
    bsz, s, _ = x.shape
    cos, sin = rope_tables(s)
    for layer in range(DEPTH):
        lam_init = 0.8 - 0.6 * math.exp(-0.3 * layer)
        h = rmsnorm(x, attn_norm_g[layer])
        proj = h @ w_in[layer]
        gm_u, gm_v, ssm_z, ssm_xbc, ssm_dt, da_q, da_k, da_v, gates = split_cols(proj, IN_SIZES)
        y_a = chunked_spatial_gating(jax.nn.gelu(gm_u), jax.nn.gelu(gm_v), gm_v_norm_g[layer],
                                     gm_w_s[layer], gm_b_s[layer])
        y_b = mamba2_mixer(ssm_z, ssm_xbc, ssm_dt, ssm_conv_w[layer], ssm_conv_b[layer],
                           ssm_dt_bias[layer], ssm_a_log[layer], ssm_d[layer], ssm_norm_g[layer])
        y_c = diff_attention(da_q, da_k, da_v, da_q_norm_g[layer], da_k_norm_g[layer],
                             da_lambda[layer], da_subln_g[layer], lam_init, cos, sin)
        g = jax.nn.sigmoid(gates).reshape(bsz, s, N_BRANCH, D_MODEL)
        merged = (g[:, :, 0] * (y_a @ w_branch_a[layer])
                  + g[:, :, 1] * (y_b @ w_branch_b[layer])
                  + g[:, :, 2] * (y_c @ w_branch_c[layer]))
        x = x + merged @ w_out[layer]
        h = rmsnorm(x, ffn_norm_g[layer])
        up = causal_dwconv(h @ ffn_w_up[layer], ffn_conv_w[layer], ffn_conv_b[layer])
        gate, val = jnp.split(up, 2, axis=-1)
        x = x + (jax.nn.silu(gate) * val) @ ffn_w_down[layer]
    return x
```

```python
import math
import numpy as np
from contextlib import ExitStack
import ml_dtypes
import concourse.bass as bass
import concourse.mybir as mybir
from concourse.bass_utils import run_bass_kernel_spmd

F32 = mybir.dt.float32
BF16 = mybir.dt.bfloat16
AF = mybir.ActivationFunctionType
ALU = mybir.AluOpType
AX = mybir.AxisListType
NPBF = ml_dtypes.bfloat16

D = 1024
S = 8192
NCORE = 8
T = 2048
NCH = 16
EPS = 1e-6
FFN = 2816


class Ctx:
    NDMA = 8

    def __init__(self, nc, es):
        self.nc = nc
        self.es = es
        self.top = es
        self.eng = {"pe": nc.tensor, "act": nc.scalar, "dve": nc.vector,
                    "pool": nc.gpsimd, "sp": nc.sync}
        self.sems = []
        self.esem = {}
        self.cnt = {}
        self.known = {e: {} for e in self.eng}
        for e in self.eng:
            self.esem[e] = self._newsem("s_" + e)
            self.cnt[e] = 0
        self.dsem, self.dval, self.drr = {}, {}, {}
        for q in ("sp", "pool", "act"):
            self.dsem[q] = [self._newsem(f"d_{q}{i}") for i in range(self.NDMA)]
            self.dval[q] = [0] * self.NDMA
            self.drr[q] = 0
        self.lastw = {}
        self.readers = {}
        self.n_inst = 0

    def _newsem(self, name):
        s = self.top.enter_context(self.nc.semaphore(name))
        self.sems.append(s)
        return len(self.sems) - 1

    def sb(self, name, shape, dt):
        self.n_alloc = getattr(self, "n_alloc", 0) + 1
        return self.es.enter_context(self.nc.sbuf_tensor(f"{name}_{self.n_alloc}", shape, dt))

    def ps(self, name, shape, dt):
        self.n_alloc = getattr(self, "n_alloc", 0) + 1
        return self.es.enter_context(self.nc.psum_tensor(f"{name}_{self.n_alloc}", shape, dt))

    def op16(self, e, fn, reads=(), writes=()):
        self._deps(e, reads, writes)
        i = self.drr[e]
        self.drr[e] = (i + 1) % self.NDMA
        s = self.dsem[e][i]
        v = self.dval[e][i]
        if v > 0 and self.known[e].get(s, 0) < v:
            self.eng[e].wait_ge(self.sems[s], v)
            self.known[e][s] = v
        inst = fn(self.eng[e])
        inst.then_inc(self.sems[s], 16)
        self.dval[e][i] = v + 16
        self._record((s, v + 16), reads, writes)
        self.n_inst += 1
        return inst

    def allgather(self, src, dst, skeys, dkeys):
        self._deps("pool", skeys, dkeys)
        inst = self.nc.gpsimd.collective_compute("AllGather", ALU.bypass,
                                                 replica_groups=[[0, 1, 2, 3], [4, 5, 6, 7]],
                                                 ins=[src], outs=[dst])
        s = self._newsem(f"cc{len(self.sems)}")
        inst.then_inc(self.sems[s], 1)
        self._record((s, 1), skeys, dkeys)
        self.n_inst += 1
        return inst

    def _deps(self, e, reads, writes):
        deps = {}
        own = self.esem.get(e)

        def add(tok, raw):
            if tok is None:
                return
            s, v = tok
            if deps.get(s, 0) < v:
                deps[s] = v
        for k in reads:
            add(self.lastw.get(k), True)
        for k in writes:
            add(self.lastw.get(k), False)
            for s, v in self.readers.get(k, {}).items():
                add((s, v), False)
        eng = self.eng[e]
        for s, v in deps.items():
            if e == "pe" and s == self.esem["pe"]:
                continue
            if self.known[e].get(s, 0) < v:
                eng.wait_ge(self.sems[s], v)
                self.known[e][s] = v

    def _record(self, tok, reads, writes):
        s, v = tok
        for k in reads:
            r = self.readers.setdefault(k, {})
            if r.get(s, 0) < v:
                r[s] = v
        for k in writes:
            self.lastw[k] = tok
            self.readers[k] = {}

    def op(self, e, fn, reads=(), writes=()):
        self._deps(e, reads, writes)
        inst = fn(self.eng[e])
        self.cnt[e] += 1
        inst.then_inc(self.sems[self.esem[e]], 1)
        self._record((self.esem[e], self.cnt[e]), reads, writes)
        self.n_inst += 1
        return inst

    def dma(self, q, out, in_, reads=(), writes=(), **kw):
        self._deps(q, reads, writes)
        eng = self.eng[q]
        i = self.drr[q]
        self.drr[q] = (i + 1) % self.NDMA
        s = self.dsem[q][i]
        v = self.dval[q][i]
        if v > 0 and self.known[q].get(s, 0) < v:
            eng.wait_ge(self.sems[s], v)
            self.known[q][s] = v
        inst = eng.dma_start(out=out, in_=in_, **kw)
        inst.then_inc(self.sems[s], 16)
        self.dval[q][i] = v + 16
        self._record((s, v + 16), reads, writes)
        self.n_inst += 1
        return inst

    def barrier(self):
        toks = [(self.esem[e], self.cnt[e]) for e in self.eng if self.cnt[e] > 0]
        for q in self.dsem:
            for i in range(self.NDMA):
                if self.dval[q][i] > 0:
                    toks.append((self.dsem[q][i], self.dval[q][i]))
        for e in self.eng:
            for s_, v in toks:
                if e == "pe" and s_ == self.esem["pe"]:
                    continue
                if self.known[e].get(s_, 0) < v:
                    self.eng[e].wait_ge(self.sems[s_], v)
                    self.known[e][s_] = v

    def new_epoch(self):
        self.barrier()
        for e in self.eng:
            self.esem[e] = self._newsem(f"s_{e}_{len(self.sems)}")
            self.cnt[e] = 0

    def finish(self):
        for q in self.dsem:
            for i in range(self.NDMA):
                if self.dval[q][i] > 0:
                    self.nc.sync.wait_ge(self.sems[self.dsem[q][i]], self.dval[q][i])

    def mm(self, out, lhsT, rhs, start, stop, reads, writes, **kw):
        return self.op("pe", lambda E: E.matmul(out, lhsT, rhs, start=start, stop=stop, **kw),
                       reads, writes)

    def tr(self, out, in_, ident, reads, writes):
        return self.op("pe", lambda E: E.transpose(out, in_, ident), reads, writes)

    def act(self, out, in_, func, reads, writes, **kw):
        return self.op("act", lambda E: E.activation(out=out, in_=in_, func=func, **kw), reads, writes)

    def tt(self, e, out, in0, in1, op, reads, writes):
        return self.op(e, lambda E: E.tensor_tensor(out=out, in0=in0, in1=in1, op=op), reads, writes)

    def ts(self, e, out, in0, s1, s2, op0, op1, reads, writes):
        if op1 is None:
            return self.op(e, lambda E: E.tensor_scalar(out=out, in0=in0, scalar1=s1, scalar2=None, op0=op0),
                           reads, writes)
        return self.op(e, lambda E: E.tensor_scalar(out=out, in0=in0, scalar1=s1, scalar2=s2, op0=op0, op1=op1),
                       reads, writes)

    def stt(self, out, in0, scalar, in1, op0, op1, reads, writes):
        return self.op("dve", lambda E: E.scalar_tensor_tensor(out=out, in0=in0, scalar=scalar, in1=in1,
                                                              op0=op0, op1=op1), reads, writes)

    def copy(self, e, out, in_, reads, writes):
        if e == "act":
            return self.act(out, in_, AF.Copy, reads, writes)
        return self.op(e, lambda E: E.tensor_copy(out=out, in_=in_), reads, writes)


def make_ident(c, ident):
    c.op("pool", lambda E: E.memset(ident[:], 1.0), writes=["ident"])
    c.op("pool", lambda E: E.affine_select(out=ident[:], in_=ident[:], pattern=[[-1, 128]],
                                           compare_op=ALU.is_equal, fill=0.0, base=0,
                                           channel_multiplier=1), reads=["ident"], writes=["ident"])


def build_hT(c, x_ext, g_sb, hT, nch, ident, pT_bank, pfx="h"):
    pT, pkey = pT_bank
    xt = [c.sb(f"{pfx}_xt{i}", [128, D], F32) for i in range(2)]
    xs = [c.sb(f"{pfx}_xs{i}", [128, D], BF16) for i in range(2)]
    junk = c.sb(f"{pfx}_junk", [128, D], BF16)
    ss = [c.sb(f"{pfx}_ss{i}", [128, 4], F32) for i in range(2)]
    for ch in range(nch):
        s = ch % 2
        c.dma("sp", xt[s][:], x_ext[ch * 128:(ch + 1) * 128, :], writes=[(pfx + "xt", s)])
        c.act(junk[:], xt[s][:], AF.Square, [(pfx + "xt", s)], [pfx + "junk", (pfx + "ss", s)],
              accum_out=ss[s][:, 0:1])
        c.act(ss[s][:, 1:2], ss[s][:, 0:1], AF.Ln, [(pfx + "ss", s), "eps"], [(pfx + "ss", s)],
              scale=1.0 / D, bias=EPS_AP[0])
        c.act(ss[s][:, 2:3], ss[s][:, 1:2], AF.Exp, [(pfx + "ss", s)], [(pfx + "ss", s)], scale=-0.5)
        c.act(xs[s][:], xt[s][:], AF.Copy, [(pfx + "xt", s), (pfx + "ss", s)], [(pfx + "xs", s)],
              scale=ss[s][:, 2:3])
        for k in range(8):
            c.tr(pT[:, k * 128:(k + 1) * 128], xs[s][:, k * 128:(k + 1) * 128], ident[:],
                 [(pfx + "xs", s), "ident"], [pkey])
        c.tt("dve", hT[:, :, ch * 128:(ch + 1) * 128],
             pT.rearrange("p (k t) -> p k t", k=8),
             g_sb[:, :].unsqueeze(2).to_broadcast([128, 8, 128]), ALU.mult,
             [pkey, "g_sb"], [("hT", ch)])


EPS_AP = [None]


def setup_consts(c):
    eps = c.sb("eps_t", [128, 1], F32)
    c.op("pool", lambda E: E.memset(eps[:], EPS), writes=["eps"])
    EPS_AP[0] = eps[:, 0:1]
    one = c.sb("one_t", [128, 1], F32)
    c.op("pool", lambda E: E.memset(one[:], 1.0), writes=["eps"])
    ONE_AP[0] = one[:, 0:1]
    ident = c.sb("ident", [128, 128], BF16)
    make_ident(c, ident)
    return ident


def emit_ffn(c, nc, io):
    x_ext, g_in, w_up, cw_in, cb_in, w_dn, x_out = (io[k] for k in ("x_ext", "g", "w_up", "cw", "cb", "w_dn", "x_out"))
    halo_out = io.get("halo_out")
    with Section(c):
        ident = setup_consts(c)
        banks = [c.ps(f"bank{i}", [128, 512], F32) for i in range(8)]
        g_sb = c.sb("g_sb", [128, 8], F32)
        cw = c.sb("cw_sb", [128, 44, 3], F32)
        cb = c.sb("cb_sb", [128, 44], F32)
        c.dma("sp", g_sb[:], g_in, writes=["g_sb"])
        c.dma("sp", cw[:], cw_in, writes=["cw"])
        c.dma("sp", cb[:], cb_in, writes=["cb"])
        hT = c.sb("hT", [128, 8, (NCH + 1) * 128], BF16)
        wd = c.sb("wd", [128, 22, D], BF16)
        c.dma("pool", wd[:], w_dn.rearrange("(i p) n -> p i n", p=128), writes=["wd"])
        build_hT(c, x_ext, g_sb, hT, NCH + 1, ident, (banks[7][:].bitcast(BF16), "bank7"))
        allhT = [("hT", ch) for ch in range(NCH + 1)]
        actT = c.sb("actT", [128, 22, 1024], BF16)
        wu = [c.sb(f"wu{i}", [128, 2, 8, 128], BF16) for i in range(2)]
        acc = [[c.sb(f"acc{gv}{i}", [128, 512], F32) for i in range(2)] for gv in range(2)]
        sg = [c.sb(f"sg{i}", [128, 512], F32) for i in range(2)]
        xbuf = [[c.sb(f"xbuf{gv}{i}", [128, 514], BF16) for i in range(2)] for gv in range(2)]
        pend = None
        xr = [c.sb(f"xr{i}", [128, D], F32) for i in range(2)]
        xo = [c.sb(f"xo{i}", [128, D], F32) for i in range(2)]
        w_up_v = w_up.rearrange("(k p) n -> p k n", p=128)
        un = 0
        for ps_ in range(2):
            for i in range(22):
                s = (ps_ * 22 + i) % 2
                for gv in range(2):
                    col0 = gv * FFN + i * 128
                    c.dma("pool", wu[s][:, gv], w_up_v[:, :, col0:col0 + 128], writes=[("wu", s)])
                for tl in range(2):
                    t0 = ps_ * 1024 + tl * 512
                    c0 = 128 + t0
                    b2 = un % 2
                    un += 1
                    for gv in range(2):
                        fb = gv * 22 + i
                        pk = ("pg", gv, b2)
                        pt = banks[gv * 2 + b2]
                        xb = xbuf[gv][b2]
                        xk = ("xb", gv, b2)
                        if tl == 0:
                            hk = "bank6"
                            ph = banks[6][:, gv * 2:gv * 2 + 2]
                            for k in range(8):
                                c.mm(ph, wu[s][:, gv, k, :], hT[:, k, c0 - 2:c0], k == 0, k == 7,
                                     [("wu", s)] + allhT, [hk])
                            c.copy("act", xb[:, 0:2], ph, [hk], [xk])
                        else:
                            c.copy("act", xb[:, 0:2], xbuf[gv][1 - b2][:, 512:514], [("xb", gv, 1 - b2)], [xk])
                        for k in range(8):
                            c.mm(pt[:], wu[s][:, gv, k, :], hT[:, k, c0:c0 + 512], k == 0, k == 7,
                                 [("wu", s)] + allhT, [pk])
                        a = acc[gv][b2]
                        ak = ("acc", gv, b2)
                        c.act(xb[:, 2:514], pt[:], AF.Copy, [pk], [xk])
                        c.act(a[:], pt[:], AF.Identity, [pk, "cw", "cb"], [ak],
                              scale=cw[:, fb, 2:3], bias=cb[:, fb:fb + 1])
                        c.stt(a[:], xb[:, 1:513], cw[:, fb, 1:2], a[:], ALU.mult, ALU.add, [xk, ak, "cw"], [ak])
                        c.stt(a[:], xb[:, 0:512], cw[:, fb, 0:1], a[:], ALU.mult, ALU.add, [xk, ak, "cw"], [ak])
                    if pend is not None:
                        pend()
                    def tail(b2=b2, i=i, tl=tl):
                        c.act(sg[b2][:], acc[0][b2][:], AF.Silu, [("acc", 0, b2)], [("sg", b2)])
                        c.tt("dve", actT[:, i, tl * 512:(tl + 1) * 512], sg[b2][:], acc[1][b2][:], ALU.mult,
                             [("sg", b2), ("acc", 1, b2)], [("actT", i)])
                    pend = tail
            pend()
            pend = None
            allact = [("actT", i) for i in range(22)]
            for ch in range(8):
                gch = ps_ * 8 + ch
                s = gch % 2
                c.dma("sp", xr[s][:], x_ext[(gch + 1) * 128:(gch + 2) * 128, :], writes=[("xr", s)])
                for hf in range(2):
                    pd = banks[4 + hf]
                    for i in range(22):
                        c.mm(pd[:], actT[:, i, ch * 128:(ch + 1) * 128], wd[:, i, hf * 512:(hf + 1) * 512],
                             i == 0, i == 21, allact + ["wd"], [("pd", hf)])
                    c.tt("dve", xo[s][:, hf * 512:(hf + 1) * 512], pd[:], xr[s][:, hf * 512:(hf + 1) * 512],
                         ALU.add, [("pd", hf), ("xr", s)], [("xo", s)])
                c.dma("sp", x_out[gch * 128:(gch + 1) * 128, :], xo[s][:], reads=[("xo", s)], writes=["d_xout"])
                if halo_out is not None and gch == NCH - 1:
                    c.dma("sp", halo_out, xo[s][120:128, :], reads=[("xo", s)], writes=["d_halo"])


class Section:
    def __init__(self, c):
        self.c = c

    def __enter__(self):
        self.old = self.c.es
        self.st = ExitStack()
        self.st.__enter__()
        self.c.es = self.st
        return self

    def __exit__(self, *a):
        self.c.barrier()
        self.c.es = self.old
        return self.st.__exit__(*a)


def gelu_tanh(c, src, skeys, out, okeys, W, sl, n):
    xs, sq, t = W["gx"][sl], W["gs"][sl], W["gt"][sl]
    kx, ks, kt = ("gx", sl), ("gs", sl), ("gt", sl)
    c.act(xs[:, :n], src, AF.Copy, skeys, [kx])
    c.tt("dve", sq[:, :n], xs[:, :n], src, ALU.mult, [kx] + list(skeys), [ks])
    c.ts("dve", t[:, :n], sq[:, :n], 0.044715, 1.0, ALU.mult, ALU.add, [ks], [kt])
    c.tt("dve", sq[:, :n], t[:, :n], xs[:, :n], ALU.mult, [kt, kx], [ks])
    c.act(t[:, :n], sq[:, :n], AF.Sigmoid, [ks], [kt], scale=1.5957691216057308)
    c.tt("dve", out, t[:, :n], xs[:, :n], ALU.mult, [kt, kx], okeys)


def gelu_work(c):
    return {k: [c.sb(f"{k}{i}", [128, 512], F32) for i in range(2)] for k in ("gx", "gs", "gt")}


def rstd_from_ss(c, ss, key, n_inv, w=1):
    c.act(ss[:, w:2 * w], ss[:, 0:w], AF.Ln, [key, "eps"], [key], scale=n_inv, bias=EPS_AP[0])
    c.act(ss[:, 2 * w:3 * w], ss[:, w:2 * w], AF.Exp, [key], [key], scale=-0.5)


def emit_front(c, nc, io):
    (x_ext, g_in, w_u, w_v, w_z, w_xbc, w_dt, w_q, w_k, w_v2, w_g, vgain, wsT_in, bs_in, cw_in, cb_in, dtb_in,
     alog_in, dsk_in, qg_in, kg_in, cos_in, sin_in) = (io[k] for k in (
        "x_ext", "g", "w_u", "w_v", "w_z", "w_xbc", "w_dt", "w_q", "w_k", "w_v2", "w_g", "vgain", "wsT", "bs",
        "cw", "cb", "dtb", "alog", "dsk", "qg", "kg", "cos", "sin"))
    yaT_o, zs_o, yloc_o, eacg_o, CT_o, gT_o = (io[k] for k in ("yaT", "zs", "yloc", "eacg", "CT", "gT"))
    st_send = io["st_send"]
    q_send, k_send, v_send = io["q_send"], io["k_send"], io["v_send"]

    def wview(w):
        return w.rearrange("(k p) n -> p k n", p=128)

    with Section(c):
        ident = setup_consts(c)
        banks = [c.ps(f"bank{i}", [128, 512], F32) for i in range(8)]
        BK = lambda i: ("bank", i)
        g_sb = c.sb("g_sb", [128, 8], F32)
        c.dma("sp", g_sb[:], g_in, writes=["g_sb"])
        hT = c.sb("hT", [128, 8, (NCH + 1) * 128], BF16)
        build_hT(c, x_ext, g_sb, hT, NCH + 1, ident, (banks[7][:].bitcast(BF16), BK(7)))
        allhT = [("hT", ch) for ch in range(NCH + 1)]
        dt_all = c.sb("dt_all", [128, NCH, 16], F32)
        xbcT = c.sb("xbcT", [128, 12, T], BF16)

        with Section(c):
            W = gelu_work(c)
            uT = c.sb("uT", [128, 4, T], BF16)
            wblk = [c.sb(f"wblk{i}", [128, 8, 128], BF16) for i in range(2)]
            un = 0
            for fb in range(4):
                s = fb % 2
                c.dma("pool", wblk[s][:], wview(w_u)[:, :, fb * 128:(fb + 1) * 128], writes=[("wblk", s)])
                for tl in range(4):
                    b = un % 2
                    un += 1
                    for k in range(8):
                        c.mm(banks[b][:], wblk[s][:, k, :], hT[:, k, 128 + tl * 512:128 + (tl + 1) * 512],
                             k == 0, k == 7, [("wblk", s)] + allhT, [BK(b)])
                    gelu_tanh(c, banks[b][:], [BK(b)], uT[:, fb, tl * 512:(tl + 1) * 512], [("uT", fb)], W, b, 512)
            alluT = [("uT", fb) for fb in range(4)]
            ws_f = c.sb("ws_f", [128, 4, 128], F32)
            ws_b = c.sb("ws_b", [128, 4, 128], BF16)
            c.dma("sp", ws_f[:], wsT_in, writes=["ws_f"])
            c.op("pool", lambda E: E.affine_select(out=ws_f[:], in_=ws_f[:], pattern=[[0, 4], [1, 128]],
                                                   compare_op=ALU.is_ge, fill=0.0, base=0,
                                                   channel_multiplier=-1), ["ws_f"], ["ws_f"])
            c.copy("pool", ws_b[:], ws_f[:], ["ws_f"], ["ws_b"])
            bs_f = c.sb("bs_f", [1, 512], F32)
            bs_b = c.sb("bs_b", [1, 512], BF16)
            ones_r = c.sb("ones_r", [1, 128], BF16)
            c.dma("sp", bs_f[:], bs_in, writes=["bs_f"])
            c.copy("pool", bs_b[:], bs_f[:], ["bs_f"], ["bs_b"])
            c.op("pool", lambda E: E.memset(ones_r[:], 1.0), writes=["ones_r"])
            vg_bc = c.sb("vg_bc", [128, 512], F32)
            c.dma("sp", vg_bc[:], vgain.partition_broadcast(128), writes=["vg_bc"])
            wv = c.sb("wv", [128, 8, 512], BF16)
            c.dma("pool", wv[:], wview(w_v), writes=["wv"])
            gv = [c.sb(f"gv{i}", [128, 512], F32) for i in range(2)]
            vn = [c.sb(f"vn{i}", [128, 512], BF16) for i in range(2)]
            vss = [c.sb(f"vss{i}", [128, 4], F32) for i in range(2)]
            vjunk = c.sb("vjunk", [128, 512], BF16)
            def v_stage1(ch):
                s = ch % 2
                cs = slice(128 + ch * 128, 128 + (ch + 1) * 128)
                pb = 2 + s
                for k in range(8):
                    c.mm(banks[pb][:], hT[:, k, cs], wv[:, k, :], k == 0, k == 7, allhT + ["wv"], [BK(pb)])
                gelu_tanh(c, banks[pb][:], [BK(pb)], gv[s][:], [("gv", s)], W, s, 512)

            def v_stage2(ch):
                s = ch % 2
                c.act(vjunk[:], gv[s][:], AF.Square, [("gv", s)], ["vjunk", ("vss", s)], accum_out=vss[s][:, 0:1])
                rstd_from_ss(c, vss[s], ("vss", s), 1.0 / 512)
                c.stt(vn[s][:], gv[s][:], vss[s][:, 2:3], vg_bc[:], ALU.mult, ALU.mult,
                      [("gv", s), ("vss", s), "vg_bc"], [("vn", s)])

            def v_stage3(ch):
                s = ch % 2
                pm = 4 + s
                for g in range(4):
                    c.mm(banks[pm][:, g * 128:(g + 1) * 128], vn[s][:, g * 128:(g + 1) * 128], ws_b[:, g, :],
                         True, False, [("vn", s), "ws_b"], [BK(pm)])
                    c.mm(banks[pm][:, g * 128:(g + 1) * 128], ones_r[0:1, :], bs_b[0:1, g * 128:(g + 1) * 128],
                         False, True, ["ones_r", "bs_b"], [BK(pm)])
                uv = uT[:, :, ch * 128:(ch + 1) * 128]
                c.tt("dve", uv, uv, banks[pm][:].rearrange("p (g t) -> p g t", g=4), ALU.mult,
                     alluT + [BK(pm)], [("ya", ch)])

            v_stage1(0)
            for ch in range(NCH):
                if ch + 1 < NCH:
                    v_stage1(ch + 1)
                v_stage2(ch)
                v_stage3(ch)
            c.dma("sp", yaT_o.rearrange("(g p) t -> p g t", p=128), uT[:],
                  reads=[("ya", ch) for ch in range(NCH)], writes=["d_yaT"])

        with Section(c):
            wblk = [c.sb(f"gwblk{i}", [128, 8, 128], BF16) for i in range(2)]
            gt = [c.sb(f"gt{i}", [128, T], BF16) for i in range(2)]
            un = 0
            for fb in range(24):
                s = fb % 2
                c.dma("pool", wblk[s][:], wview(w_g)[:, :, fb * 128:(fb + 1) * 128], writes=[("gwblk", s)])
                for tl in range(4):
                    b = un % 4
                    un += 1
                    for k in range(8):
                        c.mm(banks[b][:], wblk[s][:, k, :], hT[:, k, 128 + tl * 512:128 + (tl + 1) * 512],
                             k == 0, k == 7, [("gwblk", s)] + allhT, [BK(b)])
                    c.act(gt[s][:, tl * 512:(tl + 1) * 512], banks[b][:], AF.Sigmoid, [BK(b)], [("gt", s)])
                c.dma("sp", gT_o[fb * 128:(fb + 1) * 128, :], gt[s][:], reads=[("gt", s)], writes=["d_gT"])

        with Section(c):
            xbc_conv(c, banks, hT, allhT, wview(w_xbc), cw_in, cb_in, xbcT, CT_o)

        with Section(c):
            wz = c.sb("wz", [128, 8, 1024], BF16)
            wq = c.sb("wq", [128, 8, 512], BF16)
            wk = c.sb("wk", [128, 8, 512], BF16)
            wv2 = c.sb("wv2", [128, 8, 512], BF16)
            wdt = c.sb("wdt", [128, 8, 16], BF16)
            for t_, w_, k_ in ((wz, w_z, "wz"), (wdt, w_dt, "wdt"), (wq, w_q, "wq"), (wk, w_k, "wk"), (wv2, w_v2, "wv2")):
                c.dma("pool", t_[:], wview(w_), writes=[k_])
            dtb_bc = c.sb("dtb_bc", [128, 16], F32)
            c.dma("sp", dtb_bc[:], dtb_in.partition_broadcast(128), writes=["dtb_bc"])
            qg_bc = c.sb("qg_bc", [128, 64], F32)
            kg_bc = c.sb("kg_bc", [128, 64], F32)
            c.dma("sp", qg_bc[:], qg_in.partition_broadcast(128), writes=["qg_bc"])
            c.dma("sp", kg_bc[:], kg_in.partition_broadcast(128), writes=["kg_bc"])
            c.ts("dve", qg_bc[:], qg_bc[:], 0.125, None, ALU.mult, None, ["qg_bc"], ["qg_bc"])
            cos_sb = c.sb("cos_sb", [128, NCH, 8], F32)
            sin_sb = c.sb("sin_sb", [128, NCH, 8], F32)
            c.dma("sp", cos_sb[:], cos_in.rearrange("(c p) f -> p c f", p=128), writes=["cos_sb"])
            c.dma("sp", sin_sb[:], sin_in.rearrange("(c p) f -> p c f", p=128), writes=["sin_sb"])
            qT_sb = c.sb("qT_sb", [128, 4, T], BF16)
            kT_sb = c.sb("kT_sb", [128, 4, T], BF16)
            zsb = [c.sb(f"zsb{i}", [128, 1024], BF16) for i in range(2)]
            va = [c.sb(f"va{i}", [128, 4, 130], BF16) for i in range(2)]
            for i in range(2):
                c.op("pool", lambda E: E.memset(va[i][:, :, 128:129], 1.0), writes=[("va", i)])
                c.op("pool", lambda E: E.memset(va[i][:, :, 129:130], 0.0), writes=[("va", i)])
            sp_t = [c.sb(f"sp_t{i}", [128, 4, 16], F32) for i in range(2)]
            qsq = [c.sb(f"qsq{i}", [128, 512], F32) for i in range(2)]
            qn = [c.sb(f"qn{i}", [128, 512], F32) for i in range(2)]
            qb = [c.sb(f"qb{i}", [128, 512], BF16) for i in range(4)]
            pend_tr = []
            qss = [c.sb(f"qss{i}", [128, 24], F32) for i in range(2)]
            rp = [c.sb(f"rp{i}", [128, 4, 8, 8], F32) for i in range(2)]
            pTb = banks[7][:].bitcast(BF16)
            it = 0
            for ch in range(NCH):
                s = ch % 2
                cs = slice(128 + ch * 128, 128 + (ch + 1) * 128)
                for hf in range(2):
                    for k in range(8):
                        c.mm(banks[hf][:], hT[:, k, cs], wz[:, k, hf * 512:(hf + 1) * 512], k == 0, k == 7,
                             allhT + ["wz"], [BK(hf)])
                    c.act(zsb[s][:, hf * 512:(hf + 1) * 512], banks[hf][:], AF.Silu, [BK(hf)], [("zsb", s)])
                c.dma("sp", zs_o[ch * 128:(ch + 1) * 128, :], zsb[s][:], reads=[("zsb", s)], writes=["d_zs"])
                for k in range(8):
                    c.mm(banks[6][:, 0:16], hT[:, k, cs], wdt[:, k, :], k == 0, k == 7, allhT + ["wdt"], [BK(6)])
                st = sp_t[s]
                ks_ = ("sp_t", s)
                c.tt("dve", st[:, 0, :], banks[6][:, 0:16], dtb_bc[:], ALU.add, [BK(6), "dtb_bc"], [ks_])
                c.ts("dve", st[:, 1, :], st[:, 0, :], -1.0, None, ALU.mult, None, [ks_], [ks_])
                c.tt("dve", st[:, 1, :], st[:, 1, :], st[:, 0, :], ALU.max, [ks_], [ks_])
                c.act(st[:, 2, :], st[:, 1, :], AF.Exp, [ks_], [ks_], scale=-1.0)
                c.act(st[:, 3, :], st[:, 2, :], AF.Ln, [ks_, "eps"], [ks_], bias=ONE_AP[0])
                c.ts("dve", st[:, 1, :], st[:, 0, :], 0.0, None, ALU.max, None, [ks_], [ks_])
                c.tt("dve", dt_all[:, ch, :], st[:, 1, :], st[:, 3, :], ALU.add, [ks_], [("dt_all", ch)])
                for (w_t, wkey, g_bc, gkey, dstT, dkey) in ((wq, "wq", qg_bc, "qg_bc", qT_sb, "qT"),
                                                            (wk, "wk", kg_bc, "kg_bc", kT_sb, "kT")):
                    b2 = it % 2
                    b4 = it % 4
                    it += 1
                    pb = 2 + b2
                    for k in range(8):
                        c.mm(banks[pb][:], hT[:, k, cs], w_t[:, k, :], k == 0, k == 7, allhT + [wkey], [BK(pb)])
                    if len(pend_tr) >= 2:
                        pend_tr.pop(0)()
                    c.act(qsq[b2][:], banks[pb][:], AF.Square, [BK(pb)], [("qsq", b2)])
                    c.op("dve", lambda E: E.tensor_reduce(out=qss[b2][:, 0:8],
                                                         in_=qsq[b2][:].rearrange("p (g d) -> p g d", g=8),
                                                         axis=AX.X, op=ALU.add), [("qsq", b2)], [("qss", b2)])
                    rstd_from_ss(c, qss[b2], ("qss", b2), 1.0 / 64, w=8)
                    c.tt("dve", qn[b2][:].rearrange("p (g d) -> p g d", g=8),
                         banks[pb][:].rearrange("p (g d) -> p g d", g=8),
                         qss[b2][:, 16:24].unsqueeze(2).to_broadcast([128, 8, 64]), ALU.mult,
                         [BK(pb), ("qss", b2)], [("qn", b2)])
                    q3 = qn[b2][:].rearrange("p (g d) -> p g d", g=8)
                    c.tt("dve", q3, q3, g_bc[:, :].unsqueeze(1).to_broadcast([128, 8, 64]), ALU.mult,
                         [("qn", b2), gkey], [("qn", b2)])
                    c.copy("act", qb[b4][:], qn[b2][:], [("qn", b2)], [("qb", b4)])
                    qb3 = qb[b4][:].rearrange("p (g d) -> p g d", g=8)
                    cosb = cos_sb[:, ch, :].unsqueeze(1).to_broadcast([128, 8, 8])
                    sinb = sin_sb[:, ch, :].unsqueeze(1).to_broadcast([128, 8, 8])
                    r = rp[b2]
                    rk = ("rp", b2)
                    c.tt("dve", r[:, 0], q3[:, :, 0:8], cosb, ALU.mult, [("qn", b2), "cos_sb"], [rk])
                    c.tt("dve", r[:, 1], q3[:, :, 8:16], sinb, ALU.mult, [("qn", b2), "sin_sb"], [rk])
                    c.tt("dve", r[:, 2], q3[:, :, 8:16], cosb, ALU.mult, [("qn", b2), "cos_sb"], [rk])
                    c.tt("dve", r[:, 3], q3[:, :, 0:8], sinb, ALU.mult, [("qn", b2), "sin_sb"], [rk])
                    c.tt("dve", qb3[:, :, 0:8], r[:, 0], r[:, 1], ALU.subtract, [rk, ("qb", b4)], [("qb", b4)])
                    c.tt("dve", qb3[:, :, 8:16], r[:, 2], r[:, 3], ALU.add, [rk, ("qb", b4)], [("qb", b4)])

                    def do_tr(b4=b4, dstT=dstT, dkey=dkey, ch=ch):
                        for h in range(4):
                            c.tr(pTb[:, h * 128:(h + 1) * 128], qb[b4][:, h * 128:(h + 1) * 128], ident[:],
                                 [("qb", b4), "ident"], [BK(7)])
                        c.copy("act", dstT[:, :, ch * 128:(ch + 1) * 128],
                               pTb[:, 0:512].rearrange("p (h t) -> p h t", h=4), [BK(7)], [(dkey, ch)])
                    pend_tr.append(do_tr)
                for k in range(8):
                    c.mm(banks[4 + s][:], hT[:, k, cs], wv2[:, k, :], k == 0, k == 7, allhT + ["wv2"], [BK(4 + s)])
                c.copy("dve", va[s][:, :, 0:128], banks[4 + s][:].rearrange("p (h d) -> p h d", h=4),
                       [BK(4 + s)], [("va", s)])
                vp, lb = VPIECE[ch]
                c.dma("sp", v_send[vp].rearrange("(h p) (b d) -> p h b d", p=128, d=130)[:, :, lb, :], va[s][:],
                      reads=[("va", s)], writes=[("d_vs", vp)])
            while pend_tr:
                pend_tr.pop(0)()
            for pc in range(2):
                c.dma("sp", q_send[pc].rearrange("(h p) t -> p h t", p=128), qT_sb[:, :, pc * 1024:(pc + 1) * 1024],
                      reads=[("qT", ch) for ch in range(NCH)], writes=[("d_qs", pc)])
                c.dma("sp", k_send[pc].rearrange("(h p) t -> p h t", p=128), kT_sb[:, :, pc * 1024:(pc + 1) * 1024],
                      reads=[("kT", ch) for ch in range(NCH)], writes=[("d_ks", pc)])

        if io.get("after_qkv") is not None:
            io["after_qkv"]()
        with Section(c):
            ssd_loop(c, nc, banks, ident, dt_all, alog_in, dsk_in, xbcT, yloc_o, eacg_o,
                     st_send[:, 0:1024], st_send[:, 1024:1040])


ONE_AP = [None]


def xbc_conv(c, banks, hT, allhT, wxv, cw_in, cb_in, xbcT, CT_o):
    BK = lambda i: ("bank", i)
    cw = c.sb("scw", [128, 12, 4], F32)
    cb = c.sb("scb", [128, 12], F32)
    c.dma("sp", cw[:], cw_in, writes=["scw"])
    c.dma("sp", cb[:], cb_in, writes=["scb"])
    wblk = [c.sb(f"xwblk{i}", [128, 8, 128], BF16) for i in range(2)]
    acc = [c.sb(f"xacc{i}", [128, 512], F32) for i in range(2)]
    xbuf = [c.sb(f"xxb{i}", [128, 515], BF16) for i in range(2)]
    un = 0
    pend = None
    for fb in range(12):
        s = fb % 2
        c.dma("pool", wblk[s][:], wxv[:, :, fb * 128:(fb + 1) * 128], writes=[("xwblk", s)])
        for tl in range(4):
            b = un % 2
            un += 1
            c0 = 128 + tl * 512
            pt = banks[b]
            pk = BK(b)
            xb = xbuf[b]
            xk = ("xxb", b)
            if tl == 0:
                ph = banks[6][:, 0:3]
                for k in range(8):
                    c.mm(ph, wblk[s][:, k, :], hT[:, k, c0 - 3:c0], k == 0, k == 7, [("xwblk", s)] + allhT, [BK(6)])
                c.copy("act", xb[:, 0:3], ph, [BK(6)], [xk])
            else:
                c.copy("act", xb[:, 0:3], xbuf[1 - b][:, 512:515], [("xxb", 1 - b)], [xk])
            for k in range(8):
                c.mm(pt[:], wblk[s][:, k, :], hT[:, k, c0:c0 + 512], k == 0, k == 7, [("xwblk", s)] + allhT, [pk])
            a = acc[b]
            ak = ("xacc", b)
            c.act(xb[:, 3:515], pt[:], AF.Copy, [pk], [xk])
            c.act(a[:], pt[:], AF.Identity, [pk, "scw", "scb"], [ak], scale=cw[:, fb, 3:4], bias=cb[:, fb:fb + 1])
            for sh in (1, 2, 3):
                c.stt(a[:], xb[:, 3 - sh:515 - sh], cw[:, fb, 3 - sh:4 - sh], a[:], ALU.mult, ALU.add,
                      [xk, ak, "scw"], [ak])
            if pend is not None:
                pend()

            def tail(b=b, fb=fb, tl=tl):
                c.act(xbcT[:, fb, tl * 512:(tl + 1) * 512], acc[b][:], AF.Silu, [("xacc", b)], [("xbcT", fb)])
            pend = tail
    pend()
    allx = [("xbcT", fb) for fb in range(12)]
    c.dma("sp", CT_o.rearrange("(g p) t -> p g t", p=128), xbcT[:, 10:12, :], reads=allx, writes=["d_CT"])


def ssd_loop(c, nc, banks, ident, dt_all, alog_in, dsk_in, xbcT, yloc_o, eacg_o, sfin_o, logd_o):
    BK = lambda i: ("bank", i)
    allx = [("xbcT", fb) for fb in range(12)]
    triT = c.sb("triT", [128, 128], F32)
    tri_b = c.sb("tri_b", [128, 128], BF16)
    lstr = c.sb("lstr", [128, 128], F32)
    ones = c.sb("ones_f", [128, 128], F32)
    c.op("pool", lambda E: E.memset(ones[:], 1.0), writes=["ones_f"])
    c.op("pool", lambda E: E.memset(triT[:], 1.0), writes=["triT"])
    c.op("pool", lambda E: E.affine_select(out=triT[:], in_=triT[:], pattern=[[1, 128]], compare_op=ALU.is_ge,
                                           fill=0.0, base=0, channel_multiplier=-1), ["triT"], ["triT"])
    c.copy("pool", tri_b[:], triT[:], ["triT"], ["tri_b"])
    c.op("pool", lambda E: E.memset(lstr[:], 1.0), writes=["lstr"])
    c.op("pool", lambda E: E.affine_select(out=lstr[:], in_=lstr[:], pattern=[[-1, 128]], compare_op=ALU.is_gt,
                                           fill=0.0, base=0, channel_multiplier=1), ["lstr"], ["lstr"])
    a_bc = c.sb("a_bc", [128, 16], F32)
    d_bc = c.sb("d_bc", [128, 16], F32)
    c.dma("sp", a_bc[:], alog_in.partition_broadcast(128), writes=["a_bc"])
    c.dma("sp", d_bc[:], dsk_in.partition_broadcast(128), writes=["d_bc"])
    c.act(a_bc[:], a_bc[:], AF.Exp, ["a_bc"], ["a_bc"])
    c.ts("dve", a_bc[:], a_bc[:], -1.0, None, ALU.mult, None, ["a_bc"], ["a_bc"])
    state = c.sb("state", [128, 1024], F32)
    prev_bf = c.sb("prev_bf", [128, 1024], BF16)
    offs = c.sb("offs", [128, 16], F32)
    eacg_all = c.sb("eacg_all", [128, NCH, 16], F32)
    c.op("pool", lambda E: E.memset(state[:], 0.0), writes=["state"])
    c.op("pool", lambda E: E.memset(prev_bf[:], 0.0), writes=["prev_bf"])
    c.op("pool", lambda E: E.memset(offs[:], 0.0), writes=["offs"])
    xs_tok = [c.sb(f"xs_tok{i}", [128, 1024], BF16) for i in range(2)]
    B_tok = [c.sb(f"B_tok{i}", [128, 256], BF16) for i in range(2)]
    sm = [c.sb(f"sm{i}", [128, 12, 16], F32) for i in range(2)]
    Lh_ = [c.sb(f"Lh{i}", [128, 16, 128], BF16) for i in range(2)]
    decT_ = [c.sb(f"decT{i}", [128, 16, 128], BF16) for i in range(2)]
    cb_sb_ = [c.sb(f"cb_sb{i}", [128, 2, 128], BF16) for i in range(2)]
    MT = [c.sb(f"MT{i}", [128, 16, 128], BF16) for i in range(2)]
    xdt = [c.sb(f"xdt{i}", [128, 1024], BF16) for i in range(2)]
    xdt2 = [c.sb(f"xdt2{i}", [128, 1024], BF16) for i in range(2)]
    ydt_ = [c.sb(f"ydt{i}", [128, 1024], F32) for i in range(2)]
    ytmp = c.sb("ytmp", [128, 1024], F32)
    yl = [c.sb(f"yl{i}", [128, 1024], F32) for i in range(2)]
    t2_ = [c.sb(f"t2{i}", [128, 1024], F32) for i in range(2)]
    pA = banks[0][:].bitcast(BF16)
    pC = banks[1]
    pCb = banks[1][:].bitcast(BF16)

    def stage_a(ch):
        s = ch % 2
        cs = slice(ch * 128, (ch + 1) * 128)
        v = sm[s]
        vk = ("sm", s)
        Lh, decT, cb_sb, t2, ydt = Lh_[s], decT_[s], cb_sb_[s], t2_[s], ydt_[s]
        for fb in range(8):
            c.tr(pA[:, fb * 128:(fb + 1) * 128], xbcT[:, fb, cs], ident[:], allx + ["ident"], [BK(0)])
        c.copy("act", xs_tok[s][:], pA, [BK(0)], [("xs_tok", s)])
        for g in range(2):
            c.tr(pCb[:, 640 + g * 128:640 + (g + 1) * 128], xbcT[:, 8 + g, cs], ident[:], allx + ["ident"], [BK(1)])
        c.copy("act", B_tok[s][:], pCb[:, 640:896], [BK(1)], [("B_tok", s)])
        dt = dt_all[:, ch, :]
        c.tt("dve", v[:, 0, :], dt, a_bc[:], ALU.mult, [("dt_all", ch), "a_bc"], [vk])
        c.mm(pC[:, 0:16], triT[:], v[:, 0, :], True, True, ["triT", vk], [BK(1)])
        c.mm(pC[:, 16:32], ones[:], v[:, 0, :], True, True, ["ones_f", vk], [BK(1)])
        c.copy("dve", v[:, 1:3, :], pC[:, 0:32].rearrange("p (a h) -> p a h", a=2), [BK(1)], [vk])
        for hf, eng in ((0, "dve"), (1, "pool")):
            c.tt(eng, Lh[:, hf * 8:(hf + 1) * 8, :], lstr[:, :].unsqueeze(1).to_broadcast([128, 8, 128]),
                 v[:, 0, hf * 8:(hf + 1) * 8].unsqueeze(2).to_broadcast([128, 8, 128]), ALU.mult,
                 ["lstr", vk], [("Lh", s, hf)])
        for rnd in range(2):
            for q in range(2):
                for hh in range(4):
                    h = rnd * 8 + q * 4 + hh
                    c.mm(banks[2 + q][:, hh * 128:(hh + 1) * 128], Lh[:, h, :], tri_b[:], True, True,
                         [("Lh", s, rnd), "tri_b"], [BK(2 + q)])
                c.act(decT[:, rnd * 8 + q * 4:rnd * 8 + q * 4 + 4, :],
                      banks[2 + q][:].rearrange("p (h t) -> p h t", h=4), AF.Exp, [BK(2 + q)], [("decT", s, rnd)])
        for g in range(2):
            c.mm(pC[:, 32 + g * 128:32 + (g + 1) * 128], xbcT[:, 8 + g, cs], xbcT[:, 10 + g, cs], True, True,
                 allx, [BK(1)])
        c.tt("dve", cb_sb[:], pC[:, 32:288].rearrange("p (g t) -> p g t", g=2),
             triT[:, :].unsqueeze(1).to_broadcast([128, 2, 128]), ALU.mult, [BK(1), "triT"], [("cb_sb", s)])
        for g in range(2):
            c.tt("dve", MT[s][:, g * 8:(g + 1) * 8, :], decT[:, g * 8:(g + 1) * 8, :],
                 cb_sb[:, g, :].unsqueeze(1).to_broadcast([128, 8, 128]), ALU.mult,
                 [("decT", s, g), ("cb_sb", s)], [("MT", s, g)])
        c.act(v[:, 3, :], v[:, 1, :], AF.Exp, [vk], [vk])
        c.tt("dve", v[:, 4, :], v[:, 2, :], v[:, 1, :], ALU.subtract, [vk], [vk])
        c.act(v[:, 5, :], v[:, 4, :], AF.Exp, [vk], [vk])
        c.tt("dve", v[:, 6, :], v[:, 5, :], dt, ALU.mult, [vk, ("dt_all", ch)], [vk])
        c.tt("dve", v[:, 7, :], v[:, 1, :], offs[:], ALU.add, [vk, "offs"], [vk])
        c.act(eacg_all[:, ch, :], v[:, 7, :], AF.Exp, [vk], [("eacg", ch)])
        c.tt("dve", offs[:], offs[:], v[:, 2, :], ALU.add, [vk, "offs"], ["offs"])
        c.act(v[:, 8, :], v[:, 2, :], AF.Exp, [vk], [vk])
        x3 = xs_tok[s][:].rearrange("p (h d) -> p h d", h=16)
        c.tt("dve", xdt[s][:].rearrange("p (h d) -> p h d", h=16), x3,
             dt.unsqueeze(2).to_broadcast([128, 16, 64]), ALU.mult, [("xs_tok", s), ("dt_all", ch)], [("xdt", s)])
        c.tt("pool", xdt2[s][:].rearrange("p (h d) -> p h d", h=16), x3,
             v[:, 6, :].unsqueeze(2).to_broadcast([128, 16, 64]), ALU.mult, [("xs_tok", s), vk], [("xdt2", s)])
        c.tt("pool", t2[:].rearrange("p (h d) -> p h d", h=16), x3,
             d_bc[:, :].unsqueeze(2).to_broadcast([128, 16, 64]), ALU.mult, [("xs_tok", s), "d_bc"], [("t2", s)])
        for h in range(16):
            pb = 4 + h // 8
            c.mm(banks[pb][:, (h % 8) * 64:(h % 8 + 1) * 64], MT[s][:, h, :], xdt[s][:, h * 64:(h + 1) * 64],
                 True, True, [("MT", s, h // 8), ("xdt", s)], [BK(pb)])
        for g in range(2):
            hs = slice(g * 512, (g + 1) * 512)
            c.tt("dve", ydt[:, hs], banks[4 + g][:], t2[:, hs], ALU.add, [BK(4 + g), ("t2", s)], [("ydt", s, g)])

    def stage_b(ch):
        s = ch % 2
        cs = slice(ch * 128, (ch + 1) * 128)
        v = sm[s]
        vk = ("sm", s)
        ydt = ydt_[s]
        for g in range(2):
            c.mm(banks[6 + g][:], xbcT[:, 10 + g, cs], prev_bf[:, g * 512:(g + 1) * 512], True, True,
                 allx + ["prev_bf"], [BK(6 + g)])
        for g in range(2):
            hs = slice(g * 512, (g + 1) * 512)
            c.tt("dve", ytmp[:, hs].rearrange("p (h d) -> p h d", h=8),
                 banks[6 + g][:].rearrange("p (h d) -> p h d", h=8),
                 v[:, 3, g * 8:(g + 1) * 8].unsqueeze(2).to_broadcast([128, 8, 64]), ALU.mult,
                 [BK(6 + g), vk], [("ytmp", g)])
        c.tt("dve", yl[s][:], ytmp[:], ydt[:], ALU.add, [("ytmp", 0), ("ytmp", 1), ("ydt", s, 0), ("ydt", s, 1)],
             [("yl", s)])
        c.dma("sp", yloc_o[ch * 128:(ch + 1) * 128, :], yl[s][:], reads=[("yl", s)], writes=["d_yloc"])
        for g in range(2):
            c.mm(banks[6 + g][:], B_tok[s][:, g * 128:(g + 1) * 128], xdt2[s][:, g * 512:(g + 1) * 512], True, True,
                 [("B_tok", s), ("xdt2", s)], [BK(6 + g)])
        c.tt("pool", state[:].rearrange("p (h d) -> p h d", h=16), state[:].rearrange("p (h d) -> p h d", h=16),
             v[:, 8, :].unsqueeze(2).to_broadcast([128, 16, 64]), ALU.mult, ["state", vk], ["state"])
        for g in range(2):
            hs = slice(g * 512, (g + 1) * 512)
            c.tt("dve", state[:, hs], state[:, hs], banks[6 + g][:], ALU.add, ["state", BK(6 + g)], ["state"])
        c.copy("act", prev_bf[:], state[:], ["state"], ["prev_bf"])

    stage_a(0)
    for ch in range(NCH):
        if ch + 1 < NCH:
            stage_a(ch + 1)
        stage_b(ch)
    c.dma("sp", sfin_o, state[:], reads=["state"], writes=["d_st"])
    c.dma("sp", logd_o, offs[:], reads=["offs"], writes=["d_st"])
    c.dma("sp", eacg_o.rearrange("(c p) h -> p c h", p=128), eacg_all[:], reads=[("eacg", ch) for ch in range(NCH)],
          writes=["d_eacg"])


IN_SIZES = (512, 512, 1024, 1536, 16, 512, 512, 512, 3072)
IN_OFFS = np.concatenate([[0], np.cumsum(IN_SIZES)]).astype(int)


def rope_tables_np():
    pos = np.arange(S, dtype=np.float32)
    inv_freq = (1.0 / (np.float32(500000.0) ** (np.arange(0, 16, 2, dtype=np.float32) / np.float32(16)))).astype(np.float32)
    ang = (pos[:, None] * inv_freq[None, :]).astype(np.float32)
    return np.cos(ang).astype(np.float32), np.sin(ang).astype(np.float32)


def ext_rows(full, b, j, halo=128):
    h = np.zeros((halo,) + full.shape[2:], full.dtype)
    if j > 0:
        h = full[b, j * T - halo:j * T]
    return np.ascontiguousarray(np.concatenate([h, full[b, j * T:(j + 1) * T]], 0))


VPIECE = [(0, b) for b in range(6)] + [(1, b) for b in range(6)] + [(2, b) for b in range(4)]
VPW = (6, 6, 4)


def emit_attn(c, nc, io, layer):
    lam_init = 0.8 - 0.6 * math.exp(-0.3 * layer)
    lam_in, idx_in = io["lam"], io["idx"]
    q_g, k_g, v_g, y_send = io["q_g"], io["k_g"], io["v_g"], io["y_send"]
    NKB = S // 128
    with Section(c):
        setup_consts(c)
        sc = [c.ps(f"sc{i}", [128, 1024], F32) for i in range(2)]
        accs = c.ps("accs", [128, 2048], F32)
        qT = c.sb("qT", [128, S], BF16)
        kT = c.sb("kT", [128, S], BF16)
        V = c.sb("V", [128, NKB, 130], BF16)
        idx = c.sb("idx", [128, 4], mybir.dt.int32)
        c.dma("sp", idx[:], idx_in, writes=["idx"])
        for i in range(4):
            for pc in range(2):
                sl = slice(i * 2048 + pc * 1024, i * 2048 + (pc + 1) * 1024)
                c.op16("pool", lambda E: E.indirect_dma_start(
                    out=qT[:, sl], out_offset=None, in_=q_g[pc],
                    in_offset=bass.IndirectOffsetOnAxis(ap=idx[:, i:i + 1], axis=0)),
                    ["idx", ("g_q", pc)], [("qT", i)])
                c.op16("pool", lambda E: E.indirect_dma_start(
                    out=kT[:, sl], out_offset=None, in_=k_g[pc],
                    in_offset=bass.IndirectOffsetOnAxis(ap=idx[:, i:i + 1], axis=0)),
                    ["idx", ("g_k", pc)], [("kT", i)])
            b0 = 0
            for vp in range(3):
                c.op16("pool", lambda E: E.indirect_dma_start(
                    out=V[:, i * 16 + b0:i * 16 + b0 + VPW[vp], :].rearrange("p b d -> p (b d)"), out_offset=None,
                    in_=v_g[vp], in_offset=bass.IndirectOffsetOnAxis(ap=idx[:, i:i + 1], axis=0)),
                    ["idx", ("g_v", vp)], [("V", i)])
                b0 += VPW[vp]
        lv = c.sb("lv", [128, 256], F32)
        c.dma("sp", lv[:], lam_in.partition_broadcast(128), writes=["lv"])
        lp = c.sb("lp", [128, 2, 64], F32)
        l2 = c.sb("l2", [128, 8], F32)
        lv4 = lv[:].rearrange("p (a d) -> p a d", a=4)
        c.tt("dve", lp[:, 0, :], lv4[:, 0, :], lv4[:, 1, :], ALU.mult, ["lv"], ["lp"])
        c.tt("dve", lp[:, 1, :], lv4[:, 2, :], lv4[:, 3, :], ALU.mult, ["lv", "lp"], ["lp"])
        c.op("dve", lambda E: E.tensor_reduce(out=l2[:, 0:2], in_=lp[:], axis=AX.X, op=ALU.add), ["lp"], ["l2"])
        c.act(l2[:, 2:4], l2[:, 0:2], AF.Exp, ["l2"], ["l2"])
        c.tt("dve", l2[:, 4:5], l2[:, 3:4], l2[:, 2:3], ALU.subtract, ["l2"], ["l2"])
        c.ts("dve", l2[:, 5:6], l2[:, 4:5], -lam_init, None, ALU.add, None, ["l2"], ["l2"])
        sgc = c.sb("sgc", [128, 1], F32)
        c.dma("sp", sgc[:], io["sgc"], writes=["sgc"])
        c.ts("dve", sgc[:], sgc[:], 1.0 - lam_init, None, ALU.mult, None, ["sgc"], ["sgc"])
        ones = c.sb("ones_f", [128, 128], F32)
        c.op("pool", lambda E: E.memset(ones[:], 1.0), writes=["ones_f"])
        ones_b = c.sb("ones_b", [128, 128], BF16)
        c.op("pool", lambda E: E.memset(ones_b[:], 1.0), writes=["ones_b"])
        tri = c.sb("tri", [128, 128], BF16)
        c.op("pool", lambda E: E.memset(tri[:], 1.0), writes=["tri"])
        c.op("pool", lambda E: E.affine_select(out=tri[:], in_=tri[:], pattern=[[1, 128]], compare_op=ALU.is_ge,
                                               fill=0.0, base=0, channel_multiplier=-1), ["tri"], ["tri"])
        PT = [c.sb(f"PT{i}", [128, 2, 512], BF16) for i in range(2)]
        accS = [c.sb(f"accS{i}", [128, 2, 512], F32) for i in range(2)]
        rL = c.sb("rL", [128, 2, 512], F32)
        o_t = c.sb("o_t", [128, 512], F32)
        o_u = c.sb("o_u", [128, 512], F32)
        yb = [c.sb(f"ybo{i}", [128, 512], BF16) for i in range(2)]
        units = [(qg, kb) for qg in range(S // 512) for kb in range(4 * qg + 4)]

        def emit_scores(u):
            qg, kb = units[u]
            r = kb - 4 * qg
            c0 = max(r, 0) * 128
            b2 = u % 2
            for cp in range(2):
                ps_ = slice(cp * 64, (cp + 1) * 64)
                c.mm(sc[b2][:, cp * 512 + c0:(cp + 1) * 512], kT[ps_, kb * 128:(kb + 1) * 128],
                     qT[ps_, qg * 512 + c0:(qg + 1) * 512], True, True,
                     [("qT", qg // 4), ("kT", kb // 16)], [("sc", b2)])
            c.act(PT[b2][:, :, c0:512], sc[b2][:].rearrange("p (a q) -> p a q", a=2)[:, :, c0:512], AF.Exp,
                  [("sc", b2)], [("PT", b2)])
            if r >= 0:
                c.tt("dve", PT[b2][:, :, c0:c0 + 128], PT[b2][:, :, c0:c0 + 128],
                     tri[:, :].unsqueeze(1).to_broadcast([128, 2, 128]), ALU.mult,
                     [("PT", b2), "tri"], [("PT", b2)])

        def emit_pv(u):
            qg, kb = units[u]
            r = kb - 4 * qg
            c0 = max(r, 0) * 128
            b2 = u % 2
            g2 = qg % 2
            for cp in range(2):
                c.mm(accs[:, cp * 512 + c0:(cp + 1) * 512], V[:, kb, 0:128], PT[b2][:, cp, c0:512],
                     kb == 0, False, [("PT", b2), ("V", kb // 16)], [("acc", cp)], skip_group_check=True)
            c.mm(accs[:, 1536 + c0:2048], ones_b[:], PT[b2][:, 1, c0:512], kb == 0, False,
                 [("PT", b2), "ones_b"], [("acc", 3)], skip_group_check=True)
            if kb == 0:
                c.copy("dve", accS[g2][:, 0, :], PT[b2][:, 0, :], [("PT", b2)], [("accS", g2)])
            else:
                c.tt("dve", accS[g2][:, 0, c0:512], accS[g2][:, 0, c0:512], PT[b2][:, 0, c0:512], ALU.add,
                     [("PT", b2), ("accS", g2)], [("accS", g2)])

        o0s = [c.sb(f"o0s{i}", [128, 512], F32) for i in range(2)]
        o1s = [c.sb(f"o1s{i}", [128, 512], F32) for i in range(2)]
        pending = []

        def finalize(qg, u_now, gap):
            g2 = qg % 2

            def f0():
                c.copy("act", o0s[g2][:], accs[:, 0:512], [("acc", 0)], [("o0s", g2)])
                c.copy("act", o1s[g2][:], accs[:, 512:1024], [("acc", 1)], [("o1s", g2)])
                c.mm(accs[:, 1024:1536], ones[:], accS[g2][:, 0, :], True, True, ["ones_f", ("accS", g2)], [("acc", 2)])
                c.act(rL[:, 1, :], accs[:, 1536:2048], AF.Ln, [("acc", 3)], [("rL", 1)])

            def f1():
                c.act(rL[:, 0, :], accs[:, 1024:1536], AF.Ln, [("acc", 2)], [("rL", 0)])
                c.act(rL[:, 0, :], rL[:, 0, :], AF.Exp, [("rL", 0)], [("rL", 0)], scale=-1.0)
                c.act(rL[:, 1, :], rL[:, 1, :], AF.Exp, [("rL", 1)], [("rL", 1)], scale=-1.0)

            def f2():
                c.tt("dve", o_t[:], o0s[g2][:], rL[:, 0, :], ALU.mult, [("o0s", g2), ("rL", 0)], ["o_t"])
                c.tt("dve", o_u[:], o1s[g2][:], rL[:, 1, :], ALU.mult, [("o1s", g2), ("rL", 1)], ["o_u"])

            def f3():
                c.stt(o_t[:], o_u[:], l2[:, 5:6], o_t[:], ALU.mult, ALU.add, ["o_u", "o_t", "l2"], ["o_t"])
                c.tt("dve", o_u[:], o_t[:], o_t[:], ALU.mult, ["o_t", "o_u"], ["o_u"])
                c.mm(accs[:, 1024:1536], ones[:], o_u[:], True, True, ["ones_f", "o_u"], [("acc", 2)])

            def f4():
                c.act(o_u[:], accs[:, 1024:1536], AF.Ln, [("acc", 2), "eps", "o_u"], ["o_u"], scale=1.0 / 128,
                      bias=EPS_AP[0])
                c.act(o_u[:], o_u[:], AF.Exp, ["o_u"], ["o_u"], scale=-0.5)
                c.stt(yb[g2][:], o_t[:], sgc[:, 0:1], o_u[:], ALU.mult, ALU.mult, ["o_t", "o_u", "sgc"], [("ybo", g2)])
                j_, col0 = qg // 4, (qg % 4) * 512
                c.dma("sp", y_send[col0 // 1024][j_ * 128:(j_ + 1) * 128, (col0 % 1024):(col0 % 1024) + 512],
                      yb[g2][:], reads=[("ybo", g2)], writes=[("d_ys", col0 // 1024)])
            for k, f in enumerate((f0, f1, f2, f3, f4)):
                pending.append((u_now + k * gap, f))

        emit_scores(0)
        nu = len(units)
        for u in range(nu):
            if u + 1 < nu:
                emit_scores(u + 1)
            emit_pv(u)
            qg, kb = units[u]
            if kb == 4 * qg + 3:
                finalize(qg, u, 2 if u + 1 < nu else 0)
            while pending and pending[0][0] <= u:
                pending.pop(0)[1]()
        while pending:
            pending.pop(0)[1]()


def emit_back(c, nc, io):
    (x_ext, yaT_in, zs_in, yloc_in, eacg_in, CT_in, st_g, mlt_in, mk_in, y_g, gT_in, ng_in, w_a, w_b, w_c, w_o,
     idx_in, xm_o, halo_out) = (io[k] for k in (
        "x_ext", "yaT", "zs", "yloc", "eacg", "CT", "st_g", "mlt", "mk", "y_g", "gT", "ng", "w_a", "w_b", "w_c",
        "w_o", "idx", "x_mid", "halo_out"))
    x_in = x_ext[128:, :]

    def wview(w):
        return w.rearrange("(k p) n -> p k n", p=128)
    with Section(c):
        ident = setup_consts(c)
        banks = [c.ps(f"bank{i}", [128, 512], F32) for i in range(8)]
        BK = lambda i: ("bank", i)
        H_bf = c.sb("H_bf", [128, 1024], BF16)
        idx = c.sb("idx", [128, 4], mybir.dt.int32)
        c.dma("sp", idx[:], idx_in, writes=["idx"])
        ybT = c.sb("ybT", [128, 8, T], BF16)
        ycT = c.sb("ycT", [128, 4, T], BF16)
        yaT = c.sb("yaT", [128, 4, T], BF16)
        c.dma("sp", yaT[:], yaT_in.rearrange("(g p) t -> p g t", p=128), writes=["yaT"])
        wa = c.sb("wa", [128, 4, D], BF16)
        wb = c.sb("wb", [128, 8, D], BF16)
        wc = c.sb("wc", [128, 4, D], BF16)
        wo = c.sb("wo", [128, 8, D], BF16)
        for t_, w_, k_ in ((wa, w_a, "wa"), (wb, w_b, "wb"), (wc, w_c, "wc"), (wo, w_o, "wo")):
            c.dma("pool", t_[:], wview(w_), writes=[k_])
        with Section(c):
            sf = c.sb("sf", [128, 4, 1024], F32)
            ld = c.sb("ld", [128, 4, 16], F32)
            mlt = c.sb("mlt", [128, 4], F32)
            mk = c.sb("mk", [128, 16], F32)
            stv = st_g.rearrange("(r p) n -> p r n", p=128)
            c.dma("sp", sf[:], stv[:, :, 0:1024], reads=["g_st"], writes=["sf"])
            c.dma("sp", ld[:], stv[:, :, 1024:1040], reads=["g_st"], writes=["ld"])
            c.dma("sp", mlt[:], mlt_in, writes=["mlt"])
            c.dma("sp", mk[:], mk_in, writes=["mk"])
            H = c.sb("H", [128, 1024], F32)
            Ht = c.sb("Ht", [128, 1024], F32)
            e = c.sb("e", [128, 16], F32)
            c.op("pool", lambda E: E.memset(H[:], 0.0), writes=["H"])
            for i in range(4):
                c.op("pool", lambda E: E.memset(e[:], 0.0), writes=["e"])
                for k in range(4):
                    c.stt(e[:], ld[:, k, :], mk[:, i * 4 + k:i * 4 + k + 1], e[:], ALU.mult, ALU.add,
                          ["ld", "mk", "e"], ["e"])
                c.act(e[:], e[:], AF.Exp, ["e"], ["e"])
                c.ts("dve", e[:], e[:], mlt[:, i:i + 1], None, ALU.mult, None, ["e", "mlt"], ["e"])
                c.tt("dve", Ht[:].rearrange("p (h d) -> p h d", h=16), sf[:, i, :].rearrange("p (h d) -> p h d", h=16),
                     e[:, :].unsqueeze(2).to_broadcast([128, 16, 64]), ALU.mult, ["sf", "e"], ["Ht"])
                c.tt("dve", H[:], H[:], Ht[:], ALU.add, ["H", "Ht"], ["H"])
            c.copy("act", H_bf[:], H[:], ["H"], ["H_bf"])
        for h in range(4):
            for pc in range(2):
                c.op16("pool", lambda E: E.indirect_dma_start(
                    out=ycT[:, h, pc * 1024:(pc + 1) * 1024], out_offset=None,
                    in_=y_g[pc], in_offset=bass.IndirectOffsetOnAxis(ap=idx[:, h:h + 1], axis=0)),
                    ["idx", ("g_y", pc)], [("ycT", h)])
        with Section(c):
            ng = c.sb("ng", [128, 8], F32)
            c.dma("sp", ng[:], ng_in, writes=["ng"])
            CT = c.sb("CT", [128, 2, T], BF16)
            c.dma("sp", CT[:], CT_in.rearrange("(g p) t -> p g t", p=128), writes=["CT"])
            eacg = c.sb("eacg", [128, NCH, 16], F32)
            c.dma("sp", eacg[:], eacg_in.rearrange("(c p) h -> p c h", p=128), writes=["eacg"])
            yl = [c.sb(f"yl{i}", [128, 1024], F32) for i in range(2)]
            zt = [c.sb(f"zt{i}", [128, 1024], BF16) for i in range(2)]
            yt = [c.sb(f"yt{i}", [128, 1024], F32) for i in range(2)]
            yb = [c.sb(f"yb{i}", [128, 1024], BF16) for i in range(2)]
            yj = c.sb("yj", [128, 1024], BF16)
            yss = [c.sb(f"yss{i}", [128, 4], F32) for i in range(2)]
            pT = banks[7][:].bitcast(BF16)
            pT2 = banks[6][:].bitcast(BF16)
            def b_stage1(ch):
                s = ch % 2
                rs = slice(ch * 128, (ch + 1) * 128)
                c.dma("sp", yl[s][:], yloc_in[rs, :], writes=[("yl", s)])
                c.dma("sp", zt[s][:], zs_in[rs, :], writes=[("zt", s)])
                for g in range(2):
                    c.mm(banks[g][:], CT[:, g, rs], H_bf[:, g * 512:(g + 1) * 512], True, True, ["CT", "H_bf"], [BK(g)])
                    hs = slice(g * 512, (g + 1) * 512)
                    c.tt("dve", yt[s][:, hs].rearrange("p (h d) -> p h d", h=8),
                         banks[g][:].rearrange("p (h d) -> p h d", h=8),
                         eacg[:, ch, g * 8:(g + 1) * 8].unsqueeze(2).to_broadcast([128, 8, 64]), ALU.mult,
                         [BK(g), "eacg"], [("yt", s, g)])
                ytk = [("yt", s, 0), ("yt", s, 1)]
                c.tt("dve", yt[s][:], yt[s][:], yl[s][:], ALU.add, ytk + [("yl", s)], ytk)
                c.tt("dve", yt[s][:], yt[s][:], zt[s][:], ALU.mult, ytk + [("zt", s)], ytk)
                c.act(yj[:], yt[s][:], AF.Square, ytk, ["yj", ("yss", s)], accum_out=yss[s][:, 0:1])
                rstd_from_ss(c, yss[s], ("yss", s), 1.0 / 1024)

            def b_stage2(ch):
                s = ch % 2
                rs = slice(ch * 128, (ch + 1) * 128)
                ytk = [("yt", s, 0), ("yt", s, 1)]
                c.act(yb[s][:], yt[s][:], AF.Copy, ytk + [("yss", s)], [("yb", s)], scale=yss[s][:, 2:3])
                for k in range(8):
                    c.tr(pT[:, k * 128:(k + 1) * 128], yb[s][:, k * 128:(k + 1) * 128], ident[:],
                         [("yb", s), "ident"], [BK(7)])
                c.tt("dve", ybT[:, :, rs], pT.rearrange("p (k t) -> p k t", k=8),
                     ng[:, :].unsqueeze(2).to_broadcast([128, 8, 128]), ALU.mult, [BK(7), "ng"], [("ybT", ch)])

            b_stage1(0)
            for ch in range(NCH):
                if ch + 1 < NCH:
                    b_stage1(ch + 1)
                b_stage2(ch)
        with Section(c):
            ally = [("ybT", ch) for ch in range(NCH)] + [("ycT", h) for h in range(4)] + ["yaT"]
            gts = [c.sb(f"gts{i}", [128, 3, 512], BF16) for i in range(2)]
            m1 = [c.sb(f"m1{i}", [128, 512], F32) for i in range(2)]
            m2 = [c.sb(f"m2{i}", [128, 512], F32) for i in range(2)]
            m3 = [c.sb(f"m3{i}", [128, 512], F32) for i in range(2)]
            mT = [c.sb(f"mT{i}", [128, 8, 512], BF16) for i in range(2)]
            xr = [c.sb(f"xr{i}", [128, D], F32) for i in range(2)]
            xo = [c.sb(f"xo{i}", [128, D], F32) for i in range(2)]
            gview = gT_in.rearrange("(b o p) t -> p b o t", b=3, p=128)
            un = 0
            for tl in range(4):
                ts_ = slice(tl * 512, (tl + 1) * 512)
                mt = mT[tl % 2]
                for ob in range(8):
                    s = un % 2
                    un += 1
                    os_ = slice(ob * 128, (ob + 1) * 128)
                    c.dma("sp", gts[s][:], gview[:, :, ob, ts_], writes=[("gts", s)])
                    for k in range(4):
                        c.mm(banks[0][:], wa[:, k, os_], yaT[:, k, ts_], k == 0, k == 3, ["wa"] + ally, [BK(0)])
                    for k in range(8):
                        c.mm(banks[1][:], wb[:, k, os_], ybT[:, k, ts_], k == 0, k == 7, ["wb"] + ally, [BK(1)])
                    for k in range(4):
                        c.mm(banks[2][:], wc[:, k, os_], ycT[:, k, ts_], k == 0, k == 3, ["wc"] + ally, [BK(2)])
                    c.tt("dve", m1[s][:], banks[0][:], gts[s][:, 0, :], ALU.mult, [BK(0), ("gts", s)], [("m1", s)])
                    c.tt("dve", m2[s][:], banks[1][:], gts[s][:, 1, :], ALU.mult, [BK(1), ("gts", s)], [("m2", s)])
                    c.tt("dve", m3[s][:], banks[2][:], gts[s][:, 2, :], ALU.mult, [BK(2), ("gts", s)], [("m3", s)])
                    c.tt("dve", m1[s][:], m1[s][:], m2[s][:], ALU.add, [("m1", s), ("m2", s)], [("m1", s)])
                    c.tt("dve", mt[:, ob, :], m1[s][:], m3[s][:], ALU.add, [("m1", s), ("m3", s)], [("mT", tl % 2, ob)])
                allm = [("mT", tl % 2, ob) for ob in range(8)]
                for cc in range(4):
                    gch = tl * 4 + cc
                    s2 = gch % 2
                    c.dma("sp", xr[s2][:], x_in[gch * 128:(gch + 1) * 128, :], writes=[("xr", s2)])
                    for hf in range(2):
                        pd = banks[4 + hf]
                        for k in range(8):
                            c.mm(pd[:], mt[:, k, cc * 128:(cc + 1) * 128], wo[:, k, hf * 512:(hf + 1) * 512],
                                 k == 0, k == 7, allm + ["wo"], [BK(4 + hf)])
                        c.tt("dve", xo[s2][:, hf * 512:(hf + 1) * 512], pd[:], xr[s2][:, hf * 512:(hf + 1) * 512],
                             ALU.add, [BK(4 + hf), ("xr", s2)], [("xo", s2)])
                    c.dma("sp", xm_o[gch * 128:(gch + 1) * 128, :], xo[s2][:], reads=[("xo", s2)], writes=["d_xm"])
                    if gch == NCH - 1:
                        c.dma("sp", halo_out, xo[s2][120:128, :], reads=[("xo", s2)], writes=["d_halo"])


def halo_select(c, nc, halo_g, selp_in, dst_rows, dkey):
    with Section(c):
        hg = c.sb("hg", [8, 4, D], F32)
        sel = c.sb("sel", [8, 4], F32)
        acc = c.sb("hacc", [8, D], F32)
        c.dma("sp", hg[:], halo_g.rearrange("(r p) n -> p r n", p=8), reads=["g_halo"], writes=["hg"])
        c.dma("sp", sel[:], selp_in[0:8, :], writes=["sel"])
        c.ts("dve", acc[:], hg[:, 0, :], sel[:, 0:1], None, ALU.mult, None, ["hg", "sel"], ["hacc"])
        for r in range(1, 4):
            c.stt(acc[:], hg[:, r, :], sel[:, r:r + 1], acc[:], ALU.mult, ALU.add, ["hg", "sel", "hacc"], ["hacc"])
        c.dma("sp", dst_rows, acc[:], reads=["hacc"], writes=[dkey])


LAYER_IN = [
    ("g_a", [128, 8]), ("w_u", [D, 512]), ("w_v", [D, 512]), ("w_z", [D, 1024]), ("w_xbc", [D, 1536]),
    ("w_dt", [D, 16]), ("w_q", [D, 512]), ("w_k", [D, 512]), ("w_v2", [D, 512]), ("w_g", [D, 3072]),
    ("vgain", [1, 512]), ("wsT", [128, 4, 128]), ("bs", [1, 512]), ("scw", [128, 12, 4]), ("scb", [128, 12]),
    ("dtb", [1, 16]), ("alog", [1, 16]), ("dsk", [1, 16]), ("qg", [1, 64]), ("kg", [1, 64]),
    ("lam", [1, 256]), ("sgc", [128, 1]), ("ng", [128, 8]),
    ("w_a", [512, D]), ("w_b", [D, D]), ("w_c", [512, D]), ("w_o", [D, D]),
    ("g_f", [128, 8]), ("w_up", [D, 2 * FFN]), ("fcw", [128, 44, 3]), ("fcb", [128, 44]), ("w_dn", [FFN, D]),
]


def build_fused(nl=2):
    nc = bass.Bass("TRN2", target_bir_lowering=False)
    I32 = mybir.dt.int32

    def din(name, shape, dt=F32):
        return nc.dram_tensor(name, shape, dt, kind="ExternalInput").ap()

    def scr(name, shape, dt):
        return nc.dram_tensor(name, shape, dt, kind="Internal").ap()
    x_ext0 = din("x_ext0", [(NCH + 1) * 128, D])
    L = [{k: din(f"{k}_{l}", sh) for k, sh in LAYER_IN} for l in range(nl)]
    cos_in, sin_in = din("cos", [T, 8]), din("sin", [T, 8])
    idx_in = din("idx", [128, 4], I32)
    mlt_in, mk_in, selp_in = din("mlt", [128, 4]), din("mk", [128, 16]), din("selp", [128, 4])
    x_out = nc.dram_tensor("x_out", [T, D], F32, kind="ExternalOutput").ap()
    yaT, zs, yloc = scr("s_yaT", [512, T], BF16), scr("s_zs", [T, 1024], BF16), scr("s_yloc", [T, 1024], F32)
    eacg, CT, gT = scr("s_eacg", [T, 16], F32), scr("s_CT", [256, T], BF16), scr("s_gT", [3072, T], BF16)
    st_send, st_g = scr("s_st", [128, 1040], F32), scr("g_st", [512, 1040], F32)
    q_send = [scr(f"s_q{i}", [512, 1024], BF16) for i in range(2)]
    k_send = [scr(f"s_k{i}", [512, 1024], BF16) for i in range(2)]
    v_send = [scr(f"s_v{i}", [512, VPW[i] * 130], BF16) for i in range(3)]
    q_g = [scr(f"g_q{i}", [2048, 1024], BF16) for i in range(2)]
    k_g = [scr(f"g_k{i}", [2048, 1024], BF16) for i in range(2)]
    v_g = [scr(f"g_v{i}", [2048, VPW[i] * 130], BF16) for i in range(3)]
    y_send = [scr(f"s_y{i}", [512, 1024], BF16) for i in range(2)]
    y_g = [scr(f"g_y{i}", [2048, 1024], BF16) for i in range(2)]
    halo_send, halo_g = scr("s_halo", [8, D], F32), scr("g_halo", [32, D], F32)
    xm_ext = scr("xm_ext", [(NCH + 1) * 128, D], F32)
    x1_ext = scr("x1_ext", [(NCH + 1) * 128, D], F32)
    with ExitStack() as es:
        c = Ctx(nc, es)
        with Section(c):
            zt = c.sb("zt", [128, D], F32)
            c.op("pool", lambda E: E.memset(zt[:], 0.0), writes=["zt"])
            c.dma("sp", xm_ext[0:128, :], zt[:], reads=["zt"], writes=["d_xm"])
            c.dma("sp", x1_ext[0:128, :], zt[:], reads=["zt"], writes=["d_xout"])
        for l in range(nl):
            W = L[l]
            x_ext = x_ext0 if l == 0 else x1_ext
            last = l == nl - 1
            def ag_qkv():
                for pc in range(2):
                    c.allgather(q_send[pc], q_g[pc], [("d_qs", pc)], [("g_q", pc)])
                    c.allgather(k_send[pc], k_g[pc], [("d_ks", pc)], [("g_k", pc)])
                for vp in range(3):
                    c.allgather(v_send[vp], v_g[vp], [("d_vs", vp)], [("g_v", vp)])
            emit_front(c, nc, {
                "x_ext": x_ext, "g": W["g_a"], "w_u": W["w_u"], "w_v": W["w_v"], "w_z": W["w_z"],
                "w_xbc": W["w_xbc"], "w_dt": W["w_dt"], "w_q": W["w_q"], "w_k": W["w_k"], "w_v2": W["w_v2"],
                "w_g": W["w_g"], "vgain": W["vgain"], "wsT": W["wsT"], "bs": W["bs"], "cw": W["scw"],
                "cb": W["scb"], "dtb": W["dtb"], "alog": W["alog"], "dsk": W["dsk"], "qg": W["qg"], "kg": W["kg"],
                "cos": cos_in, "sin": sin_in, "yaT": yaT, "zs": zs, "yloc": yloc, "eacg": eacg, "CT": CT, "gT": gT,
                "st_send": st_send, "q_send": q_send, "k_send": k_send, "v_send": v_send,
                "after_qkv": ag_qkv})
            c.new_epoch()
            c.allgather(st_send, st_g, ["d_st"], ["g_st"])
            emit_attn(c, nc, {"lam": W["lam"], "sgc": W["sgc"], "idx": idx_in, "q_g": q_g, "k_g": k_g, "v_g": v_g,
                              "y_send": y_send}, l)
            c.new_epoch()
            for pc in range(2):
                c.allgather(y_send[pc], y_g[pc], [("d_ys", pc)], [("g_y", pc)])
            emit_back(c, nc, {"x_ext": x_ext, "yaT": yaT, "zs": zs, "yloc": yloc, "eacg": eacg, "CT": CT,
                              "st_g": st_g, "mlt": mlt_in, "mk": mk_in, "y_g": y_g, "gT": gT, "ng": W["ng"],
                              "w_a": W["w_a"], "w_b": W["w_b"], "w_c": W["w_c"], "w_o": W["w_o"], "idx": idx_in,
                              "x_mid": xm_ext[128:, :], "halo_out": halo_send})
            c.new_epoch()
            c.allgather(halo_send, halo_g, ["d_halo"], ["g_halo"])
            halo_select(c, nc, halo_g, selp_in, xm_ext[120:128, :], "d_xm")
            emit_ffn(c, nc, {"x_ext": xm_ext, "g": W["g_f"], "w_up": W["w_up"], "cw": W["fcw"], "cb": W["fcb"],
                             "w_dn": W["w_dn"], "x_out": x_out if last else x1_ext[128:, :],
                             "halo_out": None if last else halo_send})
            c.new_epoch()
            if not last:
                c.allgather(halo_send, halo_g, ["d_halo"], ["g_halo"])
                halo_select(c, nc, halo_g, selp_in, x1_ext[120:128, :], "d_xout")
        c.barrier()
        c.finish()
    return nc


def fused_inputs(inp, core, cos, sin, nl=2):
    b, j = core // 4, core % 4
    ca = np.ascontiguousarray
    m = {"x_ext0": ext_rows(inp["x"], b, j),
         "cos": ca(cos[j * T:(j + 1) * T]), "sin": ca(sin[j * T:(j + 1) * T]),
         "idx": ca(np.array([[r * 512 + j * 128 + p for r in range(4)] for p in range(128)], np.int32))}
    mlt = np.zeros((128, 4), np.float32)
    mk = np.zeros((128, 4, 4), np.float32)
    selp = np.zeros((128, 4), np.float32)
    for i in range(4):
        if i < j:
            mlt[:, i] = 1.0
        if i == j - 1:
            selp[:, i] = 1.0
        for k in range(4):
            if i < k < j:
                mk[:, i, k] = 1.0
    m["mlt"], m["mk"], m["selp"] = mlt, ca(mk.reshape(128, 16)), selp
    for l in range(nl):
        w_in = inp["w_in"][l]
        sec = [ca(w_in[:, IN_OFFS[i]:IN_OFFS[i + 1]]) for i in range(9)]
        d = {
            "g_a": ca(inp["attn_norm_g"][l].reshape(8, 128).T),
            "w_u": sec[0], "w_v": sec[1], "w_z": sec[2], "w_xbc": sec[3], "w_dt": sec[4],
            "w_q": sec[5], "w_k": sec[6], "w_v2": sec[7], "w_g": sec[8],
            "vgain": ca(inp["gm_v_norm_g"][l].reshape(1, 512)),
            "wsT": ca(inp["gm_w_s"][l].transpose(2, 0, 1)),
            "bs": ca(inp["gm_b_s"][l].reshape(1, 512)),
            "scw": ca(inp["ssm_conv_w"][l].reshape(12, 128, 4).transpose(1, 0, 2)),
            "scb": ca(inp["ssm_conv_b"][l].reshape(12, 128).T),
            "dtb": ca(inp["ssm_dt_bias"][l].reshape(1, 16)),
            "alog": ca(inp["ssm_a_log"][l].reshape(1, 16)),
            "dsk": ca(inp["ssm_d"][l].reshape(1, 16)),
            "qg": ca(inp["da_q_norm_g"][l].reshape(1, 64)),
            "kg": ca(inp["da_k_norm_g"][l].reshape(1, 64)),
            "lam": ca(inp["da_lambda"][l].reshape(1, 256)),
            "sgc": ca(inp["da_subln_g"][l].reshape(128, 1)),
            "ng": ca(inp["ssm_norm_g"][l].reshape(8, 128).T),
            "w_a": ca(inp["w_branch_a"][l]), "w_b": ca(inp["w_branch_b"][l]),
            "w_c": ca(inp["w_branch_c"][l]), "w_o": ca(inp["w_out"][l]),
            "g_f": ca(inp["ffn_norm_g"][l].reshape(8, 128).T),
            "w_up": ca(inp["ffn_w_up"][l]),
            "fcw": ca(inp["ffn_conv_w"][l].reshape(44, 128, 3).transpose(1, 0, 2)),
            "fcb": ca(inp["ffn_conv_b"][l].reshape(44, 128).T),
            "w_dn": ca(inp["ffn_w_down"][l]),
        }
        for k, v in d.items():
            m[f"{k}_{l}"] = v
    return m


_NC_CACHE = {}


def kernel(**inp):
    inp = {k: np.asarray(v, dtype=np.float32) for k, v in inp.items()}
    cores = list(range(NCORE))
    cos, sin = rope_tables_np()
    if "nc" not in _NC_CACHE:
        _NC_CACHE["nc"] = build_fused(2)
    maps = [fused_inputs(inp, cc, cos, sin) for cc in cores]
    res = run_bass_kernel_spmd(_NC_CACHE["nc"], maps, core_ids=cores).results
    x = np.stack([np.concatenate([res[b * 4 + j]["x_out"] for j in range(4)], 0) for b in range(2)])
    return x.astype(np.float32)
```

```python
import math
import numpy as np
from contextlib import ExitStack
import ml_dtypes
import concourse.bass as bass
import concourse.mybir as mybir
from concourse.bass_utils import run_bass_kernel_spmd

F32 = mybir.dt.float32
BF16 = mybir.dt.bfloat16
AF = mybir.ActivationFunctionType
ALU = mybir.AluOpType
AX = mybir.AxisListType
NPBF = ml_dtypes.bfloat16

D = 1024
S = 8192
NCORE = 8
T = 2048
NCH = 16
EPS = 1e-6
FFN = 2816


class Ctx:
    NDMA = 8

    def __init__(self, nc, es):
        self.nc = nc
        self.es = es
        self.top = es
        self.eng = {"pe": nc.tensor, "act": nc.scalar, "dve": nc.vector,
                    "pool": nc.gpsimd, "sp": nc.sync}
        self.sems = []
        self.esem = {}
        self.cnt = {}
        self.known = {e: {} for e in self.eng}
        for e in self.eng:
            self.esem[e] = self._newsem("s_" + e)
            self.cnt[e] = 0
        self.dsem, self.dval, self.drr = {}, {}, {}
        for q in ("sp", "pool", "act"):
            self.dsem[q] = [self._newsem(f"d_{q}{i}") for i in range(self.NDMA)]
            self.dval[q] = [0] * self.NDMA
            self.drr[q] = 0
        self.lastw = {}
        self.readers = {}
        self.n_inst = 0

    def _newsem(self, name):
        s = self.top.enter_context(self.nc.semaphore(name))
        self.sems.append(s)
        return len(self.sems) - 1

    def sb(self, name, shape, dt):
        self.n_alloc = getattr(self, "n_alloc", 0) + 1
        return self.es.enter_context(self.nc.sbuf_tensor(f"{name}_{self.n_alloc}", shape, dt))

    def ps(self, name, shape, dt):
        self.n_alloc = getattr(self, "n_alloc", 0) + 1
        return self.es.enter_context(self.nc.psum_tensor(f"{name}_{self.n_alloc}", shape, dt))

    def op16(self, e, fn, reads=(), writes=()):
        self._deps(e, reads, writes)
        i = self.drr[e]
        self.drr[e] = (i + 1) % self.NDMA
        s = self.dsem[e][i]
        v = self.dval[e][i]
        if v > 0 and self.known[e].get(s, 0) < v:
            self.eng[e].wait_ge(self.sems[s], v)
            self.known[e][s] = v
        inst = fn(self.eng[e])
        inst.then_inc(self.sems[s], 16)
        self.dval[e][i] = v + 16
        self._record((s, v + 16), reads, writes)
        self.n_inst += 1
        return inst

    def allgather(self, src, dst, skeys, dkeys):
        self._deps("pool", skeys, dkeys)
        inst = self.nc.gpsimd.collective_compute("AllGather", ALU.bypass,
                                                 replica_groups=[[0, 1, 2, 3], [4, 5, 6, 7]],
                                                 ins=[src], outs=[dst])
        s = self._newsem(f"cc{len(self.sems)}")
        inst.then_inc(self.sems[s], 1)
        self._record((s, 1), skeys, dkeys)
        self.n_inst += 1
        return inst

    def _deps(self, e, reads, writes):
        deps = {}
        own = self.esem.get(e)

        def add(tok, raw):
            if tok is None:
                return
            s, v = tok
            if deps.get(s, 0) < v:
                deps[s] = v
        for k in reads:
            add(self.lastw.get(k), True)
        for k in writes:
            add(self.lastw.get(k), False)
            for s, v in self.readers.get(k, {}).items():
                add((s, v), False)
        eng = self.eng[e]
        for s, v in deps.items():
            if e == "pe" and s == self.esem["pe"]:
                continue
            if self.known[e].get(s, 0) < v:
                eng.wait_ge(self.sems[s], v)
                self.known[e][s] = v

    def _record(self, tok, reads, writes):
        s, v = tok
        for k in reads:
            r = self.readers.setdefault(k, {})
            if r.get(s, 0) < v:
                r[s] = v
        for k in writes:
            self.lastw[k] = tok
            self.readers[k] = {}

    def op(self, e, fn, reads=(), writes=()):
        self._deps(e, reads, writes)
        inst = fn(self.eng[e])
        self.cnt[e] += 1
        inst.then_inc(self.sems[self.esem[e]], 1)
        self._record((self.esem[e], self.cnt[e]), reads, writes)
        self.n_inst += 1
        return inst

    def dma(self, q, out, in_, reads=(), writes=(), **kw):
        self._deps(q, reads, writes)
        eng = self.eng[q]
        i = self.drr[q]
        self.drr[q] = (i + 1) % self.NDMA
        s = self.dsem[q][i]
        v = self.dval[q][i]
        if v > 0 and self.known[q].get(s, 0) < v:
            eng.wait_ge(self.sems[s], v)
            self.known[q][s] = v
        inst = eng.dma_start(out=out, in_=in_, **kw)
        inst.then_inc(self.sems[s], 16)
        self.dval[q][i] = v + 16
        self._record((s, v + 16), reads, writes)
        self.n_inst += 1
        return inst

    def barrier(self):
        toks = [(self.esem[e], self.cnt[e]) for e in self.eng if self.cnt[e] > 0]
        for q in self.dsem:
            for i in range(self.NDMA):
                if self.dval[q][i] > 0:
                    toks.append((self.dsem[q][i], self.dval[q][i]))
        for e in self.eng:
            for s_, v in toks:
                if e == "pe" and s_ == self.esem["pe"]:
                    continue
                if self.known[e].get(s_, 0) < v:
                    self.eng[e].wait_ge(self.sems[s_], v)
                    self.known[e][s_] = v

    def new_epoch(self):
        self.barrier()
        for e in self.eng:
            self.esem[e] = self._newsem(f"s_{e}_{len(self.sems)}")
            self.cnt[e] = 0

    def finish(self):
        for q in self.dsem:
            for i in range(self.NDMA):
                if self.dval[q][i] > 0:
                    self.nc.sync.wait_ge(self.sems[self.dsem[q][i]], self.dval[q][i])

    def mm(self, out, lhsT, rhs, start, stop, reads, writes, **kw):
        return self.op("pe", lambda E: E.matmul(out, lhsT, rhs, start=start, stop=stop, **kw),
                       reads, writes)

    def tr(self, out, in_, ident, reads, writes):
        return self.op("pe", lambda E: E.transpose(out, in_, ident), reads, writes)

    def act(self, out, in_, func, reads, writes, **kw):
        return self.op("act", lambda E: E.activation(out=out, in_=in_, func=func, **kw), reads, writes)

    def tt(self, e, out, in0, in1, op, reads, writes):
        return self.op(e, lambda E: E.tensor_tensor(out=out, in0=in0, in1=in1, op=op), reads, writes)

    def ts(self, e, out, in0, s1, s2, op0, op1, reads, writes):
        if op1 is None:
            return self.op(e, lambda E: E.tensor_scalar(out=out, in0=in0, scalar1=s1, scalar2=None, op0=op0),
                           reads, writes)
        return self.op(e, lambda E: E.tensor_scalar(out=out, in0=in0, scalar1=s1, scalar2=s2, op0=op0, op1=op1),
                       reads, writes)

    def stt(self, out, in0, scalar, in1, op0, op1, reads, writes):
        return self.op("dve", lambda E: E.scalar_tensor_tensor(out=out, in0=in0, scalar=scalar, in1=in1,
                                                              op0=op0, op1=op1), reads, writes)

    def copy(self, e, out, in_, reads, writes):
        if e == "act":
            return self.act(out, in_, AF.Copy, reads, writes)
        return self.op(e, lambda E: E.tensor_copy(out=out, in_=in_), reads, writes)


def make_ident(c, ident):
    c.op("pool", lambda E: E.memset(ident[:], 1.0), writes=["ident"])
    c.op("pool", lambda E: E.affine_select(out=ident[:], in_=ident[:], pattern=[[-1, 128]],
                                           compare_op=ALU.is_equal, fill=0.0, base=0,
                                           channel_multiplier=1), reads=["ident"], writes=["ident"])


def build_hT(c, x_ext, g_sb, hT, nch, ident, pT_bank, pfx="h"):
    pT, pkey = pT_bank
    xt = [c.sb(f"{pfx}_xt{i}", [128, D], F32) for i in range(2)]
    xs = [c.sb(f"{pfx}_xs{i}", [128, D], BF16) for i in range(2)]
    junk = c.sb(f"{pfx}_junk", [128, D], BF16)
    ss = [c.sb(f"{pfx}_ss{i}", [128, 4], F32) for i in range(2)]
    for ch in range(nch):
        s = ch % 2
        c.dma("sp", xt[s][:], x_ext[ch * 128:(ch + 1) * 128, :], writes=[(pfx + "xt", s)])
        c.act(junk[:], xt[s][:], AF.Square, [(pfx + "xt", s)], [pfx + "junk", (pfx + "ss", s)],
              accum_out=ss[s][:, 0:1])
        c.act(ss[s][:, 1:2], ss[s][:, 0:1], AF.Ln, [(pfx + "ss", s), "eps"], [(pfx + "ss", s)],
              scale=1.0 / D, bias=EPS_AP[0])
        c.act(ss[s][:, 2:3], ss[s][:, 1:2], AF.Exp, [(pfx + "ss", s)], [(pfx + "ss", s)], scale=-0.5)
        c.act(xs[s][:], xt[s][:], AF.Copy, [(pfx + "xt", s), (pfx + "ss", s)], [(pfx + "xs", s)],
              scale=ss[s][:, 2:3])
        for k in range(8):
            c.tr(pT[:, k * 128:(k + 1) * 128], xs[s][:, k * 128:(k + 1) * 128], ident[:],
                 [(pfx + "xs", s), "ident"], [pkey])
        c.tt("dve", hT[:, :, ch * 128:(ch + 1) * 128],
             pT.rearrange("p (k t) -> p k t", k=8),
             g_sb[:, :].unsqueeze(2).to_broadcast([128, 8, 128]), ALU.mult,
             [pkey, "g_sb"], [("hT", ch)])


EPS_AP = [None]


def setup_consts(c):
    eps = c.sb("eps_t", [128, 1], F32)
    c.op("pool", lambda E: E.memset(eps[:], EPS), writes=["eps"])
    EPS_AP[0] = eps[:, 0:1]
    one = c.sb("one_t", [128, 1], F32)
    c.op("pool", lambda E: E.memset(one[:], 1.0), writes=["eps"])
    ONE_AP[0] = one[:, 0:1]
    ident = c.sb("ident", [128, 128], BF16)
    make_ident(c, ident)
    return ident


def emit_ffn(c, nc, io):
    x_ext, g_in, w_up, cw_in, cb_in, w_dn, x_out = (io[k] for k in ("x_ext", "g", "w_up", "cw", "cb", "w_dn", "x_out"))
    halo_out = io.get("halo_out")
    with Section(c):
        ident = setup_consts(c)
        banks = [c.ps(f"bank{i}", [128, 512], F32) for i in range(8)]
        g_sb = c.sb("g_sb", [128, 8], F32)
        cw = c.sb("cw_sb", [128, 44, 3], F32)
        cb = c.sb("cb_sb", [128, 44], F32)
        c.dma("sp", g_sb[:], g_in, writes=["g_sb"])
        c.dma("sp", cw[:], cw_in, writes=["cw"])
        c.dma("sp", cb[:], cb_in, writes=["cb"])
        hT = c.sb("hT", [128, 8, (NCH + 1) * 128], BF16)
        wd = c.sb("wd", [128, 22, D], BF16)
        c.dma("pool", wd[:], w_dn.rearrange("(i p) n -> p i n", p=128), writes=["wd"])
        build_hT(c, x_ext, g_sb, hT, NCH + 1, ident, (banks[7][:].bitcast(BF16), "bank7"))
        allhT = [("hT", ch) for ch in range(NCH + 1)]
        actT = c.sb("actT", [128, 22, 1024], BF16)
        wu = [c.sb(f"wu{i}", [128, 2, 8, 128], BF16) for i in range(2)]
        acc = [[c.sb(f"acc{gv}{i}", [128, 512], F32) for i in range(2)] for gv in range(2)]
        sg = [c.sb(f"sg{i}", [128, 512], F32) for i in range(2)]
        xbuf = [[c.sb(f"xbuf{gv}{i}", [128, 514], BF16) for i in range(2)] for gv in range(2)]
        pend = None
        xr = [c.sb(f"xr{i}", [128, D], F32) for i in range(2)]
        xo = [c.sb(f"xo{i}", [128, D], F32) for i in range(2)]
        w_up_v = w_up.rearrange("(k p) n -> p k n", p=128)
        un = 0
        for ps_ in range(2):
            for i in range(22):
                s = (ps_ * 22 + i) % 2
                for gv in range(2):
                    col0 = gv * FFN + i * 128
                    c.dma("pool", wu[s][:, gv], w_up_v[:, :, col0:col0 + 128], writes=[("wu", s)])
                for tl in range(2):
                    t0 = ps_ * 1024 + tl * 512
                    c0 = 128 + t0
                    b2 = un % 2
                    un += 1
                    for gv in range(2):
                        fb = gv * 22 + i
                        pk = ("pg", gv, b2)
                        pt = banks[gv * 2 + b2]
                        xb = xbuf[gv][b2]
                        xk = ("xb", gv, b2)
                        if tl == 0:
                            hk = "bank6"
                            ph = banks[6][:, gv * 2:gv * 2 + 2]
                            for k in range(8):
                                c.mm(ph, wu[s][:, gv, k, :], hT[:, k, c0 - 2:c0], k == 0, k == 7,
                                     [("wu", s)] + allhT, [hk])
                            c.copy("act", xb[:, 0:2], ph, [hk], [xk])
                        else:
                            c.copy("act", xb[:, 0:2], xbuf[gv][1 - b2][:, 512:514], [("xb", gv, 1 - b2)], [xk])
                        for k in range(8):
                            c.mm(pt[:], wu[s][:, gv, k, :], hT[:, k, c0:c0 + 512], k == 0, k == 7,
                                 [("wu", s)] + allhT, [pk])
                        a = acc[gv][b2]
                        ak = ("acc", gv, b2)
                        c.act(xb[:, 2:514], pt[:], AF.Copy, [pk], [xk])
                        c.act(a[:], pt[:], AF.Identity, [pk, "cw", "cb"], [ak],
                              scale=cw[:, fb, 2:3], bias=cb[:, fb:fb + 1])
                        c.stt(a[:], xb[:, 1:513], cw[:, fb, 1:2], a[:], ALU.mult, ALU.add, [xk, ak, "cw"], [ak])
                        c.stt(a[:], xb[:, 0:512], cw[:, fb, 0:1], a[:], ALU.mult, ALU.add, [xk, ak, "cw"], [ak])
                    if pend is not None:
                        pend()
                    def tail(b2=b2, i=i, tl=tl):
                        c.act(sg[b2][:], acc[0][b2][:], AF.Silu, [("acc", 0, b2)], [("sg", b2)])
                        c.tt("dve", actT[:, i, tl * 512:(tl + 1) * 512], sg[b2][:], acc[1][b2][:], ALU.mult,
                             [("sg", b2), ("acc", 1, b2)], [("actT", i)])
                    pend = tail
            pend()
            pend = None
            allact = [("actT", i) for i in range(22)]
            for ch in range(8):
                gch = ps_ * 8 + ch
                s = gch % 2
                c.dma("sp", xr[s][:], x_ext[(gch + 1) * 128:(gch + 2) * 128, :], writes=[("xr", s)])
                for hf in range(2):
                    pd = banks[4 + hf]
                    for i in range(22):
                        c.mm(pd[:], actT[:, i, ch * 128:(ch + 1) * 128], wd[:, i, hf * 512:(hf + 1) * 512],
                             i == 0, i == 21, allact + ["wd"], [("pd", hf)])
                    c.tt("dve", xo[s][:, hf * 512:(hf + 1) * 512], pd[:], xr[s][:, hf * 512:(hf + 1) * 512],
                         ALU.add, [("pd", hf), ("xr", s)], [("xo", s)])
                c.dma("sp", x_out[gch * 128:(gch + 1) * 128, :], xo[s][:], reads=[("xo", s)], writes=["d_xout"])
                if halo_out is not None and gch == NCH - 1:
                    c.dma("sp", halo_out, xo[s][120:128, :], reads=[("xo", s)], writes=["d_halo"])


class Section:
    def __init__(self, c, keep=False):
        self.c = c
        self.keep = keep

    def __enter__(self):
        self.old = self.c.es
        self.st = ExitStack()
        self.st.__enter__()
        self.c.es = self.st
        return self

    def __exit__(self, *a):
        if self.keep and a[0] is None:
            if not hasattr(self.c, "kept"):
                self.c.kept = []
            self.c.kept.append(self.st)
            self.c.es = self.old
            return False
        self.c.barrier()
        self.c.es = self.old
        return self.st.__exit__(*a)


def release_kept(c):
    kept = getattr(c, "kept", [])
    if kept:
        c.barrier()
        while kept:
            kept.pop().__exit__(None, None, None)


def gelu_tanh(c, src, skeys, out, okeys, W, sl, n):
    xs, sq, t = W["gx"][sl], W["gs"][sl], W["gt"][sl]
    kx, ks, kt = ("gx", sl), ("gs", sl), ("gt", sl)
    c.act(xs[:, :n], src, AF.Copy, skeys, [kx])
    c.tt("dve", sq[:, :n], xs[:, :n], src, ALU.mult, [kx] + list(skeys), [ks])
    c.ts("dve", t[:, :n], sq[:, :n], 0.044715, 1.0, ALU.mult, ALU.add, [ks], [kt])
    c.tt("dve", sq[:, :n], t[:, :n], xs[:, :n], ALU.mult, [kt, kx], [ks])
    c.act(t[:, :n], sq[:, :n], AF.Sigmoid, [ks], [kt], scale=1.5957691216057308)
    c.tt("dve", out, t[:, :n], xs[:, :n], ALU.mult, [kt, kx], okeys)


def gelu_work(c):
    return {k: [c.sb(f"{k}{i}", [128, 512], F32) for i in range(2)] for k in ("gx", "gs", "gt")}


def rstd_from_ss(c, ss, key, n_inv, w=1):
    c.act(ss[:, w:2 * w], ss[:, 0:w], AF.Ln, [key, "eps"], [key], scale=n_inv, bias=EPS_AP[0])
    c.act(ss[:, 2 * w:3 * w], ss[:, w:2 * w], AF.Exp, [key], [key], scale=-0.5)


def emit_front(c, nc, io):
    (x_ext, g_in, w_u, w_v, w_z, w_xbc, w_dt, w_q, w_k, w_v2, w_g, vgain, wsT_in, bs_in, cw_in, cb_in, dtb_in,
     alog_in, dsk_in, qg_in, kg_in, cos_in, sin_in) = (io[k] for k in (
        "x_ext", "g", "w_u", "w_v", "w_z", "w_xbc", "w_dt", "w_q", "w_k", "w_v2", "w_g", "vgain", "wsT", "bs",
        "cw", "cb", "dtb", "alog", "dsk", "qg", "kg", "cos", "sin"))
    yaT_o, zs_o, yloc_o, eacg_o, CT_o, gT_o = (io[k] for k in ("yaT", "zs", "yloc", "eacg", "CT", "gT"))
    st_send = io["st_send"]
    q_send, k_send, v_send = io["q_send"], io["k_send"], io["v_send"]

    def wview(w):
        return w.rearrange("(k p) n -> p k n", p=128)

    with Section(c):
        ident = setup_consts(c)
        banks = [c.ps(f"bank{i}", [128, 512], F32) for i in range(8)]
        BK = lambda i: ("bank", i)
        g_sb = c.sb("g_sb", [128, 8], F32)
        c.dma("sp", g_sb[:], g_in, writes=["g_sb"])
        hT = c.sb("hT", [128, 8, (NCH + 1) * 128], BF16)
        build_hT(c, x_ext, g_sb, hT, NCH + 1, ident, (banks[7][:].bitcast(BF16), BK(7)))
        allhT = [("hT", ch) for ch in range(NCH + 1)]
        dt_all = c.sb("dt_all", [128, NCH, 16], F32)
        xbcT = c.sb("xbcT", [128, 12, T], BF16)

        with Section(c, keep=True):
            W = gelu_work(c)
            uT = c.sb("uT", [128, 4, T], BF16)
            wblk = [c.sb(f"wblk{i}", [128, 8, 128], BF16) for i in range(2)]
            un = 0
            for fb in range(4):
                s = fb % 2
                c.dma("pool", wblk[s][:], wview(w_u)[:, :, fb * 128:(fb + 1) * 128], writes=[("wblk", s)])
                for tl in range(4):
                    b = un % 2
                    un += 1
                    for k in range(8):
                        c.mm(banks[b][:], wblk[s][:, k, :], hT[:, k, 128 + tl * 512:128 + (tl + 1) * 512],
                             k == 0, k == 7, [("wblk", s)] + allhT, [BK(b)])
                    gelu_tanh(c, banks[b][:], [BK(b)], uT[:, fb, tl * 512:(tl + 1) * 512], [("uT", fb)], W, b, 512)
            alluT = [("uT", fb) for fb in range(4)]
            ws_f = c.sb("ws_f", [128, 4, 128], F32)
            ws_b = c.sb("ws_b", [128, 4, 128], BF16)
            c.dma("sp", ws_f[:], wsT_in, writes=["ws_f"])
            c.op("pool", lambda E: E.affine_select(out=ws_f[:], in_=ws_f[:], pattern=[[0, 4], [1, 128]],
                                                   compare_op=ALU.is_ge, fill=0.0, base=0,
                                                   channel_multiplier=-1), ["ws_f"], ["ws_f"])
            c.copy("pool", ws_b[:], ws_f[:], ["ws_f"], ["ws_b"])
            bs_f = c.sb("bs_f", [1, 512], F32)
            bs_b = c.sb("bs_b", [1, 512], BF16)
            ones_r = c.sb("ones_r", [1, 128], BF16)
            c.dma("sp", bs_f[:], bs_in, writes=["bs_f"])
            c.copy("pool", bs_b[:], bs_f[:], ["bs_f"], ["bs_b"])
            c.op("pool", lambda E: E.memset(ones_r[:], 1.0), writes=["ones_r"])
            vg_bc = c.sb("vg_bc", [128, 512], F32)
            c.dma("sp", vg_bc[:], vgain.partition_broadcast(128), writes=["vg_bc"])
            wv = c.sb("wv", [128, 8, 512], BF16)
            c.dma("pool", wv[:], wview(w_v), writes=["wv"])
            gv = [c.sb(f"gv{i}", [128, 512], F32) for i in range(2)]
            vn = [c.sb(f"vn{i}", [128, 512], BF16) for i in range(2)]
            vss = [c.sb(f"vss{i}", [128, 4], F32) for i in range(2)]
            vjunk = c.sb("vjunk", [128, 512], BF16)
            def v_stage1(ch):
                s = ch % 2
                cs = slice(128 + ch * 128, 128 + (ch + 1) * 128)
                pb = 2 + s
                for k in range(8):
                    c.mm(banks[pb][:], hT[:, k, cs], wv[:, k, :], k == 0, k == 7, allhT + ["wv"], [BK(pb)])
                gelu_tanh(c, banks[pb][:], [BK(pb)], gv[s][:], [("gv", s)], W, s, 512)

            def v_stage2(ch):
                s = ch % 2
                c.act(vjunk[:], gv[s][:], AF.Square, [("gv", s)], ["vjunk", ("vss", s)], accum_out=vss[s][:, 0:1])
                rstd_from_ss(c, vss[s], ("vss", s), 1.0 / 512)
                c.stt(vn[s][:], gv[s][:], vss[s][:, 2:3], vg_bc[:], ALU.mult, ALU.mult,
                      [("gv", s), ("vss", s), "vg_bc"], [("vn", s)])

            def v_stage3(ch):
                s = ch % 2
                pm = 4 + s
                for g in range(4):
                    c.mm(banks[pm][:, g * 128:(g + 1) * 128], vn[s][:, g * 128:(g + 1) * 128], ws_b[:, g, :],
                         True, False, [("vn", s), "ws_b"], [BK(pm)])
                    c.mm(banks[pm][:, g * 128:(g + 1) * 128], ones_r[0:1, :], bs_b[0:1, g * 128:(g + 1) * 128],
                         False, True, ["ones_r", "bs_b"], [BK(pm)])
                uv = uT[:, :, ch * 128:(ch + 1) * 128]
                c.tt("dve", uv, uv, banks[pm][:].rearrange("p (g t) -> p g t", g=4), ALU.mult,
                     alluT + [BK(pm)], [("ya", ch)])

            v_stage1(0)
            for ch in range(NCH):
                if ch + 1 < NCH:
                    v_stage1(ch + 1)
                v_stage2(ch)
                v_stage3(ch)
            c.dma("sp", yaT_o.rearrange("(g p) t -> p g t", p=128), uT[:],
                  reads=[("ya", ch) for ch in range(NCH)], writes=["d_yaT"])

        with Section(c, keep=True):
            wblk = [c.sb(f"gwblk{i}", [128, 8, 128], BF16) for i in range(2)]
            gt = [c.sb(f"gt{i}", [128, T], BF16) for i in range(2)]
            un = 0
            for fb in range(24):
                s = fb % 2
                c.dma("pool", wblk[s][:], wview(w_g)[:, :, fb * 128:(fb + 1) * 128], writes=[("gwblk", s)])
                for tl in range(4):
                    b = un % 4
                    un += 1
                    for k in range(8):
                        c.mm(banks[b][:], wblk[s][:, k, :], hT[:, k, 128 + tl * 512:128 + (tl + 1) * 512],
                             k == 0, k == 7, [("gwblk", s)] + allhT, [BK(b)])
                    c.act(gt[s][:, tl * 512:(tl + 1) * 512], banks[b][:], AF.Sigmoid, [BK(b)], [("gt", s)])
                c.dma("sp", gT_o[fb * 128:(fb + 1) * 128, :], gt[s][:], reads=[("gt", s)], writes=["d_gT"])

        with Section(c, keep=True):
            xbc_conv(c, banks, hT, allhT, wview(w_xbc), cw_in, cb_in, xbcT, CT_o)

        release_kept(c)
        with Section(c):
            wz = c.sb("wz", [128, 8, 1024], BF16)
            wq = c.sb("wq", [128, 8, 512], BF16)
            wk = c.sb("wk", [128, 8, 512], BF16)
            wv2 = c.sb("wv2", [128, 8, 512], BF16)
            wdt = c.sb("wdt", [128, 8, 16], BF16)
            for t_, w_, k_ in ((wz, w_z, "wz"), (wdt, w_dt, "wdt"), (wq, w_q, "wq"), (wk, w_k, "wk"), (wv2, w_v2, "wv2")):
                c.dma("pool", t_[:], wview(w_), writes=[k_])
            dtb_bc = c.sb("dtb_bc", [128, 16], F32)
            c.dma("sp", dtb_bc[:], dtb_in.partition_broadcast(128), writes=["dtb_bc"])
            qg_bc = c.sb("qg_bc", [128, 64], F32)
            kg_bc = c.sb("kg_bc", [128, 64], F32)
            c.dma("sp", qg_bc[:], qg_in.partition_broadcast(128), writes=["qg_bc"])
            c.dma("sp", kg_bc[:], kg_in.partition_broadcast(128), writes=["kg_bc"])
            c.ts("dve", qg_bc[:], qg_bc[:], 0.125, None, ALU.mult, None, ["qg_bc"], ["qg_bc"])
            cos_sb = c.sb("cos_sb", [128, NCH, 8], F32)
            sin_sb = c.sb("sin_sb", [128, NCH, 8], F32)
            c.dma("sp", cos_sb[:], cos_in.rearrange("(c p) f -> p c f", p=128), writes=["cos_sb"])
            c.dma("sp", sin_sb[:], sin_in.rearrange("(c p) f -> p c f", p=128), writes=["sin_sb"])
            qT_sb = c.sb("qT_sb", [128, 4, T], BF16)
            kT_sb = c.sb("kT_sb", [128, 4, T], BF16)
            zsb = [c.sb(f"zsb{i}", [128, 1024], BF16) for i in range(2)]
            va = [c.sb(f"va{i}", [128, 4, 130], BF16) for i in range(2)]
            for i in range(2):
                c.op("pool", lambda E: E.memset(va[i][:, :, 128:129], 1.0), writes=[("va", i)])
                c.op("pool", lambda E: E.memset(va[i][:, :, 129:130], 0.0), writes=[("va", i)])
            sp_t = [c.sb(f"sp_t{i}", [128, 4, 16], F32) for i in range(2)]
            qsq = [c.sb(f"qsq{i}", [128, 512], F32) for i in range(2)]
            qn = [c.sb(f"qn{i}", [128, 512], F32) for i in range(2)]
            qb = [c.sb(f"qb{i}", [128, 512], BF16) for i in range(4)]
            pend_tr = []
            qss = [c.sb(f"qss{i}", [128, 24], F32) for i in range(2)]
            rp = [c.sb(f"rp{i}", [128, 4, 8, 8], F32) for i in range(2)]
            pTb = banks[7][:].bitcast(BF16)
            it = 0
            for ch in range(NCH):
                s = ch % 2
                cs = slice(128 + ch * 128, 128 + (ch + 1) * 128)
                for hf in range(2):
                    for k in range(8):
                        c.mm(banks[hf][:], hT[:, k, cs], wz[:, k, hf * 512:(hf + 1) * 512], k == 0, k == 7,
                             allhT + ["wz"], [BK(hf)])
                    c.act(zsb[s][:, hf * 512:(hf + 1) * 512], banks[hf][:], AF.Silu, [BK(hf)], [("zsb", s)])
                c.dma("sp", zs_o[ch * 128:(ch + 1) * 128, :], zsb[s][:], reads=[("zsb", s)], writes=["d_zs"])
                for k in range(8):
                    c.mm(banks[6][:, 0:16], hT[:, k, cs], wdt[:, k, :], k == 0, k == 7, allhT + ["wdt"], [BK(6)])
                st = sp_t[s]
                ks_ = ("sp_t", s)
                c.tt("dve", st[:, 0, :], banks[6][:, 0:16], dtb_bc[:], ALU.add, [BK(6), "dtb_bc"], [ks_])
                c.ts("dve", st[:, 1, :], st[:, 0, :], -1.0, None, ALU.mult, None, [ks_], [ks_])
                c.tt("dve", st[:, 1, :], st[:, 1, :], st[:, 0, :], ALU.max, [ks_], [ks_])
                c.act(st[:, 2, :], st[:, 1, :], AF.Exp, [ks_], [ks_], scale=-1.0)
                c.act(st[:, 3, :], st[:, 2, :], AF.Ln, [ks_, "eps"], [ks_], bias=ONE_AP[0])
                c.ts("dve", st[:, 1, :], st[:, 0, :], 0.0, None, ALU.max, None, [ks_], [ks_])
                c.tt("dve", dt_all[:, ch, :], st[:, 1, :], st[:, 3, :], ALU.add, [ks_], [("dt_all", ch)])
                for (w_t, wkey, g_bc, gkey, dstT, dkey) in ((wq, "wq", qg_bc, "qg_bc", qT_sb, "qT"),
                                                            (wk, "wk", kg_bc, "kg_bc", kT_sb, "kT")):
                    b2 = it % 2
                    b4 = it % 4
                    it += 1
                    pb = 2 + b2
                    for k in range(8):
                        c.mm(banks[pb][:], hT[:, k, cs], w_t[:, k, :], k == 0, k == 7, allhT + [wkey], [BK(pb)])
                    if len(pend_tr) >= 2:
                        pend_tr.pop(0)()
                    c.act(qsq[b2][:], banks[pb][:], AF.Square, [BK(pb)], [("qsq", b2)])
                    c.op("dve", lambda E: E.tensor_reduce(out=qss[b2][:, 0:8],
                                                         in_=qsq[b2][:].rearrange("p (g d) -> p g d", g=8),
                                                         axis=AX.X, op=ALU.add), [("qsq", b2)], [("qss", b2)])
                    rstd_from_ss(c, qss[b2], ("qss", b2), 1.0 / 64, w=8)
                    c.tt("dve", qn[b2][:].rearrange("p (g d) -> p g d", g=8),
                         banks[pb][:].rearrange("p (g d) -> p g d", g=8),
                         qss[b2][:, 16:24].unsqueeze(2).to_broadcast([128, 8, 64]), ALU.mult,
                         [BK(pb), ("qss", b2)], [("qn", b2)])
                    q3 = qn[b2][:].rearrange("p (g d) -> p g d", g=8)
                    c.tt("dve", q3, q3, g_bc[:, :].unsqueeze(1).to_broadcast([128, 8, 64]), ALU.mult,
                         [("qn", b2), gkey], [("qn", b2)])
                    c.copy("act", qb[b4][:], qn[b2][:], [("qn", b2)], [("qb", b4)])
                    qb3 = qb[b4][:].rearrange("p (g d) -> p g d", g=8)
                    cosb = cos_sb[:, ch, :].unsqueeze(1).to_broadcast([128, 8, 8])
                    sinb = sin_sb[:, ch, :].unsqueeze(1).to_broadcast([128, 8, 8])
                    r = rp[b2]
                    rk = ("rp", b2)
                    c.tt("dve", r[:, 0], q3[:, :, 0:8], cosb, ALU.mult, [("qn", b2), "cos_sb"], [rk])
                    c.tt("dve", r[:, 1], q3[:, :, 8:16], sinb, ALU.mult, [("qn", b2), "sin_sb"], [rk])
                    c.tt("dve", r[:, 2], q3[:, :, 8:16], cosb, ALU.mult, [("qn", b2), "cos_sb"], [rk])
                    c.tt("dve", r[:, 3], q3[:, :, 0:8], sinb, ALU.mult, [("qn", b2), "sin_sb"], [rk])
                    c.tt("dve", qb3[:, :, 0:8], r[:, 0], r[:, 1], ALU.subtract, [rk, ("qb", b4)], [("qb", b4)])
                    c.tt("dve", qb3[:, :, 8:16], r[:, 2], r[:, 3], ALU.add, [rk, ("qb", b4)], [("qb", b4)])

                    def do_tr(b4=b4, dstT=dstT, dkey=dkey, ch=ch):
                        for h in range(4):
                            c.tr(pTb[:, h * 128:(h + 1) * 128], qb[b4][:, h * 128:(h + 1) * 128], ident[:],
                                 [("qb", b4), "ident"], [BK(7)])
                        c.copy("act", dstT[:, :, ch * 128:(ch + 1) * 128],
                               pTb[:, 0:512].rearrange("p (h t) -> p h t", h=4), [BK(7)], [(dkey, ch)])
                    pend_tr.append(do_tr)
                for k in range(8):
                    c.mm(banks[4 + s][:], hT[:, k, cs], wv2[:, k, :], k == 0, k == 7, allhT + ["wv2"], [BK(4 + s)])
                c.copy("dve", va[s][:, :, 0:128], banks[4 + s][:].rearrange("p (h d) -> p h d", h=4),
                       [BK(4 + s)], [("va", s)])
                vp, lb = VPIECE[ch]
                c.dma("sp", v_send[vp].rearrange("(h p) (b d) -> p h b d", p=128, d=130)[:, :, lb, :], va[s][:],
                      reads=[("va", s)], writes=[("d_vs", vp)])
            while pend_tr:
                pend_tr.pop(0)()
            for pc in range(2):
                c.dma("sp", q_send[pc].rearrange("(h p) t -> p h t", p=128), qT_sb[:, :, pc * 1024:(pc + 1) * 1024],
                      reads=[("qT", ch) for ch in range(NCH)], writes=[("d_qs", pc)])
                c.dma("sp", k_send[pc].rearrange("(h p) t -> p h t", p=128), kT_sb[:, :, pc * 1024:(pc + 1) * 1024],
                      reads=[("kT", ch) for ch in range(NCH)], writes=[("d_ks", pc)])

        if io.get("after_qkv") is not None:
            io["after_qkv"]()
        with Section(c):
            ssd_loop(c, nc, banks, ident, dt_all, alog_in, dsk_in, xbcT, yloc_o, eacg_o,
                     st_send[:, 0:1024], st_send[:, 1024:1040])


ONE_AP = [None]


def xbc_conv(c, banks, hT, allhT, wxv, cw_in, cb_in, xbcT, CT_o):
    BK = lambda i: ("bank", i)
    cw = c.sb("scw", [128, 12, 4], F32)
    cb = c.sb("scb", [128, 12], F32)
    c.dma("sp", cw[:], cw_in, writes=["scw"])
    c.dma("sp", cb[:], cb_in, writes=["scb"])
    wblk = [c.sb(f"xwblk{i}", [128, 8, 128], BF16) for i in range(2)]
    acc = [c.sb(f"xacc{i}", [128, 512], F32) for i in range(2)]
    xbuf = [c.sb(f"xxb{i}", [128, 515], BF16) for i in range(2)]
    un = 0
    pend = None
    for fb in range(12):
        s = fb % 2
        c.dma("pool", wblk[s][:], wxv[:, :, fb * 128:(fb + 1) * 128], writes=[("xwblk", s)])
        for tl in range(4):
            b = un % 2
            un += 1
            c0 = 128 + tl * 512
            pt = banks[b]
            pk = BK(b)
            xb = xbuf[b]
            xk = ("xxb", b)
            if tl == 0:
                ph = banks[6][:, 0:3]
                for k in range(8):
                    c.mm(ph, wblk[s][:, k, :], hT[:, k, c0 - 3:c0], k == 0, k == 7, [("xwblk", s)] + allhT, [BK(6)])
                c.copy("act", xb[:, 0:3], ph, [BK(6)], [xk])
            else:
                c.copy("act", xb[:, 0:3], xbuf[1 - b][:, 512:515], [("xxb", 1 - b)], [xk])
            for k in range(8):
                c.mm(pt[:], wblk[s][:, k, :], hT[:, k, c0:c0 + 512], k == 0, k == 7, [("xwblk", s)] + allhT, [pk])
            a = acc[b]
            ak = ("xacc", b)
            c.act(xb[:, 3:515], pt[:], AF.Copy, [pk], [xk])
            c.act(a[:], pt[:], AF.Identity, [pk, "scw", "scb"], [ak], scale=cw[:, fb, 3:4], bias=cb[:, fb:fb + 1])
            for sh in (1, 2, 3):
                c.stt(a[:], xb[:, 3 - sh:515 - sh], cw[:, fb, 3 - sh:4 - sh], a[:], ALU.mult, ALU.add,
                      [xk, ak, "scw"], [ak])
            if pend is not None:
                pend()

            def tail(b=b, fb=fb, tl=tl):
                c.act(xbcT[:, fb, tl * 512:(tl + 1) * 512], acc[b][:], AF.Silu, [("xacc", b)], [("xbcT", fb)])
            pend = tail
    pend()
    allx = [("xbcT", fb) for fb in range(12)]
    c.dma("sp", CT_o.rearrange("(g p) t -> p g t", p=128), xbcT[:, 10:12, :], reads=allx, writes=["d_CT"])


def ssd_loop(c, nc, banks, ident, dt_all, alog_in, dsk_in, xbcT, yloc_o, eacg_o, sfin_o, logd_o):
    BK = lambda i: ("bank", i)
    allx = [("xbcT", fb) for fb in range(12)]
    triT = c.sb("triT", [128, 128], F32)
    tri_b = c.sb("tri_b", [128, 128], BF16)
    lstr = c.sb("lstr", [128, 128], F32)
    ones = c.sb("ones_f", [128, 128], F32)
    c.op("pool", lambda E: E.memset(ones[:], 1.0), writes=["ones_f"])
    c.op("pool", lambda E: E.memset(triT[:], 1.0), writes=["triT"])
    c.op("pool", lambda E: E.affine_select(out=triT[:], in_=triT[:], pattern=[[1, 128]], compare_op=ALU.is_ge,
                                           fill=0.0, base=0, channel_multiplier=-1), ["triT"], ["triT"])
    c.copy("pool", tri_b[:], triT[:], ["triT"], ["tri_b"])
    c.op("pool", lambda E: E.memset(lstr[:], 1.0), writes=["lstr"])
    c.op("pool", lambda E: E.affine_select(out=lstr[:], in_=lstr[:], pattern=[[-1, 128]], compare_op=ALU.is_gt,
                                           fill=0.0, base=0, channel_multiplier=1), ["lstr"], ["lstr"])
    a_bc = c.sb("a_bc", [128, 16], F32)
    d_bc = c.sb("d_bc", [128, 16], F32)
    c.dma("sp", a_bc[:], alog_in.partition_broadcast(128), writes=["a_bc"])
    c.dma("sp", d_bc[:], dsk_in.partition_broadcast(128), writes=["d_bc"])
    c.act(a_bc[:], a_bc[:], AF.Exp, ["a_bc"], ["a_bc"])
    c.ts("dve", a_bc[:], a_bc[:], -1.0, None, ALU.mult, None, ["a_bc"], ["a_bc"])
    state = c.sb("state", [128, 1024], F32)
    prev_bf = c.sb("prev_bf", [128, 1024], BF16)
    offs = c.sb("offs", [128, 16], F32)
    eacg_all = c.sb("eacg_all", [128, NCH, 16], F32)
    c.op("pool", lambda E: E.memset(state[:], 0.0), writes=["state"])
    c.op("pool", lambda E: E.memset(prev_bf[:], 0.0), writes=["prev_bf"])
    c.op("pool", lambda E: E.memset(offs[:], 0.0), writes=["offs"])
    xs_tok = [c.sb(f"xs_tok{i}", [128, 1024], BF16) for i in range(2)]
    B_tok = [c.sb(f"B_tok{i}", [128, 256], BF16) for i in range(2)]
    sm = [c.sb(f"sm{i}", [128, 12, 16], F32) for i in range(2)]
    Lh_ = [c.sb(f"Lh{i}", [128, 16, 128], BF16) for i in range(2)]
    decT_ = [c.sb(f"decT{i}", [128, 16, 128], BF16) for i in range(2)]
    cb_sb_ = [c.sb(f"cb_sb{i}", [128, 2, 128], BF16) for i in range(2)]
    MT = [c.sb(f"MT{i}", [128, 16, 128], BF16) for i in range(2)]
    xdt = [c.sb(f"xdt{i}", [128, 1024], BF16) for i in range(2)]
    xdt2 = [c.sb(f"xdt2{i}", [128, 1024], BF16) for i in range(2)]
    ydt_ = [c.sb(f"ydt{i}", [128, 1024], F32) for i in range(2)]
    ytmp = c.sb("ytmp", [128, 1024], F32)
    yl = [c.sb(f"yl{i}", [128, 1024], F32) for i in range(2)]
    t2_ = [c.sb(f"t2{i}", [128, 1024], F32) for i in range(2)]
    pA = banks[0][:].bitcast(BF16)
    pC = banks[1]
    pCb = banks[1][:].bitcast(BF16)

    def stage_a(ch):
        s = ch % 2
        cs = slice(ch * 128, (ch + 1) * 128)
        v = sm[s]
        vk = ("sm", s)
        Lh, decT, cb_sb, t2, ydt = Lh_[s], decT_[s], cb_sb_[s], t2_[s], ydt_[s]
        for fb in range(8):
            c.tr(pA[:, fb * 128:(fb + 1) * 128], xbcT[:, fb, cs], ident[:], allx + ["ident"], [BK(0)])
        c.copy("act", xs_tok[s][:], pA, [BK(0)], [("xs_tok", s)])
        for g in range(2):
            c.tr(pCb[:, 640 + g * 128:640 + (g + 1) * 128], xbcT[:, 8 + g, cs], ident[:], allx + ["ident"], [BK(1)])
        c.copy("act", B_tok[s][:], pCb[:, 640:896], [BK(1)], [("B_tok", s)])
        dt = dt_all[:, ch, :]
        c.tt("dve", v[:, 0, :], dt, a_bc[:], ALU.mult, [("dt_all", ch), "a_bc"], [vk])
        c.mm(pC[:, 0:16], triT[:], v[:, 0, :], True, True, ["triT", vk], [BK(1)])
        c.mm(pC[:, 16:32], ones[:], v[:, 0, :], True, True, ["ones_f", vk], [BK(1)])
        c.copy("dve", v[:, 1:3, :], pC[:, 0:32].rearrange("p (a h) -> p a h", a=2), [BK(1)], [vk])
        for hf, eng in ((0, "dve"), (1, "pool")):
            c.tt(eng, Lh[:, hf * 8:(hf + 1) * 8, :], lstr[:, :].unsqueeze(1).to_broadcast([128, 8, 128]),
                 v[:, 0, hf * 8:(hf + 1) * 8].unsqueeze(2).to_broadcast([128, 8, 128]), ALU.mult,
                 ["lstr", vk], [("Lh", s, hf)])
        for rnd in range(2):
            for q in range(2):
                for hh in range(4):
                    h = rnd * 8 + q * 4 + hh
                    c.mm(banks[2 + q][:, hh * 128:(hh + 1) * 128], Lh[:, h, :], tri_b[:], True, True,
                         [("Lh", s, rnd), "tri_b"], [BK(2 + q)])
                c.act(decT[:, rnd * 8 + q * 4:rnd * 8 + q * 4 + 4, :],
                      banks[2 + q][:].rearrange("p (h t) -> p h t", h=4), AF.Exp, [BK(2 + q)], [("decT", s, rnd)])
        for g in range(2):
            c.mm(pC[:, 32 + g * 128:32 + (g + 1) * 128], xbcT[:, 8 + g, cs], xbcT[:, 10 + g, cs], True, True,
                 allx, [BK(1)])
        c.tt("dve", cb_sb[:], pC[:, 32:288].rearrange("p (g t) -> p g t", g=2),
             triT[:, :].unsqueeze(1).to_broadcast([128, 2, 128]), ALU.mult, [BK(1), "triT"], [("cb_sb", s)])
        for g in range(2):
            c.tt("dve", MT[s][:, g * 8:(g + 1) * 8, :], decT[:, g * 8:(g + 1) * 8, :],
                 cb_sb[:, g, :].unsqueeze(1).to_broadcast([128, 8, 128]), ALU.mult,
                 [("decT", s, g), ("cb_sb", s)], [("MT", s, g)])
        c.act(v[:, 3, :], v[:, 1, :], AF.Exp, [vk], [vk])
        c.tt("dve", v[:, 4, :], v[:, 2, :], v[:, 1, :], ALU.subtract, [vk], [vk])
        c.act(v[:, 5, :], v[:, 4, :], AF.Exp, [vk], [vk])
        c.tt("dve", v[:, 6, :], v[:, 5, :], dt, ALU.mult, [vk, ("dt_all", ch)], [vk])
        c.tt("dve", v[:, 7, :], v[:, 1, :], offs[:], ALU.add, [vk, "offs"], [vk])
        c.act(eacg_all[:, ch, :], v[:, 7, :], AF.Exp, [vk], [("eacg", ch)])
        c.tt("dve", offs[:], offs[:], v[:, 2, :], ALU.add, [vk, "offs"], ["offs"])
        c.act(v[:, 8, :], v[:, 2, :], AF.Exp, [vk], [vk])
        x3 = xs_tok[s][:].rearrange("p (h d) -> p h d", h=16)
        c.tt("dve", xdt[s][:].rearrange("p (h d) -> p h d", h=16), x3,
             dt.unsqueeze(2).to_broadcast([128, 16, 64]), ALU.mult, [("xs_tok", s), ("dt_all", ch)], [("xdt", s)])
        c.tt("pool", xdt2[s][:].rearrange("p (h d) -> p h d", h=16), x3,
             v[:, 6, :].unsqueeze(2).to_broadcast([128, 16, 64]), ALU.mult, [("xs_tok", s), vk], [("xdt2", s)])
        c.tt("pool", t2[:].rearrange("p (h d) -> p h d", h=16), x3,
             d_bc[:, :].unsqueeze(2).to_broadcast([128, 16, 64]), ALU.mult, [("xs_tok", s), "d_bc"], [("t2", s)])
        for h in range(16):
            pb = 4 + h // 8
            c.mm(banks[pb][:, (h % 8) * 64:(h % 8 + 1) * 64], MT[s][:, h, :], xdt[s][:, h * 64:(h + 1) * 64],
                 True, True, [("MT", s, h // 8), ("xdt", s)], [BK(pb)])
        for g in range(2):
            hs = slice(g * 512, (g + 1) * 512)
            c.tt("dve", ydt[:, hs], banks[4 + g][:], t2[:, hs], ALU.add, [BK(4 + g), ("t2", s)], [("ydt", s, g)])

    def stage_b(ch):
        s = ch % 2
        cs = slice(ch * 128, (ch + 1) * 128)
        v = sm[s]
        vk = ("sm", s)
        ydt = ydt_[s]
        for g in range(2):
            c.mm(banks[6 + g][:], xbcT[:, 10 + g, cs], prev_bf[:, g * 512:(g + 1) * 512], True, True,
                 allx + ["prev_bf"], [BK(6 + g)])
        for g in range(2):
            hs = slice(g * 512, (g + 1) * 512)
            c.tt("dve", ytmp[:, hs].rearrange("p (h d) -> p h d", h=8),
                 banks[6 + g][:].rearrange("p (h d) -> p h d", h=8),
                 v[:, 3, g * 8:(g + 1) * 8].unsqueeze(2).to_broadcast([128, 8, 64]), ALU.mult,
                 [BK(6 + g), vk], [("ytmp", g)])
        c.tt("dve", yl[s][:], ytmp[:], ydt[:], ALU.add, [("ytmp", 0), ("ytmp", 1), ("ydt", s, 0), ("ydt", s, 1)],
             [("yl", s)])
        c.dma("sp", yloc_o[ch * 128:(ch + 1) * 128, :], yl[s][:], reads=[("yl", s)], writes=["d_yloc"])
        for g in range(2):
            c.mm(banks[6 + g][:], B_tok[s][:, g * 128:(g + 1) * 128], xdt2[s][:, g * 512:(g + 1) * 512], True, True,
                 [("B_tok", s), ("xdt2", s)], [BK(6 + g)])
        c.tt("pool", state[:].rearrange("p (h d) -> p h d", h=16), state[:].rearrange("p (h d) -> p h d", h=16),
             v[:, 8, :].unsqueeze(2).to_broadcast([128, 16, 64]), ALU.mult, ["state", vk], ["state"])
        for g in range(2):
            hs = slice(g * 512, (g + 1) * 512)
            c.tt("dve", state[:, hs], state[:, hs], banks[6 + g][:], ALU.add, ["state", BK(6 + g)], ["state"])
        c.copy("act", prev_bf[:], state[:], ["state"], ["prev_bf"])

    stage_a(0)
    for ch in range(NCH):
        if ch + 1 < NCH:
            stage_a(ch + 1)
        stage_b(ch)
    c.dma("sp", sfin_o, state[:], reads=["state"], writes=["d_st"])
    c.dma("sp", logd_o, offs[:], reads=["offs"], writes=["d_st"])
    c.dma("sp", eacg_o.rearrange("(c p) h -> p c h", p=128), eacg_all[:], reads=[("eacg", ch) for ch in range(NCH)],
          writes=["d_eacg"])


IN_SIZES = (512, 512, 1024, 1536, 16, 512, 512, 512, 3072)
IN_OFFS = np.concatenate([[0], np.cumsum(IN_SIZES)]).astype(int)


def rope_tables_np():
    pos = np.arange(S, dtype=np.float32)
    inv_freq = (1.0 / (np.float32(500000.0) ** (np.arange(0, 16, 2, dtype=np.float32) / np.float32(16)))).astype(np.float32)
    ang = (pos[:, None] * inv_freq[None, :]).astype(np.float32)
    return np.cos(ang).astype(np.float32), np.sin(ang).astype(np.float32)


def ext_rows(full, b, j, halo=128):
    h = np.zeros((halo,) + full.shape[2:], full.dtype)
    if j > 0:
        h = full[b, j * T - halo:j * T]
    return np.ascontiguousarray(np.concatenate([h, full[b, j * T:(j + 1) * T]], 0))


VPIECE = [(0, b) for b in range(6)] + [(1, b) for b in range(6)] + [(2, b) for b in range(4)]
VPW = (6, 6, 4)


def emit_attn(c, nc, io, layer):
    lam_init = 0.8 - 0.6 * math.exp(-0.3 * layer)
    lam_in, idx_in = io["lam"], io["idx"]
    q_g, k_g, v_g, y_send = io["q_g"], io["k_g"], io["v_g"], io["y_send"]
    NKB = S // 128
    with Section(c):
        setup_consts(c)
        sc = [c.ps(f"sc{i}", [128, 1024], F32) for i in range(2)]
        accs = c.ps("accs", [128, 2048], F32)
        qT = c.sb("qT", [128, S], BF16)
        kT = c.sb("kT", [128, S], BF16)
        V = c.sb("V", [128, NKB, 130], BF16)
        idx = c.sb("idx", [128, 4], mybir.dt.int32)
        c.dma("sp", idx[:], idx_in, writes=["idx"])
        for i in range(4):
            for pc in range(2):
                sl = slice(i * 2048 + pc * 1024, i * 2048 + (pc + 1) * 1024)
                c.op16("pool", lambda E: E.indirect_dma_start(
                    out=qT[:, sl], out_offset=None, in_=q_g[pc],
                    in_offset=bass.IndirectOffsetOnAxis(ap=idx[:, i:i + 1], axis=0)),
                    ["idx", ("g_q", pc)], [("qT", i)])
                c.op16("pool", lambda E: E.indirect_dma_start(
                    out=kT[:, sl], out_offset=None, in_=k_g[pc],
                    in_offset=bass.IndirectOffsetOnAxis(ap=idx[:, i:i + 1], axis=0)),
                    ["idx", ("g_k", pc)], [("kT", i)])
            b0 = 0
            for vp in range(3):
                c.op16("pool", lambda E: E.indirect_dma_start(
                    out=V[:, i * 16 + b0:i * 16 + b0 + VPW[vp], :].rearrange("p b d -> p (b d)"), out_offset=None,
                    in_=v_g[vp], in_offset=bass.IndirectOffsetOnAxis(ap=idx[:, i:i + 1], axis=0)),
                    ["idx", ("g_v", vp)], [("V", i)])
                b0 += VPW[vp]
        lv = c.sb("lv", [128, 256], F32)
        c.dma("sp", lv[:], lam_in.partition_broadcast(128), writes=["lv"])
        lp = c.sb("lp", [128, 2, 64], F32)
        l2 = c.sb("l2", [128, 8], F32)
        lv4 = lv[:].rearrange("p (a d) -> p a d", a=4)
        c.tt("dve", lp[:, 0, :], lv4[:, 0, :], lv4[:, 1, :], ALU.mult, ["lv"], ["lp"])
        c.tt("dve", lp[:, 1, :], lv4[:, 2, :], lv4[:, 3, :], ALU.mult, ["lv", "lp"], ["lp"])
        c.op("dve", lambda E: E.tensor_reduce(out=l2[:, 0:2], in_=lp[:], axis=AX.X, op=ALU.add), ["lp"], ["l2"])
        c.act(l2[:, 2:4], l2[:, 0:2], AF.Exp, ["l2"], ["l2"])
        c.tt("dve", l2[:, 4:5], l2[:, 3:4], l2[:, 2:3], ALU.subtract, ["l2"], ["l2"])
        c.ts("dve", l2[:, 5:6], l2[:, 4:5], -lam_init, None, ALU.add, None, ["l2"], ["l2"])
        sgc = c.sb("sgc", [128, 1], F32)
        c.dma("sp", sgc[:], io["sgc"], writes=["sgc"])
        c.ts("dve", sgc[:], sgc[:], 1.0 - lam_init, None, ALU.mult, None, ["sgc"], ["sgc"])
        ones = c.sb("ones_f", [128, 128], F32)
        c.op("pool", lambda E: E.memset(ones[:], 1.0), writes=["ones_f"])
        ones_b = c.sb("ones_b", [128, 128], BF16)
        c.op("pool", lambda E: E.memset(ones_b[:], 1.0), writes=["ones_b"])
        tri = c.sb("tri", [128, 128], BF16)
        c.op("pool", lambda E: E.memset(tri[:], 1.0), writes=["tri"])
        c.op("pool", lambda E: E.affine_select(out=tri[:], in_=tri[:], pattern=[[1, 128]], compare_op=ALU.is_ge,
                                               fill=0.0, base=0, channel_multiplier=-1), ["tri"], ["tri"])
        PT = [c.sb(f"PT{i}", [128, 2, 512], BF16) for i in range(2)]
        accS = [c.sb(f"accS{i}", [128, 2, 512], F32) for i in range(2)]
        rL = c.sb("rL", [128, 2, 512], F32)
        o_t = c.sb("o_t", [128, 512], F32)
        o_u = c.sb("o_u", [128, 512], F32)
        yb = [c.sb(f"ybo{i}", [128, 512], BF16) for i in range(2)]
        units = [(qg, kb) for qg in range(S // 512) for kb in range(4 * qg + 4)]

        def emit_scores(u):
            qg, kb = units[u]
            r = kb - 4 * qg
            c0 = max(r, 0) * 128
            b2 = u % 2
            for cp in range(2):
                ps_ = slice(cp * 64, (cp + 1) * 64)
                c.mm(sc[b2][:, cp * 512 + c0:(cp + 1) * 512], kT[ps_, kb * 128:(kb + 1) * 128],
                     qT[ps_, qg * 512 + c0:(qg + 1) * 512], True, True,
                     [("qT", qg // 4), ("kT", kb // 16)], [("sc", b2)])
            c.act(PT[b2][:, :, c0:512], sc[b2][:].rearrange("p (a q) -> p a q", a=2)[:, :, c0:512], AF.Exp,
                  [("sc", b2)], [("PT", b2)])
            if r >= 0:
                c.tt("dve", PT[b2][:, :, c0:c0 + 128], PT[b2][:, :, c0:c0 + 128],
                     tri[:, :].unsqueeze(1).to_broadcast([128, 2, 128]), ALU.mult,
                     [("PT", b2), "tri"], [("PT", b2)])

        def emit_pv(u):
            qg, kb = units[u]
            r = kb - 4 * qg
            c0 = max(r, 0) * 128
            b2 = u % 2
            g2 = qg % 2
            for cp in range(2):
                c.mm(accs[:, cp * 512 + c0:(cp + 1) * 512], V[:, kb, 0:128], PT[b2][:, cp, c0:512],
                     kb == 0, False, [("PT", b2), ("V", kb // 16)], [("acc", cp)], skip_group_check=True)
            c.mm(accs[:, 1536 + c0:2048], ones_b[:], PT[b2][:, 1, c0:512], kb == 0, False,
                 [("PT", b2), "ones_b"], [("acc", 3)], skip_group_check=True)
            if kb == 0:
                c.copy("dve", accS[g2][:, 0, :], PT[b2][:, 0, :], [("PT", b2)], [("accS", g2)])
            else:
                c.tt("dve", accS[g2][:, 0, c0:512], accS[g2][:, 0, c0:512], PT[b2][:, 0, c0:512], ALU.add,
                     [("PT", b2), ("accS", g2)], [("accS", g2)])

        o0s = [c.sb(f"o0s{i}", [128, 512], F32) for i in range(2)]
        o1s = [c.sb(f"o1s{i}", [128, 512], F32) for i in range(2)]
        pending = []

        def finalize(qg, u_now, gap):
            g2 = qg % 2

            def f0():
                c.copy("act", o0s[g2][:], accs[:, 0:512], [("acc", 0)], [("o0s", g2)])
                c.copy("act", o1s[g2][:], accs[:, 512:1024], [("acc", 1)], [("o1s", g2)])
                c.mm(accs[:, 1024:1536], ones[:], accS[g2][:, 0, :], True, True, ["ones_f", ("accS", g2)], [("acc", 2)])
                c.act(rL[:, 1, :], accs[:, 1536:2048], AF.Ln, [("acc", 3)], [("rL", 1)])

            def f1():
                c.act(rL[:, 0, :], accs[:, 1024:1536], AF.Ln, [("acc", 2)], [("rL", 0)])
                c.act(rL[:, 0, :], rL[:, 0, :], AF.Exp, [("rL", 0)], [("rL", 0)], scale=-1.0)
                c.act(rL[:, 1, :], rL[:, 1, :], AF.Exp, [("rL", 1)], [("rL", 1)], scale=-1.0)

            def f2():
                c.tt("dve", o_t[:], o0s[g2][:], rL[:, 0, :], ALU.mult, [("o0s", g2), ("rL", 0)], ["o_t"])
                c.tt("dve", o_u[:], o1s[g2][:], rL[:, 1, :], ALU.mult, [("o1s", g2), ("rL", 1)], ["o_u"])

            def f3():
                c.stt(o_t[:], o_u[:], l2[:, 5:6], o_t[:], ALU.mult, ALU.add, ["o_u", "o_t", "l2"], ["o_t"])
                c.tt("dve", o_u[:], o_t[:], o_t[:], ALU.mult, ["o_t", "o_u"], ["o_u"])
                c.mm(accs[:, 1024:1536], ones[:], o_u[:], True, True, ["ones_f", "o_u"], [("acc", 2)])

            def f4():
                c.act(o_u[:], accs[:, 1024:1536], AF.Ln, [("acc", 2), "eps", "o_u"], ["o_u"], scale=1.0 / 128,
                      bias=EPS_AP[0])
                c.act(o_u[:], o_u[:], AF.Exp, ["o_u"], ["o_u"], scale=-0.5)
                c.stt(yb[g2][:], o_t[:], sgc[:, 0:1], o_u[:], ALU.mult, ALU.mult, ["o_t", "o_u", "sgc"], [("ybo", g2)])
                j_, col0 = qg // 4, (qg % 4) * 512
                c.dma("sp", y_send[col0 // 1024][j_ * 128:(j_ + 1) * 128, (col0 % 1024):(col0 % 1024) + 512],
                      yb[g2][:], reads=[("ybo", g2)], writes=[("d_ys", col0 // 1024)])
            for k, f in enumerate((f0, f1, f2, f3, f4)):
                pending.append((u_now + k * gap, f))

        emit_scores(0)
        nu = len(units)
        for u in range(nu):
            if u + 1 < nu:
                emit_scores(u + 1)
            emit_pv(u)
            qg, kb = units[u]
            if kb == 4 * qg + 3:
                finalize(qg, u, 2 if u + 1 < nu else 0)
            while pending and pending[0][0] <= u:
                pending.pop(0)[1]()
        while pending:
            pending.pop(0)[1]()


def emit_back(c, nc, io):
    (x_ext, yaT_in, zs_in, yloc_in, eacg_in, CT_in, st_g, mlt_in, mk_in, y_g, gT_in, ng_in, w_a, w_b, w_c, w_o,
     idx_in, xm_o, halo_out) = (io[k] for k in (
        "x_ext", "yaT", "zs", "yloc", "eacg", "CT", "st_g", "mlt", "mk", "y_g", "gT", "ng", "w_a", "w_b", "w_c",
        "w_o", "idx", "x_mid", "halo_out"))
    x_in = x_ext[128:, :]

    def wview(w):
        return w.rearrange("(k p) n -> p k n", p=128)
    with Section(c):
        ident = setup_consts(c)
        banks = [c.ps(f"bank{i}", [128, 512], F32) for i in range(8)]
        BK = lambda i: ("bank", i)
        H_bf = c.sb("H_bf", [128, 1024], BF16)
        idx = c.sb("idx", [128, 4], mybir.dt.int32)
        c.dma("sp", idx[:], idx_in, writes=["idx"])
        ybT = c.sb("ybT", [128, 8, T], BF16)
        ycT = c.sb("ycT", [128, 4, T], BF16)
        yaT = c.sb("yaT", [128, 4, T], BF16)
        c.dma("sp", yaT[:], yaT_in.rearrange("(g p) t -> p g t", p=128), writes=["yaT"])
        wa = c.sb("wa", [128, 4, D], BF16)
        wb = c.sb("wb", [128, 8, D], BF16)
        wc = c.sb("wc", [128, 4, D], BF16)
        wo = c.sb("wo", [128, 8, D], BF16)
        for t_, w_, k_ in ((wa, w_a, "wa"), (wb, w_b, "wb"), (wc, w_c, "wc"), (wo, w_o, "wo")):
            c.dma("pool", t_[:], wview(w_), writes=[k_])
        with Section(c):
            sf = c.sb("sf", [128, 4, 1024], F32)
            ld = c.sb("ld", [128, 4, 16], F32)
            mlt = c.sb("mlt", [128, 4], F32)
            mk = c.sb("mk", [128, 16], F32)
            stv = st_g.rearrange("(r p) n -> p r n", p=128)
            c.dma("sp", sf[:], stv[:, :, 0:1024], reads=["g_st"], writes=["sf"])
            c.dma("sp", ld[:], stv[:, :, 1024:1040], reads=["g_st"], writes=["ld"])
            c.dma("sp", mlt[:], mlt_in, writes=["mlt"])
            c.dma("sp", mk[:], mk_in, writes=["mk"])
            H = c.sb("H", [128, 1024], F32)
            Ht = c.sb("Ht", [128, 1024], F32)
            e = c.sb("e", [128, 16], F32)
            c.op("pool", lambda E: E.memset(H[:], 0.0), writes=["H"])
            for i in range(4):
                c.op("pool", lambda E: E.memset(e[:], 0.0), writes=["e"])
                for k in range(4):
                    c.stt(e[:], ld[:, k, :], mk[:, i * 4 + k:i * 4 + k + 1], e[:], ALU.mult, ALU.add,
                          ["ld", "mk", "e"], ["e"])
                c.act(e[:], e[:], AF.Exp, ["e"], ["e"])
                c.ts("dve", e[:], e[:], mlt[:, i:i + 1], None, ALU.mult, None, ["e", "mlt"], ["e"])
                c.tt("dve", Ht[:].rearrange("p (h d) -> p h d", h=16), sf[:, i, :].rearrange("p (h d) -> p h d", h=16),
                     e[:, :].unsqueeze(2).to_broadcast([128, 16, 64]), ALU.mult, ["sf", "e"], ["Ht"])
                c.tt("dve", H[:], H[:], Ht[:], ALU.add, ["H", "Ht"], ["H"])
            c.copy("act", H_bf[:], H[:], ["H"], ["H_bf"])
        for h in range(4):
            for pc in range(2):
                c.op16("pool", lambda E: E.indirect_dma_start(
                    out=ycT[:, h, pc * 1024:(pc + 1) * 1024], out_offset=None,
                    in_=y_g[pc], in_offset=bass.IndirectOffsetOnAxis(ap=idx[:, h:h + 1], axis=0)),
                    ["idx", ("g_y", pc)], [("ycT", h)])
        with Section(c):
            ng = c.sb("ng", [128, 8], F32)
            c.dma("sp", ng[:], ng_in, writes=["ng"])
            CT = c.sb("CT", [128, 2, T], BF16)
            c.dma("sp", CT[:], CT_in.rearrange("(g p) t -> p g t", p=128), writes=["CT"])
            eacg = c.sb("eacg", [128, NCH, 16], F32)
            c.dma("sp", eacg[:], eacg_in.rearrange("(c p) h -> p c h", p=128), writes=["eacg"])
            yl = [c.sb(f"yl{i}", [128, 1024], F32) for i in range(2)]
            zt = [c.sb(f"zt{i}", [128, 1024], BF16) for i in range(2)]
            yt = [c.sb(f"yt{i}", [128, 1024], F32) for i in range(2)]
            yb = [c.sb(f"yb{i}", [128, 1024], BF16) for i in range(2)]
            yj = c.sb("yj", [128, 1024], BF16)
            yss = [c.sb(f"yss{i}", [128, 4], F32) for i in range(2)]
            pT = banks[7][:].bitcast(BF16)
            pT2 = banks[6][:].bitcast(BF16)
            def b_stage1(ch):
                s = ch % 2
                rs = slice(ch * 128, (ch + 1) * 128)
                c.dma("sp", yl[s][:], yloc_in[rs, :], writes=[("yl", s)])
                c.dma("sp", zt[s][:], zs_in[rs, :], writes=[("zt", s)])
                for g in range(2):
                    c.mm(banks[g][:], CT[:, g, rs], H_bf[:, g * 512:(g + 1) * 512], True, True, ["CT", "H_bf"], [BK(g)])
                    hs = slice(g * 512, (g + 1) * 512)
                    c.tt("dve", yt[s][:, hs].rearrange("p (h d) -> p h d", h=8),
                         banks[g][:].rearrange("p (h d) -> p h d", h=8),
                         eacg[:, ch, g * 8:(g + 1) * 8].unsqueeze(2).to_broadcast([128, 8, 64]), ALU.mult,
                         [BK(g), "eacg"], [("yt", s, g)])
                ytk = [("yt", s, 0), ("yt", s, 1)]
                c.tt("dve", yt[s][:], yt[s][:], yl[s][:], ALU.add, ytk + [("yl", s)], ytk)
                c.tt("dve", yt[s][:], yt[s][:], zt[s][:], ALU.mult, ytk + [("zt", s)], ytk)
                c.act(yj[:], yt[s][:], AF.Square, ytk, ["yj", ("yss", s)], accum_out=yss[s][:, 0:1])
                rstd_from_ss(c, yss[s], ("yss", s), 1.0 / 1024)

            def b_stage2(ch):
                s = ch % 2
                rs = slice(ch * 128, (ch + 1) * 128)
                ytk = [("yt", s, 0), ("yt", s, 1)]
                c.act(yb[s][:], yt[s][:], AF.Copy, ytk + [("yss", s)], [("yb", s)], scale=yss[s][:, 2:3])
                for k in range(8):
                    c.tr(pT[:, k * 128:(k + 1) * 128], yb[s][:, k * 128:(k + 1) * 128], ident[:],
                         [("yb", s), "ident"], [BK(7)])
                c.tt("dve", ybT[:, :, rs], pT.rearrange("p (k t) -> p k t", k=8),
                     ng[:, :].unsqueeze(2).to_broadcast([128, 8, 128]), ALU.mult, [BK(7), "ng"], [("ybT", ch)])

            b_stage1(0)
            for ch in range(NCH):
                if ch + 1 < NCH:
                    b_stage1(ch + 1)
                b_stage2(ch)
        with Section(c):
            ally = [("ybT", ch) for ch in range(NCH)] + [("ycT", h) for h in range(4)] + ["yaT"]
            gts = [c.sb(f"gts{i}", [128, 3, 512], BF16) for i in range(2)]
            m1 = [c.sb(f"m1{i}", [128, 512], F32) for i in range(2)]
            m2 = [c.sb(f"m2{i}", [128, 512], F32) for i in range(2)]
            m3 = [c.sb(f"m3{i}", [128, 512], F32) for i in range(2)]
            mT = [c.sb(f"mT{i}", [128, 8, 512], BF16) for i in range(2)]
            xr = [c.sb(f"xr{i}", [128, D], F32) for i in range(2)]
            xo = [c.sb(f"xo{i}", [128, D], F32) for i in range(2)]
            gview = gT_in.rearrange("(b o p) t -> p b o t", b=3, p=128)
            un = 0
            for tl in range(4):
                ts_ = slice(tl * 512, (tl + 1) * 512)
                mt = mT[tl % 2]
                for ob in range(8):
                    s = un % 2
                    un += 1
                    os_ = slice(ob * 128, (ob + 1) * 128)
                    c.dma("sp", gts[s][:], gview[:, :, ob, ts_], writes=[("gts", s)])
                    for k in range(4):
                        c.mm(banks[0][:], wa[:, k, os_], yaT[:, k, ts_], k == 0, k == 3, ["wa"] + ally, [BK(0)])
                    for k in range(8):
                        c.mm(banks[1][:], wb[:, k, os_], ybT[:, k, ts_], k == 0, k == 7, ["wb"] + ally, [BK(1)])
                    for k in range(4):
                        c.mm(banks[2][:], wc[:, k, os_], ycT[:, k, ts_], k == 0, k == 3, ["wc"] + ally, [BK(2)])
                    c.tt("dve", m1[s][:], banks[0][:], gts[s][:, 0, :], ALU.mult, [BK(0), ("gts", s)], [("m1", s)])
                    c.tt("dve", m2[s][:], banks[1][:], gts[s][:, 1, :], ALU.mult, [BK(1), ("gts", s)], [("m2", s)])
                    c.tt("dve", m3[s][:], banks[2][:], gts[s][:, 2, :], ALU.mult, [BK(2), ("gts", s)], [("m3", s)])
                    c.tt("dve", m1[s][:], m1[s][:], m2[s][:], ALU.add, [("m1", s), ("m2", s)], [("m1", s)])
                    c.tt("dve", mt[:, ob, :], m1[s][:], m3[s][:], ALU.add, [("m1", s), ("m3", s)], [("mT", tl % 2, ob)])
                allm = [("mT", tl % 2, ob) for ob in range(8)]
                for cc in range(4):
                    gch = tl * 4 + cc
                    s2 = gch % 2
                    c.dma("sp", xr[s2][:], x_in[gch * 128:(gch + 1) * 128, :], writes=[("xr", s2)])
                    for hf in range(2):
                        pd = banks[4 + hf]
                        for k in range(8):
                            c.mm(pd[:], mt[:, k, cc * 128:(cc + 1) * 128], wo[:, k, hf * 512:(hf + 1) * 512],
                                 k == 0, k == 7, allm + ["wo"], [BK(4 + hf)])
                        c.tt("dve", xo[s2][:, hf * 512:(hf + 1) * 512], pd[:], xr[s2][:, hf * 512:(hf + 1) * 512],
                             ALU.add, [BK(4 + hf), ("xr", s2)], [("xo", s2)])
                    c.dma("sp", xm_o[gch * 128:(gch + 1) * 128, :], xo[s2][:], reads=[("xo", s2)], writes=["d_xm"])
                    if gch == NCH - 1:
                        c.dma("sp", halo_out, xo[s2][120:128, :], reads=[("xo", s2)], writes=["d_halo"])


def halo_select(c, nc, halo_g, selp_in, dst_rows, dkey):
    with Section(c):
        hg = c.sb("hg", [8, 4, D], F32)
        sel = c.sb("sel", [8, 4], F32)
        acc = c.sb("hacc", [8, D], F32)
        c.dma("sp", hg[:], halo_g.rearrange("(r p) n -> p r n", p=8), reads=["g_halo"], writes=["hg"])
        c.dma("sp", sel[:], selp_in[0:8, :], writes=["sel"])
        c.ts("dve", acc[:], hg[:, 0, :], sel[:, 0:1], None, ALU.mult, None, ["hg", "sel"], ["hacc"])
        for r in range(1, 4):
            c.stt(acc[:], hg[:, r, :], sel[:, r:r + 1], acc[:], ALU.mult, ALU.add, ["hg", "sel", "hacc"], ["hacc"])
        c.dma("sp", dst_rows, acc[:], reads=["hacc"], writes=[dkey])


LAYER_IN = [
    ("g_a", [128, 8]), ("w_u", [D, 512]), ("w_v", [D, 512]), ("w_z", [D, 1024]), ("w_xbc", [D, 1536]),
    ("w_dt", [D, 16]), ("w_q", [D, 512]), ("w_k", [D, 512]), ("w_v2", [D, 512]), ("w_g", [D, 3072]),
    ("vgain", [1, 512]), ("wsT", [128, 4, 128]), ("bs", [1, 512]), ("scw", [128, 12, 4]), ("scb", [128, 12]),
    ("dtb", [1, 16]), ("alog", [1, 16]), ("dsk", [1, 16]), ("qg", [1, 64]), ("kg", [1, 64]),
    ("lam", [1, 256]), ("sgc", [128, 1]), ("ng", [128, 8]),
    ("w_a", [512, D]), ("w_b", [D, D]), ("w_c", [512, D]), ("w_o", [D, D]),
    ("g_f", [128, 8]), ("w_up", [D, 2 * FFN]), ("fcw", [128, 44, 3]), ("fcb", [128, 44]), ("w_dn", [FFN, D]),
]


def build_fused(nl=2):
    nc = bass.Bass("TRN2", target_bir_lowering=False)
    I32 = mybir.dt.int32

    def din(name, shape, dt=F32):
        return nc.dram_tensor(name, shape, dt, kind="ExternalInput").ap()

    def scr(name, shape, dt):
        return nc.dram_tensor(name, shape, dt, kind="Internal").ap()
    x_ext0 = din("x_ext0", [(NCH + 1) * 128, D])
    L = [{k: din(f"{k}_{l}", sh) for k, sh in LAYER_IN} for l in range(nl)]
    cos_in, sin_in = din("cos", [T, 8]), din("sin", [T, 8])
    idx_in = din("idx", [128, 4], I32)
    mlt_in, mk_in, selp_in = din("mlt", [128, 4]), din("mk", [128, 16]), din("selp", [128, 4])
    x_out = nc.dram_tensor("x_out", [T, D], F32, kind="ExternalOutput").ap()
    yaT, zs, yloc = scr("s_yaT", [512, T], BF16), scr("s_zs", [T, 1024], BF16), scr("s_yloc", [T, 1024], F32)
    eacg, CT, gT = scr("s_eacg", [T, 16], F32), scr("s_CT", [256, T], BF16), scr("s_gT", [3072, T], BF16)
    st_send, st_g = scr("s_st", [128, 1040], F32), scr("g_st", [512, 1040], F32)
    q_send = [scr(f"s_q{i}", [512, 1024], BF16) for i in range(2)]
    k_send = [scr(f"s_k{i}", [512, 1024], BF16) for i in range(2)]
    v_send = [scr(f"s_v{i}", [512, VPW[i] * 130], BF16) for i in range(3)]
    q_g = [scr(f"g_q{i}", [2048, 1024], BF16) for i in range(2)]
    k_g = [scr(f"g_k{i}", [2048, 1024], BF16) for i in range(2)]
    v_g = [scr(f"g_v{i}", [2048, VPW[i] * 130], BF16) for i in range(3)]
    y_send = [scr(f"s_y{i}", [512, 1024], BF16) for i in range(2)]
    y_g = [scr(f"g_y{i}", [2048, 1024], BF16) for i in range(2)]
    halo_send, halo_g = scr("s_halo", [8, D], F32), scr("g_halo", [32, D], F32)
    xm_ext = scr("xm_ext", [(NCH + 1) * 128, D], F32)
    x1_ext = scr("x1_ext", [(NCH + 1) * 128, D], F32)
    with ExitStack() as es:
        c = Ctx(nc, es)
        with Section(c):
            zt = c.sb("zt", [128, D], F32)
            c.op("pool", lambda E: E.memset(zt[:], 0.0), writes=["zt"])
            c.dma("sp", xm_ext[0:128, :], zt[:], reads=["zt"], writes=["d_xm"])
            c.dma("sp", x1_ext[0:128, :], zt[:], reads=["zt"], writes=["d_xout"])
        for l in range(nl):
            W = L[l]
            x_ext = x_ext0 if l == 0 else x1_ext
            last = l == nl - 1
            def ag_qkv():
                for pc in range(2):
                    c.allgather(q_send[pc], q_g[pc], [("d_qs", pc)], [("g_q", pc)])
                    c.allgather(k_send[pc], k_g[pc], [("d_ks", pc)], [("g_k", pc)])
                for vp in range(3):
                    c.allgather(v_send[vp], v_g[vp], [("d_vs", vp)], [("g_v", vp)])
            emit_front(c, nc, {
                "x_ext": x_ext, "g": W["g_a"], "w_u": W["w_u"], "w_v": W["w_v"], "w_z": W["w_z"],
                "w_xbc": W["w_xbc"], "w_dt": W["w_dt"], "w_q": W["w_q"], "w_k": W["w_k"], "w_v2": W["w_v2"],
                "w_g": W["w_g"], "vgain": W["vgain"], "wsT": W["wsT"], "bs": W["bs"], "cw": W["scw"],
                "cb": W["scb"], "dtb": W["dtb"], "alog": W["alog"], "dsk": W["dsk"], "qg": W["qg"], "kg": W["kg"],
                "cos": cos_in, "sin": sin_in, "yaT": yaT, "zs": zs, "yloc": yloc, "eacg": eacg, "CT": CT, "gT": gT,
                "st_send": st_send, "q_send": q_send, "k_send": k_send, "v_send": v_send,
                "after_qkv": ag_qkv})
            c.new_epoch()
            c.allgather(st_send, st_g, ["d_st"], ["g_st"])
            emit_attn(c, nc, {"lam": W["lam"], "sgc": W["sgc"], "idx": idx_in, "q_g": q_g, "k_g": k_g, "v_g": v_g,
                              "y_send": y_send}, l)
            c.new_epoch()
            for pc in range(2):
                c.allgather(y_send[pc], y_g[pc], [("d_ys", pc)], [("g_y", pc)])
            emit_back(c, nc, {"x_ext": x_ext, "yaT": yaT, "zs": zs, "yloc": yloc, "eacg": eacg, "CT": CT,
                              "st_g": st_g, "mlt": mlt_in, "mk": mk_in, "y_g": y_g, "gT": gT, "ng": W["ng"],
                              "w_a": W["w_a"], "w_b": W["w_b"], "w_c": W["w_c"], "w_o": W["w_o"], "idx": idx_in,
                              "x_mid": xm_ext[128:, :], "halo_out": halo_send})
            c.new_epoch()
            c.allgather(halo_send, halo_g, ["d_halo"], ["g_halo"])
            halo_select(c, nc, halo_g, selp_in, xm_ext[120:128, :], "d_xm")
            emit_ffn(c, nc, {"x_ext": xm_ext, "g": W["g_f"], "w_up": W["w_up"], "cw": W["fcw"], "cb": W["fcb"],
                             "w_dn": W["w_dn"], "x_out": x_out if last else x1_ext[128:, :],
                             "halo_out": None if last else halo_send})
            c.new_epoch()
            if not last:
                c.allgather(halo_send, halo_g, ["d_halo"], ["g_halo"])
                halo_select(c, nc, halo_g, selp_in, x1_ext[120:128, :], "d_xout")
        c.barrier()
        c.finish()
    return nc


def fused_inputs(inp, core, cos, sin, nl=2):
    b, j = core // 4, core % 4
    ca = np.ascontiguousarray
    m = {"x_ext0": ext_rows(inp["x"], b, j),
         "cos": ca(cos[j * T:(j + 1) * T]), "sin": ca(sin[j * T:(j + 1) * T]),
         "idx": ca(np.array([[r * 512 + j * 128 + p for r in range(4)] for p in range(128)], np.int32))}
    mlt = np.zeros((128, 4), np.float32)
    mk = np.zeros((128, 4, 4), np.float32)
    selp = np.zeros((128, 4), np.float32)
    for i in range(4):
        if i < j:
            mlt[:, i] = 1.0
        if i == j - 1:
            selp[:, i] = 1.0
        for k in range(4):
            if i < k < j:
                mk[:, i, k] = 1.0
    m["mlt"], m["mk"], m["selp"] = mlt, ca(mk.reshape(128, 16)), selp
    for l in range(nl):
        w_in = inp["w_in"][l]
        sec = [ca(w_in[:, IN_OFFS[i]:IN_OFFS[i + 1]]) for i in range(9)]
        d = {
            "g_a": ca(inp["attn_norm_g"][l].reshape(8, 128).T),
            "w_u": sec[0], "w_v": sec[1], "w_z": sec[2], "w_xbc": sec[3], "w_dt": sec[4],
            "w_q": sec[5], "w_k": sec[6], "w_v2": sec[7], "w_g": sec[8],
            "vgain": ca(inp["gm_v_norm_g"][l].reshape(1, 512)),
            "wsT": ca(inp["gm_w_s"][l].transpose(2, 0, 1)),
            "bs": ca(inp["gm_b_s"][l].reshape(1, 512)),
            "scw": ca(inp["ssm_conv_w"][l].reshape(12, 128, 4).transpose(1, 0, 2)),
            "scb": ca(inp["ssm_conv_b"][l].reshape(12, 128).T),
            "dtb": ca(inp["ssm_dt_bias"][l].reshape(1, 16)),
            "alog": ca(inp["ssm_a_log"][l].reshape(1, 16)),
            "dsk": ca(inp["ssm_d"][l].reshape(1, 16)),
            "qg": ca(inp["da_q_norm_g"][l].reshape(1, 64)),
            "kg": ca(inp["da_k_norm_g"][l].reshape(1, 64)),
            "lam": ca(inp["da_lambda"][l].reshape(1, 256)),
            "sgc": ca(inp["da_subln_g"][l].reshape(128, 1)),
            "ng": ca(inp["ssm_norm_g"][l].reshape(8, 128).T),
            "w_a": ca(inp["w_branch_a"][l]), "w_b": ca(inp["w_branch_b"][l]),
            "w_c": ca(inp["w_branch_c"][l]), "w_o": ca(inp["w_out"][l]),
            "g_f": ca(inp["ffn_norm_g"][l].reshape(8, 128).T),
            "w_up": ca(inp["ffn_w_up"][l]),
            "fcw": ca(inp["ffn_conv_w"][l].reshape(44, 128, 3).transpose(1, 0, 2)),
            "fcb": ca(inp["ffn_conv_b"][l].reshape(44, 128).T),
            "w_dn": ca(inp["ffn_w_down"][l]),
        }
        for k, v in d.items():
            m[f"{k}_{l}"] = v
    return m


_NC_CACHE = {}


def kernel(**inp):
    inp = {k: np.asarray(v, dtype=np.float32) for k, v in inp.items()}
    cores = list(range(NCORE))
    cos, sin = rope_tables_np()
    if "nc" not in _NC_CACHE:
        _NC_CACHE["nc"] = build_fused(2)
    maps = [fused_inputs(inp, cc, cos, sin) for cc in cores]
    res = run_bass_kernel_spmd(_NC_CACHE["nc"], maps, core_ids=cores).results
    x = np.stack([np.concatenate([res[b * 4 + j]["x_out"] for j in range(4)], 0) for b in range(2)])
    return x.astype(np.float32)
```

```python
import math
import numpy as np
from contextlib import ExitStack
import ml_dtypes
import concourse.bass as bass
import concourse.mybir as mybir
from concourse.bass_utils import run_bass_kernel_spmd

F32 = mybir.dt.float32
BF16 = mybir.dt.bfloat16
AF = mybir.ActivationFunctionType
ALU = mybir.AluOpType
AX = mybir.AxisListType
NPBF = ml_dtypes.bfloat16

D = 1024
S = 8192
NCORE = 8
T = 2048
NCH = 16
EPS = 1e-6
FFN = 2816


class Ctx:
    NDMA = 8

    def __init__(self, nc, es):
        self.nc = nc
        self.es = es
        self.top = es
        self.eng = {"pe": nc.tensor, "act": nc.scalar, "dve": nc.vector,
                    "pool": nc.gpsimd, "sp": nc.sync}
        self.sems = []
        self.esem = {}
        self.cnt = {}
        self.known = {e: {} for e in self.eng}
        for e in self.eng:
            self.esem[e] = self._newsem("s_" + e)
            self.cnt[e] = 0
        self.dsem, self.dval, self.drr = {}, {}, {}
        for q in ("sp", "pool", "act"):
            self.dsem[q] = [self._newsem(f"d_{q}{i}") for i in range(self.NDMA)]
            self.dval[q] = [0] * self.NDMA
            self.drr[q] = 0
        self.lastw = {}
        self.readers = {}
        self.n_inst = 0

    def _newsem(self, name):
        s = self.top.enter_context(self.nc.semaphore(name))
        self.sems.append(s)
        return len(self.sems) - 1

    def sb(self, name, shape, dt):
        self.n_alloc = getattr(self, "n_alloc", 0) + 1
        return self.es.enter_context(self.nc.sbuf_tensor(f"{name}_{self.n_alloc}", shape, dt))

    def ps(self, name, shape, dt):
        self.n_alloc = getattr(self, "n_alloc", 0) + 1
        return self.es.enter_context(self.nc.psum_tensor(f"{name}_{self.n_alloc}", shape, dt))

    def op16(self, e, fn, reads=(), writes=()):
        self._deps(e, reads, writes)
        i = self.drr[e]
        self.drr[e] = (i + 1) % self.NDMA
        s = self.dsem[e][i]
        v = self.dval[e][i]
        if v > 0 and self.known[e].get(s, 0) < v:
            self.eng[e].wait_ge(self.sems[s], v)
            self.known[e][s] = v
        inst = fn(self.eng[e])
        inst.then_inc(self.sems[s], 16)
        self.dval[e][i] = v + 16
        self._record((s, v + 16), reads, writes)
        self.n_inst += 1
        return inst

    def allgather(self, src, dst, skeys, dkeys):
        self._deps("pool", skeys, dkeys)
        inst = self.nc.gpsimd.collective_compute("AllGather", ALU.bypass,
                                                 replica_groups=[[0, 1, 2, 3], [4, 5, 6, 7]],
                                                 ins=[src], outs=[dst])
        s = self._newsem(f"cc{len(self.sems)}")
        inst.then_inc(self.sems[s], 1)
        self._record((s, 1), skeys, dkeys)
        self.n_inst += 1
        return inst

    def _deps(self, e, reads, writes):
        deps = {}
        own = self.esem.get(e)

        def add(tok, raw):
            if tok is None:
                return
            s, v = tok
            if deps.get(s, 0) < v:
                deps[s] = v
        for k in reads:
            add(self.lastw.get(k), True)
        for k in writes:
            add(self.lastw.get(k), False)
            for s, v in self.readers.get(k, {}).items():
                add((s, v), False)
        eng = self.eng[e]
        for s, v in deps.items():
            if e == "pe" and s == self.esem["pe"]:
                continue
            if self.known[e].get(s, 0) < v:
                eng.wait_ge(self.sems[s], v)
                self.known[e][s] = v

    def _record(self, tok, reads, writes):
        s, v = tok
        for k in reads:
            r = self.readers.setdefault(k, {})
            if r.get(s, 0) < v:
                r[s] = v
        for k in writes:
            self.lastw[k] = tok
            self.readers[k] = {}

    def op(self, e, fn, reads=(), writes=()):
        self._deps(e, reads, writes)
        inst = fn(self.eng[e])
        self.cnt[e] += 1
        inst.then_inc(self.sems[self.esem[e]], 1)
        self._record((self.esem[e], self.cnt[e]), reads, writes)
        self.n_inst += 1
        return inst

    def dma(self, q, out, in_, reads=(), writes=(), **kw):
        self._deps(q, reads, writes)
        eng = self.eng[q]
        i = self.drr[q]
        self.drr[q] = (i + 1) % self.NDMA
        s = self.dsem[q][i]
        v = self.dval[q][i]
        if v > 0 and self.known[q].get(s, 0) < v:
            eng.wait_ge(self.sems[s], v)
            self.known[q][s] = v
        inst = eng.dma_start(out=out, in_=in_, **kw)
        inst.then_inc(self.sems[s], 16)
        self.dval[q][i] = v + 16
        self._record((s, v + 16), reads, writes)
        self.n_inst += 1
        return inst

    def barrier(self):
        toks = [(self.esem[e], self.cnt[e]) for e in self.eng if self.cnt[e] > 0]
        for q in self.dsem:
            for i in range(self.NDMA):
                if self.dval[q][i] > 0:
                    toks.append((self.dsem[q][i], self.dval[q][i]))
        for e in self.eng:
            for s_, v in toks:
                if e == "pe" and s_ == self.esem["pe"]:
                    continue
                if self.known[e].get(s_, 0) < v:
                    self.eng[e].wait_ge(self.sems[s_], v)
                    self.known[e][s_] = v

    def new_epoch(self):
        self.barrier()
        for e in self.eng:
            self.esem[e] = self._newsem(f"s_{e}_{len(self.sems)}")
            self.cnt[e] = 0

    def finish(self):
        for q in self.dsem:
            for i in range(self.NDMA):
                if self.dval[q][i] > 0:
                    self.nc.sync.wait_ge(self.sems[self.dsem[q][i]], self.dval[q][i])

    def mm(self, out, lhsT, rhs, start, stop, reads, writes, **kw):
        return self.op("pe", lambda E: E.matmul(out, lhsT, rhs, start=start, stop=stop, **kw),
                       reads, writes)

    def tr(self, out, in_, ident, reads, writes):
        return self.op("pe", lambda E: E.transpose(out, in_, ident), reads, writes)

    def act(self, out, in_, func, reads, writes, **kw):
        return self.op("act", lambda E: E.activation(out=out, in_=in_, func=func, **kw), reads, writes)

    def tt(self, e, out, in0, in1, op, reads, writes):
        return self.op(e, lambda E: E.tensor_tensor(out=out, in0=in0, in1=in1, op=op), reads, writes)

    def ts(self, e, out, in0, s1, s2, op0, op1, reads, writes):
        if op1 is None:
            return self.op(e, lambda E: E.tensor_scalar(out=out, in0=in0, scalar1=s1, scalar2=None, op0=op0),
                           reads, writes)
        return self.op(e, lambda E: E.tensor_scalar(out=out, in0=in0, scalar1=s1, scalar2=s2, op0=op0, op1=op1),
                       reads, writes)

    def stt(self, out, in0, scalar, in1, op0, op1, reads, writes):
        return self.op("dve", lambda E: E.scalar_tensor_tensor(out=out, in0=in0, scalar=scalar, in1=in1,
                                                              op0=op0, op1=op1), reads, writes)

    def copy(self, e, out, in_, reads, writes):
        if e == "act":
            return self.act(out, in_, AF.Copy, reads, writes)
        return self.op(e, lambda E: E.tensor_copy(out=out, in_=in_), reads, writes)


def make_ident(c, ident):
    c.op("pool", lambda E: E.memset(ident[:], 1.0), writes=["ident"])
    c.op("pool", lambda E: E.affine_select(out=ident[:], in_=ident[:], pattern=[[-1, 128]],
                                           compare_op=ALU.is_equal, fill=0.0, base=0,
                                           channel_multiplier=1), reads=["ident"], writes=["ident"])


def build_hT(c, x_ext, g_sb, hT, nch, ident, pT_bank, pfx="h"):
    pT, pkey = pT_bank
    xt = [c.sb(f"{pfx}_xt{i}", [128, D], F32) for i in range(2)]
    xs = [c.sb(f"{pfx}_xs{i}", [128, D], BF16) for i in range(2)]
    junk = c.sb(f"{pfx}_junk", [128, D], BF16)
    ss = [c.sb(f"{pfx}_ss{i}", [128, 4], F32) for i in range(2)]
    for ch in range(nch):
        s = ch % 2
        c.dma("sp", xt[s][:], x_ext[ch * 128:(ch + 1) * 128, :], writes=[(pfx + "xt", s)])
        c.act(junk[:], xt[s][:], AF.Square, [(pfx + "xt", s)], [pfx + "junk", (pfx + "ss", s)],
              accum_out=ss[s][:, 0:1])
        c.act(ss[s][:, 1:2], ss[s][:, 0:1], AF.Ln, [(pfx + "ss", s), "eps"], [(pfx + "ss", s)],
              scale=1.0 / D, bias=EPS_AP[0])
        c.act(ss[s][:, 2:3], ss[s][:, 1:2], AF.Exp, [(pfx + "ss", s)], [(pfx + "ss", s)], scale=-0.5)
        c.act(xs[s][:], xt[s][:], AF.Copy, [(pfx + "xt", s), (pfx + "ss", s)], [(pfx + "xs", s)],
              scale=ss[s][:, 2:3])
        for k in range(8):
            c.tr(pT[:, k * 128:(k + 1) * 128], xs[s][:, k * 128:(k + 1) * 128], ident[:],
                 [(pfx + "xs", s), "ident"], [pkey])
        c.tt("dve", hT[:, :, ch * 128:(ch + 1) * 128],
             pT.rearrange("p (k t) -> p k t", k=8),
             g_sb[:, :].unsqueeze(2).to_broadcast([128, 8, 128]), ALU.mult,
             [pkey, "g_sb"], [("hT", ch)])


EPS_AP = [None]


def setup_consts(c):
    eps = c.sb("eps_t", [128, 1], F32)
    c.op("pool", lambda E: E.memset(eps[:], EPS), writes=["eps"])
    EPS_AP[0] = eps[:, 0:1]
    one = c.sb("one_t", [128, 1], F32)
    c.op("pool", lambda E: E.memset(one[:], 1.0), writes=["eps"])
    ONE_AP[0] = one[:, 0:1]
    ident = c.sb("ident", [128, 128], BF16)
    make_ident(c, ident)
    return ident


def emit_ffn(c, nc, io):
    x_ext, g_in, w_up, cw_in, cb_in, w_dn, x_out = (io[k] for k in ("x_ext", "g", "w_up", "cw", "cb", "w_dn", "x_out"))
    halo_out = io.get("halo_out")
    with Section(c):
        ident = setup_consts(c)
        banks = [c.ps(f"bank{i}", [128, 512], F32) for i in range(8)]
        g_sb = c.sb("g_sb", [128, 8], F32)
        cw = c.sb("cw_sb", [128, 44, 3], F32)
        cb = c.sb("cb_sb", [128, 44], F32)
        c.dma("sp", g_sb[:], g_in, writes=["g_sb"])
        c.dma("sp", cw[:], cw_in, writes=["cw"])
        c.dma("sp", cb[:], cb_in, writes=["cb"])
        hT = c.sb("hT", [128, 8, (NCH + 1) * 128], BF16)
        wd = c.sb("wd", [128, 22, D], BF16)
        c.dma("pool", wd[:], w_dn.rearrange("(i p) n -> p i n", p=128), writes=["wd"])
        build_hT(c, x_ext, g_sb, hT, NCH + 1, ident, (banks[7][:].bitcast(BF16), "bank7"))
        allhT = [("hT", ch) for ch in range(NCH + 1)]
        actT = c.sb("actT", [128, 22, 1024], BF16)
        wu = [c.sb(f"wu{i}", [128, 2, 8, 128], BF16) for i in range(2)]
        acc = [[c.sb(f"acc{gv}{i}", [128, 512], F32) for i in range(2)] for gv in range(2)]
        sg = [c.sb(f"sg{i}", [128, 512], F32) for i in range(2)]
        xbuf = [[c.sb(f"xbuf{gv}{i}", [128, 514], BF16) for i in range(2)] for gv in range(2)]
        pend = None
        xr = [c.sb(f"xr{i}", [128, D], F32) for i in range(2)]
        xo = [c.sb(f"xo{i}", [128, D], F32) for i in range(2)]
        w_up_v = w_up.rearrange("(k p) n -> p k n", p=128)
        un = 0
        for ps_ in range(2):
            for i in range(22):
                s = (ps_ * 22 + i) % 2
                for gv in range(2):
                    col0 = gv * FFN + i * 128
                    c.dma("pool", wu[s][:, gv], w_up_v[:, :, col0:col0 + 128], writes=[("wu", s)])
                for tl in range(2):
                    t0 = ps_ * 1024 + tl * 512
                    c0 = 128 + t0
                    b2 = un % 2
                    un += 1
                    for gv in range(2):
                        fb = gv * 22 + i
                        pk = ("pg", gv, b2)
                        pt = banks[gv * 2 + b2]
                        xb = xbuf[gv][b2]
                        xk = ("xb", gv, b2)
                        if tl == 0:
                            hk = "bank6"
                            ph = banks[6][:, gv * 2:gv * 2 + 2]
                            for k in range(8):
                                c.mm(ph, wu[s][:, gv, k, :], hT[:, k, c0 - 2:c0], k == 0, k == 7,
                                     [("wu", s)] + allhT, [hk])
                            c.copy("act", xb[:, 0:2], ph, [hk], [xk])
                        else:
                            c.copy("act", xb[:, 0:2], xbuf[gv][1 - b2][:, 512:514], [("xb", gv, 1 - b2)], [xk])
                        for k in range(8):
                            c.mm(pt[:], wu[s][:, gv, k, :], hT[:, k, c0:c0 + 512], k == 0, k == 7,
                                 [("wu", s)] + allhT, [pk])
                        a = acc[gv][b2]
                        ak = ("acc", gv, b2)
                        c.act(xb[:, 2:514], pt[:], AF.Copy, [pk], [xk])
                        c.act(a[:], pt[:], AF.Identity, [pk, "cw", "cb"], [ak],
                              scale=cw[:, fb, 2:3], bias=cb[:, fb:fb + 1])
                        c.stt(a[:], xb[:, 1:513], cw[:, fb, 1:2], a[:], ALU.mult, ALU.add, [xk, ak, "cw"], [ak])
                        c.stt(a[:], xb[:, 0:512], cw[:, fb, 0:1], a[:], ALU.mult, ALU.add, [xk, ak, "cw"], [ak])
                    if pend is not None:
                        pend()
                    def tail(b2=b2, i=i, tl=tl):
                        c.act(sg[b2][:], acc[0][b2][:], AF.Silu, [("acc", 0, b2)], [("sg", b2)])
                        c.tt("dve", actT[:, i, tl * 512:(tl + 1) * 512], sg[b2][:], acc[1][b2][:], ALU.mult,
                             [("sg", b2), ("acc", 1, b2)], [("actT", i)])
                    pend = tail
            pend()
            pend = None
            allact = [("actT", i) for i in range(22)]
            for ch in range(8):
                gch = ps_ * 8 + ch
                s = gch % 2
                c.dma("sp", xr[s][:], x_ext[(gch + 1) * 128:(gch + 2) * 128, :], writes=[("xr", s)])
                for hf in range(2):
                    pd = banks[4 + hf]
                    for i in range(22):
                        c.mm(pd[:], actT[:, i, ch * 128:(ch + 1) * 128], wd[:, i, hf * 512:(hf + 1) * 512],
                             i == 0, i == 21, allact + ["wd"], [("pd", hf)])
                    c.tt("dve", xo[s][:, hf * 512:(hf + 1) * 512], pd[:], xr[s][:, hf * 512:(hf + 1) * 512],
                         ALU.add, [("pd", hf), ("xr", s)], [("xo", s)])
                c.dma("sp", x_out[gch * 128:(gch + 1) * 128, :], xo[s][:], reads=[("xo", s)], writes=["d_xout"])
                if halo_out is not None and gch == NCH - 1:
                    c.dma("sp", halo_out, xo[s][120:128, :], reads=[("xo", s)], writes=["d_halo"])


class Section:
    def __init__(self, c, keep=False):
        self.c = c
        self.keep = keep

    def __enter__(self):
        self.old = self.c.es
        self.st = ExitStack()
        self.st.__enter__()
        self.c.es = self.st
        return self

    def __exit__(self, *a):
        if self.keep and a[0] is None:
            if not hasattr(self.c, "kept"):
                self.c.kept = []
            self.c.kept.append(self.st)
            self.c.es = self.old
            return False
        self.c.barrier()
        self.c.es = self.old
        return self.st.__exit__(*a)


def release_kept(c):
    kept = getattr(c, "kept", [])
    if kept:
        c.barrier()
        while kept:
            kept.pop().__exit__(None, None, None)


def gelu_tanh(c, src, skeys, out, okeys, W, sl, n):
    xs, sq, t = W["gx"][sl], W["gs"][sl], W["gt"][sl]
    kx, ks, kt = ("gx", sl), ("gs", sl), ("gt", sl)
    c.act(xs[:, :n], src, AF.Copy, skeys, [kx])
    c.tt("dve", sq[:, :n], xs[:, :n], src, ALU.mult, [kx] + list(skeys), [ks])
    c.ts("dve", t[:, :n], sq[:, :n], 0.044715, 1.0, ALU.mult, ALU.add, [ks], [kt])
    c.tt("dve", sq[:, :n], t[:, :n], xs[:, :n], ALU.mult, [kt, kx], [ks])
    c.act(t[:, :n], sq[:, :n], AF.Sigmoid, [ks], [kt], scale=1.5957691216057308)
    c.tt("dve", out, t[:, :n], xs[:, :n], ALU.mult, [kt, kx], okeys)


def gelu_work(c):
    return {k: [c.sb(f"{k}{i}", [128, 512], F32) for i in range(2)] for k in ("gx", "gs", "gt")}


def rstd_from_ss(c, ss, key, n_inv, w=1):
    c.act(ss[:, w:2 * w], ss[:, 0:w], AF.Ln, [key, "eps"], [key], scale=n_inv, bias=EPS_AP[0])
    c.act(ss[:, 2 * w:3 * w], ss[:, w:2 * w], AF.Exp, [key], [key], scale=-0.5)


def emit_front(c, nc, io):
    (x_ext, g_in, w_u, w_v, w_z, w_xbc, w_dt, w_q, w_k, w_v2, w_g, vgain, wsT_in, bs_in, cw_in, cb_in, dtb_in,
     alog_in, dsk_in, qg_in, kg_in, cos_in, sin_in) = (io[k] for k in (
        "x_ext", "g", "w_u", "w_v", "w_z", "w_xbc", "w_dt", "w_q", "w_k", "w_v2", "w_g", "vgain", "wsT", "bs",
        "cw", "cb", "dtb", "alog", "dsk", "qg", "kg", "cos", "sin"))
    yaT_o, zs_o, yloc_o, eacg_o, CT_o, gT_o = (io[k] for k in ("yaT", "zs", "yloc", "eacg", "CT", "gT"))
    st_send = io["st_send"]
    q_send, k_send, v_send = io["q_send"], io["k_send"], io["v_send"]

    def wview(w):
        return w.rearrange("(k p) n -> p k n", p=128)

    with Section(c):
        ident = setup_consts(c)
        banks = [c.ps(f"bank{i}", [128, 512], F32) for i in range(8)]
        BK = lambda i: ("bank", i)
        g_sb = c.sb("g_sb", [128, 8], F32)
        c.dma("sp", g_sb[:], g_in, writes=["g_sb"])
        hT = c.sb("hT", [128, 8, (NCH + 1) * 128], BF16)
        build_hT(c, x_ext, g_sb, hT, NCH + 1, ident, (banks[7][:].bitcast(BF16), BK(7)))
        allhT = [("hT", ch) for ch in range(NCH + 1)]
        dt_all = c.sb("dt_all", [128, NCH, 16], F32)
        xbcT = c.sb("xbcT", [128, 12, T], BF16)

        with Section(c, keep=True):
            W = gelu_work(c)
            uT = c.sb("uT", [128, 4, T], BF16)
            wblk = [c.sb(f"wblk{i}", [128, 8, 128], BF16) for i in range(2)]
            un = 0
            for fb in range(4):
                s = fb % 2
                c.dma("pool", wblk[s][:], wview(w_u)[:, :, fb * 128:(fb + 1) * 128], writes=[("wblk", s)])
                for tl in range(4):
                    b = un % 2
                    un += 1
                    for k in range(8):
                        c.mm(banks[b][:], wblk[s][:, k, :], hT[:, k, 128 + tl * 512:128 + (tl + 1) * 512],
                             k == 0, k == 7, [("wblk", s)] + allhT, [BK(b)])
                    gelu_tanh(c, banks[b][:], [BK(b)], uT[:, fb, tl * 512:(tl + 1) * 512], [("uT", fb)], W, b, 512)
            alluT = [("uT", fb) for fb in range(4)]
            ws_f = c.sb("ws_f", [128, 4, 128], F32)
            ws_b = c.sb("ws_b", [128, 4, 128], BF16)
            c.dma("sp", ws_f[:], wsT_in, writes=["ws_f"])
            c.op("pool", lambda E: E.affine_select(out=ws_f[:], in_=ws_f[:], pattern=[[0, 4], [1, 128]],
                                                   compare_op=ALU.is_ge, fill=0.0, base=0,
                                                   channel_multiplier=-1), ["ws_f"], ["ws_f"])
            c.copy("pool", ws_b[:], ws_f[:], ["ws_f"], ["ws_b"])
            bs_f = c.sb("bs_f", [1, 512], F32)
            bs_b = c.sb("bs_b", [1, 512], BF16)
            ones_r = c.sb("ones_r", [1, 128], BF16)
            c.dma("sp", bs_f[:], bs_in, writes=["bs_f"])
            c.copy("pool", bs_b[:], bs_f[:], ["bs_f"], ["bs_b"])
            c.op("pool", lambda E: E.memset(ones_r[:], 1.0), writes=["ones_r"])
            vg_bc = c.sb("vg_bc", [128, 512], F32)
            c.dma("sp", vg_bc[:], vgain.partition_broadcast(128), writes=["vg_bc"])
            wv = c.sb("wv", [128, 8, 512], BF16)
            c.dma("pool", wv[:], wview(w_v), writes=["wv"])
            gv = [c.sb(f"gv{i}", [128, 512], F32) for i in range(2)]
            vn = [c.sb(f"vn{i}", [128, 512], BF16) for i in range(2)]
            vss = [c.sb(f"vss{i}", [128, 4], F32) for i in range(2)]
            vjunk = c.sb("vjunk", [128, 512], BF16)
            def v_stage1(ch):
                s = ch % 2
                cs = slice(128 + ch * 128, 128 + (ch + 1) * 128)
                pb = 2 + s
                for k in range(8):
                    c.mm(banks[pb][:], hT[:, k, cs], wv[:, k, :], k == 0, k == 7, allhT + ["wv"], [BK(pb)])
                gelu_tanh(c, banks[pb][:], [BK(pb)], gv[s][:], [("gv", s)], W, s, 512)

            def v_stage2(ch):
                s = ch % 2
                c.act(vjunk[:], gv[s][:], AF.Square, [("gv", s)], ["vjunk", ("vss", s)], accum_out=vss[s][:, 0:1])
                rstd_from_ss(c, vss[s], ("vss", s), 1.0 / 512)
                c.stt(vn[s][:], gv[s][:], vss[s][:, 2:3], vg_bc[:], ALU.mult, ALU.mult,
                      [("gv", s), ("vss", s), "vg_bc"], [("vn", s)])

            def v_stage3(ch):
                s = ch % 2
                pm = 4 + s
                for g in range(4):
                    c.mm(banks[pm][:, g * 128:(g + 1) * 128], vn[s][:, g * 128:(g + 1) * 128], ws_b[:, g, :],
                         True, False, [("vn", s), "ws_b"], [BK(pm)])
                    c.mm(banks[pm][:, g * 128:(g + 1) * 128], ones_r[0:1, :], bs_b[0:1, g * 128:(g + 1) * 128],
                         False, True, ["ones_r", "bs_b"], [BK(pm)])
                uv = uT[:, :, ch * 128:(ch + 1) * 128]
                c.tt("dve", uv, uv, banks[pm][:].rearrange("p (g t) -> p g t", g=4), ALU.mult,
                     alluT + [BK(pm)], [("ya", ch)])

            v_stage1(0)
            for ch in range(NCH):
                if ch + 1 < NCH:
                    v_stage1(ch + 1)
                v_stage2(ch)
                v_stage3(ch)
            c.dma("sp", yaT_o.rearrange("(g p) t -> p g t", p=128), uT[:],
                  reads=[("ya", ch) for ch in range(NCH)], writes=["d_yaT"])

        with Section(c, keep=True):
            wblk = [c.sb(f"gwblk{i}", [128, 8, 128], BF16) for i in range(2)]
            gt = [c.sb(f"gt{i}", [128, T], BF16) for i in range(2)]
            un = 0
            for fb in range(24):
                s = fb % 2
                c.dma("pool", wblk[s][:], wview(w_g)[:, :, fb * 128:(fb + 1) * 128], writes=[("gwblk", s)])
                for tl in range(4):
                    b = un % 4
                    un += 1
                    for k in range(8):
                        c.mm(banks[b][:], wblk[s][:, k, :], hT[:, k, 128 + tl * 512:128 + (tl + 1) * 512],
                             k == 0, k == 7, [("gwblk", s)] + allhT, [BK(b)])
                    c.act(gt[s][:, tl * 512:(tl + 1) * 512], banks[b][:], AF.Sigmoid, [BK(b)], [("gt", s)])
                c.dma("sp", gT_o[fb * 128:(fb + 1) * 128, :], gt[s][:], reads=[("gt", s)], writes=["d_gT"])

        with Section(c, keep=True):
            xbc_conv(c, banks, hT, allhT, wview(w_xbc), cw_in, cb_in, xbcT, CT_o)

        release_kept(c)
        with Section(c):
            wz = c.sb("wz", [128, 8, 1024], BF16)
            wq = c.sb("wq", [128, 8, 512], BF16)
            wk = c.sb("wk", [128, 8, 512], BF16)
            wv2 = c.sb("wv2", [128, 8, 512], BF16)
            wdt = c.sb("wdt", [128, 8, 16], BF16)
            for t_, w_, k_ in ((wz, w_z, "wz"), (wdt, w_dt, "wdt"), (wq, w_q, "wq"), (wk, w_k, "wk"), (wv2, w_v2, "wv2")):
                c.dma("pool", t_[:], wview(w_), writes=[k_])
            dtb_bc = c.sb("dtb_bc", [128, 16], F32)
            c.dma("sp", dtb_bc[:], dtb_in.partition_broadcast(128), writes=["dtb_bc"])
            qg_bc = c.sb("qg_bc", [128, 64], F32)
            kg_bc = c.sb("kg_bc", [128, 64], F32)
            c.dma("sp", qg_bc[:], qg_in.partition_broadcast(128), writes=["qg_bc"])
            c.dma("sp", kg_bc[:], kg_in.partition_broadcast(128), writes=["kg_bc"])
            c.ts("dve", qg_bc[:], qg_bc[:], 0.125, None, ALU.mult, None, ["qg_bc"], ["qg_bc"])
            cos_sb = c.sb("cos_sb", [128, NCH, 8], F32)
            sin_sb = c.sb("sin_sb", [128, NCH, 8], F32)
            c.dma("sp", cos_sb[:], cos_in.rearrange("(c p) f -> p c f", p=128), writes=["cos_sb"])
            c.dma("sp", sin_sb[:], sin_in.rearrange("(c p) f -> p c f", p=128), writes=["sin_sb"])
            qT_sb = c.sb("qT_sb", [128, 4, T], BF16)
            kT_sb = c.sb("kT_sb", [128, 4, T], BF16)
            zsb = [c.sb(f"zsb{i}", [128, 1024], BF16) for i in range(2)]
            va = [c.sb(f"va{i}", [128, 4, 130], BF16) for i in range(2)]
            for i in range(2):
                c.op("pool", lambda E: E.memset(va[i][:, :, 128:129], 1.0), writes=[("va", i)])
                c.op("pool", lambda E: E.memset(va[i][:, :, 129:130], 0.0), writes=[("va", i)])
            sp_t = [c.sb(f"sp_t{i}", [128, 4, 16], F32) for i in range(2)]
            qsq = [c.sb(f"qsq{i}", [128, 512], F32) for i in range(2)]
            qn = [c.sb(f"qn{i}", [128, 512], F32) for i in range(2)]
            qb = [c.sb(f"qb{i}", [128, 512], BF16) for i in range(4)]
            pend_tr = []
            qss = [c.sb(f"qss{i}", [128, 24], F32) for i in range(2)]
            rp = [c.sb(f"rp{i}", [128, 4, 8, 8], F32) for i in range(2)]
            pTb = banks[7][:].bitcast(BF16)
            it = 0
            for ch in range(NCH):
                s = ch % 2
                cs = slice(128 + ch * 128, 128 + (ch + 1) * 128)
                for hf in range(2):
                    for k in range(8):
                        c.mm(banks[hf][:], hT[:, k, cs], wz[:, k, hf * 512:(hf + 1) * 512], k == 0, k == 7,
                             allhT + ["wz"], [BK(hf)])
                    c.act(zsb[s][:, hf * 512:(hf + 1) * 512], banks[hf][:], AF.Silu, [BK(hf)], [("zsb", s)])
                c.dma("sp", zs_o[ch * 128:(ch + 1) * 128, :], zsb[s][:], reads=[("zsb", s)], writes=["d_zs"])
                for k in range(8):
                    c.mm(banks[6][:, 0:16], hT[:, k, cs], wdt[:, k, :], k == 0, k == 7, allhT + ["wdt"], [BK(6)])
                st = sp_t[s]
                ks_ = ("sp_t", s)
                c.tt("dve", st[:, 0, :], banks[6][:, 0:16], dtb_bc[:], ALU.add, [BK(6), "dtb_bc"], [ks_])
                c.ts("dve", st[:, 1, :], st[:, 0, :], -1.0, None, ALU.mult, None, [ks_], [ks_])
                c.tt("dve", st[:, 1, :], st[:, 1, :], st[:, 0, :], ALU.max, [ks_], [ks_])
                c.act(st[:, 2, :], st[:, 1, :], AF.Exp, [ks_], [ks_], scale=-1.0)
                c.act(st[:, 3, :], st[:, 2, :], AF.Ln, [ks_, "eps"], [ks_], bias=ONE_AP[0])
                c.ts("dve", st[:, 1, :], st[:, 0, :], 0.0, None, ALU.max, None, [ks_], [ks_])
                c.tt("dve", dt_all[:, ch, :], st[:, 1, :], st[:, 3, :], ALU.add, [ks_], [("dt_all", ch)])
                for (w_t, wkey, g_bc, gkey, dstT, dkey) in ((wq, "wq", qg_bc, "qg_bc", qT_sb, "qT"),
                                                            (wk, "wk", kg_bc, "kg_bc", kT_sb, "kT")):
                    b2 = it % 2
                    b4 = it % 4
                    it += 1
                    pb = 2 + b2
                    for k in range(8):
                        c.mm(banks[pb][:], hT[:, k, cs], w_t[:, k, :], k == 0, k == 7, allhT + [wkey], [BK(pb)])
                    if len(pend_tr) >= 2:
                        pend_tr.pop(0)()
                    c.act(qsq[b2][:], banks[pb][:], AF.Square, [BK(pb)], [("qsq", b2)])
                    c.op("dve", lambda E: E.tensor_reduce(out=qss[b2][:, 0:8],
                                                         in_=qsq[b2][:].rearrange("p (g d) -> p g d", g=8),
                                                         axis=AX.X, op=ALU.add), [("qsq", b2)], [("qss", b2)])
                    rstd_from_ss(c, qss[b2], ("qss", b2), 1.0 / 64, w=8)
                    c.tt("dve", qn[b2][:].rearrange("p (g d) -> p g d", g=8),
                         banks[pb][:].rearrange("p (g d) -> p g d", g=8),
                         qss[b2][:, 16:24].unsqueeze(2).to_broadcast([128, 8, 64]), ALU.mult,
                         [BK(pb), ("qss", b2)], [("qn", b2)])
                    q3 = qn[b2][:].rearrange("p (g d) -> p g d", g=8)
                    c.tt("dve", q3, q3, g_bc[:, :].unsqueeze(1).to_broadcast([128, 8, 64]), ALU.mult,
                         [("qn", b2), gkey], [("qn", b2)])
                    c.copy("act", qb[b4][:], qn[b2][:], [("qn", b2)], [("qb", b4)])
                    qb3 = qb[b4][:].rearrange("p (g d) -> p g d", g=8)
                    cosb = cos_sb[:, ch, :].unsqueeze(1).to_broadcast([128, 8, 8])
                    sinb = sin_sb[:, ch, :].unsqueeze(1).to_broadcast([128, 8, 8])
                    r = rp[b2]
                    rk = ("rp", b2)
                    c.tt("dve", r[:, 0], q3[:, :, 0:8], cosb, ALU.mult, [("qn", b2), "cos_sb"], [rk])
                    c.tt("dve", r[:, 1], q3[:, :, 8:16], sinb, ALU.mult, [("qn", b2), "sin_sb"], [rk])
                    c.tt("dve", r[:, 2], q3[:, :, 8:16], cosb, ALU.mult, [("qn", b2), "cos_sb"], [rk])
                    c.tt("dve", r[:, 3], q3[:, :, 0:8], sinb, ALU.mult, [("qn", b2), "sin_sb"], [rk])
                    c.tt("dve", qb3[:, :, 0:8], r[:, 0], r[:, 1], ALU.subtract, [rk, ("qb", b4)], [("qb", b4)])
                    c.tt("dve", qb3[:, :, 8:16], r[:, 2], r[:, 3], ALU.add, [rk, ("qb", b4)], [("qb", b4)])

                    def do_tr(b4=b4, dstT=dstT, dkey=dkey, ch=ch):
                        for h in range(4):
                            c.tr(pTb[:, h * 128:(h + 1) * 128], qb[b4][:, h * 128:(h + 1) * 128], ident[:],
                                 [("qb", b4), "ident"], [BK(7)])
                        c.copy("act", dstT[:, :, ch * 128:(ch + 1) * 128],
                               pTb[:, 0:512].rearrange("p (h t) -> p h t", h=4), [BK(7)], [(dkey, ch)])
                    pend_tr.append(do_tr)
                for k in range(8):
                    c.mm(banks[4 + s][:], hT[:, k, cs], wv2[:, k, :], k == 0, k == 7, allhT + ["wv2"], [BK(4 + s)])
                c.copy("dve", va[s][:, :, 0:128], banks[4 + s][:].rearrange("p (h d) -> p h d", h=4),
                       [BK(4 + s)], [("va", s)])
                vp, lb = VPIECE[ch]
                c.dma("sp", v_send[vp].rearrange("(h p) (b d) -> p h b d", p=128, d=130)[:, :, lb, :], va[s][:],
                      reads=[("va", s)], writes=[("d_vs", vp)])
            while pend_tr:
                pend_tr.pop(0)()
            for pc in range(2):
                c.dma("sp", q_send[pc].rearrange("(h p) t -> p h t", p=128), qT_sb[:, :, pc * 1024:(pc + 1) * 1024],
                      reads=[("qT", ch) for ch in range(NCH)], writes=[("d_qs", pc)])
                c.dma("sp", k_send[pc].rearrange("(h p) t -> p h t", p=128), kT_sb[:, :, pc * 1024:(pc + 1) * 1024],
                      reads=[("kT", ch) for ch in range(NCH)], writes=[("d_ks", pc)])

        if io.get("after_qkv") is not None:
            io["after_qkv"]()
        with Section(c):
            ssd_loop(c, nc, banks, ident, dt_all, alog_in, dsk_in, xbcT, yloc_o, eacg_o,
                     st_send[:, 0:1024], st_send[:, 1024:1040])


ONE_AP = [None]


def xbc_conv(c, banks, hT, allhT, wxv, cw_in, cb_in, xbcT, CT_o):
    BK = lambda i: ("bank", i)
    cw = c.sb("scw", [128, 12, 4], F32)
    cb = c.sb("scb", [128, 12], F32)
    c.dma("sp", cw[:], cw_in, writes=["scw"])
    c.dma("sp", cb[:], cb_in, writes=["scb"])
    wblk = [c.sb(f"xwblk{i}", [128, 8, 128], BF16) for i in range(2)]
    acc = [c.sb(f"xacc{i}", [128, 512], F32) for i in range(2)]
    xbuf = [c.sb(f"xxb{i}", [128, 515], BF16) for i in range(2)]
    un = 0
    pend = None
    for fb in range(12):
        s = fb % 2
        c.dma("pool", wblk[s][:], wxv[:, :, fb * 128:(fb + 1) * 128], writes=[("xwblk", s)])
        for tl in range(4):
            b = un % 2
            un += 1
            c0 = 128 + tl * 512
            pt = banks[b]
            pk = BK(b)
            xb = xbuf[b]
            xk = ("xxb", b)
            if tl == 0:
                ph = banks[6][:, 0:3]
                for k in range(8):
                    c.mm(ph, wblk[s][:, k, :], hT[:, k, c0 - 3:c0], k == 0, k == 7, [("xwblk", s)] + allhT, [BK(6)])
                c.copy("act", xb[:, 0:3], ph, [BK(6)], [xk])
            else:
                c.copy("act", xb[:, 0:3], xbuf[1 - b][:, 512:515], [("xxb", 1 - b)], [xk])
            for k in range(8):
                c.mm(pt[:], wblk[s][:, k, :], hT[:, k, c0:c0 + 512], k == 0, k == 7, [("xwblk", s)] + allhT, [pk])
            a = acc[b]
            ak = ("xacc", b)
            c.act(xb[:, 3:515], pt[:], AF.Copy, [pk], [xk])
            c.act(a[:], pt[:], AF.Identity, [pk, "scw", "scb"], [ak], scale=cw[:, fb, 3:4], bias=cb[:, fb:fb + 1])
            for sh in (1, 2, 3):
                c.stt(a[:], xb[:, 3 - sh:515 - sh], cw[:, fb, 3 - sh:4 - sh], a[:], ALU.mult, ALU.add,
                      [xk, ak, "scw"], [ak])
            if pend is not None:
                pend()

            def tail(b=b, fb=fb, tl=tl):
                c.act(xbcT[:, fb, tl * 512:(tl + 1) * 512], acc[b][:], AF.Silu, [("xacc", b)], [("xbcT", fb)])
            pend = tail
    pend()
    allx = [("xbcT", fb) for fb in range(12)]
    c.dma("sp", CT_o.rearrange("(g p) t -> p g t", p=128), xbcT[:, 10:12, :], reads=allx, writes=["d_CT"])


def ssd_loop(c, nc, banks, ident, dt_all, alog_in, dsk_in, xbcT, yloc_o, eacg_o, sfin_o, logd_o):
    BK = lambda i: ("bank", i)
    allx = [("xbcT", fb) for fb in range(12)]
    triT = c.sb("triT", [128, 128], F32)
    tri_b = c.sb("tri_b", [128, 128], BF16)
    lstr = c.sb("lstr", [128, 128], F32)
    ones = c.sb("ones_f", [128, 128], F32)
    c.op("pool", lambda E: E.memset(ones[:], 1.0), writes=["ones_f"])
    c.op("pool", lambda E: E.memset(triT[:], 1.0), writes=["triT"])
    c.op("pool", lambda E: E.affine_select(out=triT[:], in_=triT[:], pattern=[[1, 128]], compare_op=ALU.is_ge,
                                           fill=0.0, base=0, channel_multiplier=-1), ["triT"], ["triT"])
    c.copy("pool", tri_b[:], triT[:], ["triT"], ["tri_b"])
    c.op("pool", lambda E: E.memset(lstr[:], 1.0), writes=["lstr"])
    c.op("pool", lambda E: E.affine_select(out=lstr[:], in_=lstr[:], pattern=[[-1, 128]], compare_op=ALU.is_gt,
                                           fill=0.0, base=0, channel_multiplier=1), ["lstr"], ["lstr"])
    a_bc = c.sb("a_bc", [128, 16], F32)
    d_bc = c.sb("d_bc", [128, 16], F32)
    c.dma("sp", a_bc[:], alog_in.partition_broadcast(128), writes=["a_bc"])
    c.dma("sp", d_bc[:], dsk_in.partition_broadcast(128), writes=["d_bc"])
    c.act(a_bc[:], a_bc[:], AF.Exp, ["a_bc"], ["a_bc"])
    c.ts("dve", a_bc[:], a_bc[:], -1.0, None, ALU.mult, None, ["a_bc"], ["a_bc"])
    state = c.sb("state", [128, 1024], F32)
    prev_bf = c.sb("prev_bf", [128, 1024], BF16)
    offs = c.sb("offs", [128, 16], F32)
    eacg_all = c.sb("eacg_all", [128, NCH, 16], F32)
    c.op("pool", lambda E: E.memset(state[:], 0.0), writes=["state"])
    c.op("pool", lambda E: E.memset(prev_bf[:], 0.0), writes=["prev_bf"])
    c.op("pool", lambda E: E.memset(offs[:], 0.0), writes=["offs"])
    xs_tok = [c.sb(f"xs_tok{i}", [128, 1024], BF16) for i in range(2)]
    B_tok = [c.sb(f"B_tok{i}", [128, 256], BF16) for i in range(2)]
    sm = [c.sb(f"sm{i}", [128, 12, 16], F32) for i in range(2)]
    Lh_ = [c.sb(f"Lh{i}", [128, 16, 128], BF16) for i in range(2)]
    decT_ = [c.sb(f"decT{i}", [128, 16, 128], BF16) for i in range(2)]
    cb_sb_ = [c.sb(f"cb_sb{i}", [128, 2, 128], BF16) for i in range(2)]
    MT = [c.sb(f"MT{i}", [128, 16, 128], BF16) for i in range(2)]
    xdt = [c.sb(f"xdt{i}", [128, 1024], BF16) for i in range(2)]
    xdt2 = [c.sb(f"xdt2{i}", [128, 1024], BF16) for i in range(2)]
    ydt_ = [c.sb(f"ydt{i}", [128, 1024], F32) for i in range(2)]
    ytmp = c.sb("ytmp", [128, 1024], F32)
    yl = [c.sb(f"yl{i}", [128, 1024], F32) for i in range(2)]
    t2_ = [c.sb(f"t2{i}", [128, 1024], F32) for i in range(2)]
    pA = banks[0][:].bitcast(BF16)
    pC = banks[1]
    pCb = banks[1][:].bitcast(BF16)

    def stage_a(ch):
        s = ch % 2
        cs = slice(ch * 128, (ch + 1) * 128)
        v = sm[s]
        vk = ("sm", s)
        Lh, decT, cb_sb, t2, ydt = Lh_[s], decT_[s], cb_sb_[s], t2_[s], ydt_[s]
        for fb in range(8):
            c.tr(pA[:, fb * 128:(fb + 1) * 128], xbcT[:, fb, cs], ident[:], allx + ["ident"], [BK(0)])
        c.copy("act", xs_tok[s][:], pA, [BK(0)], [("xs_tok", s)])
        for g in range(2):
            c.tr(pCb[:, 640 + g * 128:640 + (g + 1) * 128], xbcT[:, 8 + g, cs], ident[:], allx + ["ident"], [BK(1)])
        c.copy("act", B_tok[s][:], pCb[:, 640:896], [BK(1)], [("B_tok", s)])
        dt = dt_all[:, ch, :]
        c.tt("dve", v[:, 0, :], dt, a_bc[:], ALU.mult, [("dt_all", ch), "a_bc"], [vk])
        c.mm(pC[:, 0:16], triT[:], v[:, 0, :], True, True, ["triT", vk], [BK(1)])
        c.mm(pC[:, 16:32], ones[:], v[:, 0, :], True, True, ["ones_f", vk], [BK(1)])
        c.copy("dve", v[:, 1:3, :], pC[:, 0:32].rearrange("p (a h) -> p a h", a=2), [BK(1)], [vk])
        for hf, eng in ((0, "dve"), (1, "pool")):
            c.tt(eng, Lh[:, hf * 8:(hf + 1) * 8, :], lstr[:, :].unsqueeze(1).to_broadcast([128, 8, 128]),
                 v[:, 0, hf * 8:(hf + 1) * 8].unsqueeze(2).to_broadcast([128, 8, 128]), ALU.mult,
                 ["lstr", vk], [("Lh", s, hf)])
        for rnd in range(2):
            for q in range(2):
                for hh in range(4):
                    h = rnd * 8 + q * 4 + hh
                    c.mm(banks[2 + q][:, hh * 128:(hh + 1) * 128], Lh[:, h, :], tri_b[:], True, True,
                         [("Lh", s, rnd), "tri_b"], [BK(2 + q)])
                c.act(decT[:, rnd * 8 + q * 4:rnd * 8 + q * 4 + 4, :],
                      banks[2 + q][:].rearrange("p (h t) -> p h t", h=4), AF.Exp, [BK(2 + q)], [("decT", s, rnd)])
        for g in range(2):
            c.mm(pC[:, 32 + g * 128:32 + (g + 1) * 128], xbcT[:, 8 + g, cs], xbcT[:, 10 + g, cs], True, True,
                 allx, [BK(1)])
        c.tt("dve", cb_sb[:], pC[:, 32:288].rearrange("p (g t) -> p g t", g=2),
             triT[:, :].unsqueeze(1).to_broadcast([128, 2, 128]), ALU.mult, [BK(1), "triT"], [("cb_sb", s)])
        for g in range(2):
            c.tt("dve", MT[s][:, g * 8:(g + 1) * 8, :], decT[:, g * 8:(g + 1) * 8, :],
                 cb_sb[:, g, :].unsqueeze(1).to_broadcast([128, 8, 128]), ALU.mult,
                 [("decT", s, g), ("cb_sb", s)], [("MT", s, g)])
        c.act(v[:, 3, :], v[:, 1, :], AF.Exp, [vk], [vk])
        c.tt("dve", v[:, 4, :], v[:, 2, :], v[:, 1, :], ALU.subtract, [vk], [vk])
        c.act(v[:, 5, :], v[:, 4, :], AF.Exp, [vk], [vk])
        c.tt("dve", v[:, 6, :], v[:, 5, :], dt, ALU.mult, [vk, ("dt_all", ch)], [vk])
        c.tt("dve", v[:, 7, :], v[:, 1, :], offs[:], ALU.add, [vk, "offs"], [vk])
        c.act(eacg_all[:, ch, :], v[:, 7, :], AF.Exp, [vk], [("eacg", ch)])
        c.tt("dve", offs[:], offs[:], v[:, 2, :], ALU.add, [vk, "offs"], ["offs"])
        c.act(v[:, 8, :], v[:, 2, :], AF.Exp, [vk], [vk])
        x3 = xs_tok[s][:].rearrange("p (h d) -> p h d", h=16)
        c.tt("dve", xdt[s][:].rearrange("p (h d) -> p h d", h=16), x3,
             dt.unsqueeze(2).to_broadcast([128, 16, 64]), ALU.mult, [("xs_tok", s), ("dt_all", ch)], [("xdt", s)])
        c.tt("pool", xdt2[s][:].rearrange("p (h d) -> p h d", h=16), x3,
             v[:, 6, :].unsqueeze(2).to_broadcast([128, 16, 64]), ALU.mult, [("xs_tok", s), vk], [("xdt2", s)])
        c.tt("pool", t2[:].rearrange("p (h d) -> p h d", h=16), x3,
             d_bc[:, :].unsqueeze(2).to_broadcast([128, 16, 64]), ALU.mult, [("xs_tok", s), "d_bc"], [("t2", s)])
        for h in range(16):
            pb = 4 + h // 8
            c.mm(banks[pb][:, (h % 8) * 64:(h % 8 + 1) * 64], MT[s][:, h, :], xdt[s][:, h * 64:(h + 1) * 64],
                 True, True, [("MT", s, h // 8), ("xdt", s)], [BK(pb)])
        for g in range(2):
            hs = slice(g * 512, (g + 1) * 512)
            c.tt("dve", ydt[:, hs], banks[4 + g][:], t2[:, hs], ALU.add, [BK(4 + g), ("t2", s)], [("ydt", s, g)])

    def stage_b(ch):
        s = ch % 2
        cs = slice(ch * 128, (ch + 1) * 128)
        v = sm[s]
        vk = ("sm", s)
        ydt = ydt_[s]
        for g in range(2):
            c.mm(banks[6 + g][:], xbcT[:, 10 + g, cs], prev_bf[:, g * 512:(g + 1) * 512], True, True,
                 allx + ["prev_bf"], [BK(6 + g)])
        for g in range(2):
            hs = slice(g * 512, (g + 1) * 512)
            c.tt("dve", ytmp[:, hs].rearrange("p (h d) -> p h d", h=8),
                 banks[6 + g][:].rearrange("p (h d) -> p h d", h=8),
                 v[:, 3, g * 8:(g + 1) * 8].unsqueeze(2).to_broadcast([128, 8, 64]), ALU.mult,
                 [BK(6 + g), vk], [("ytmp", g)])
        c.tt("dve", yl[s][:], ytmp[:], ydt[:], ALU.add, [("ytmp", 0), ("ytmp", 1), ("ydt", s, 0), ("ydt", s, 1)],
             [("yl", s)])
        c.dma("sp", yloc_o[ch * 128:(ch + 1) * 128, :], yl[s][:], reads=[("yl", s)], writes=["d_yloc"])
        for g in range(2):
            c.mm(banks[6 + g][:], B_tok[s][:, g * 128:(g + 1) * 128], xdt2[s][:, g * 512:(g + 1) * 512], True, True,
                 [("B_tok", s), ("xdt2", s)], [BK(6 + g)])
        c.tt("pool", state[:].rearrange("p (h d) -> p h d", h=16), state[:].rearrange("p (h d) -> p h d", h=16),
             v[:, 8, :].unsqueeze(2).to_broadcast([128, 16, 64]), ALU.mult, ["state", vk], ["state"])
        for g in range(2):
            hs = slice(g * 512, (g + 1) * 512)
            c.tt("dve", state[:, hs], state[:, hs], banks[6 + g][:], ALU.add, ["state", BK(6 + g)], ["state"])
        c.copy("act", prev_bf[:], state[:], ["state"], ["prev_bf"])

    stage_a(0)
    for ch in range(NCH):
        if ch + 1 < NCH:
            stage_a(ch + 1)
        stage_b(ch)
    c.dma("sp", sfin_o, state[:], reads=["state"], writes=["d_st"])
    c.dma("sp", logd_o, offs[:], reads=["offs"], writes=["d_st"])
    c.dma("sp", eacg_o.rearrange("(c p) h -> p c h", p=128), eacg_all[:], reads=[("eacg", ch) for ch in range(NCH)],
          writes=["d_eacg"])


IN_SIZES = (512, 512, 1024, 1536, 16, 512, 512, 512, 3072)
IN_OFFS = np.concatenate([[0], np.cumsum(IN_SIZES)]).astype(int)


def rope_tables_np():
    pos = np.arange(S, dtype=np.float32)
    inv_freq = (1.0 / (np.float32(500000.0) ** (np.arange(0, 16, 2, dtype=np.float32) / np.float32(16)))).astype(np.float32)
    ang = (pos[:, None] * inv_freq[None, :]).astype(np.float32)
    return np.cos(ang).astype(np.float32), np.sin(ang).astype(np.float32)


def ext_rows(full, b, j, halo=128):
    h = np.zeros((halo,) + full.shape[2:], full.dtype)
    if j > 0:
        h = full[b, j * T - halo:j * T]
    return np.ascontiguousarray(np.concatenate([h, full[b, j * T:(j + 1) * T]], 0))


VPIECE = [(0, b) for b in range(6)] + [(1, b) for b in range(6)] + [(2, b) for b in range(4)]
VPW = (6, 6, 4)


def emit_attn(c, nc, io, layer):
    lam_init = 0.8 - 0.6 * math.exp(-0.3 * layer)
    lam_in, idx_in = io["lam"], io["idx"]
    q_g, k_g, v_g, y_send = io["q_g"], io["k_g"], io["v_g"], io["y_send"]
    NKB = S // 128
    with Section(c):
        setup_consts(c)
        sc = [c.ps(f"sc{i}", [128, 1024], F32) for i in range(2)]
        accs = c.ps("accs", [128, 2048], F32)
        qT = c.sb("qT", [128, S], BF16)
        kT = c.sb("kT", [128, S], BF16)
        V = c.sb("V", [128, NKB, 130], BF16)
        idx = c.sb("idx", [128, 4], mybir.dt.int32)
        c.dma("sp", idx[:], idx_in, writes=["idx"])
        for i in range(4):
            for pc in range(2):
                sl = slice(i * 2048 + pc * 1024, i * 2048 + (pc + 1) * 1024)
                c.op16("pool", lambda E: E.indirect_dma_start(
                    out=qT[:, sl], out_offset=None, in_=q_g[pc],
                    in_offset=bass.IndirectOffsetOnAxis(ap=idx[:, i:i + 1], axis=0)),
                    ["idx", ("g_q", pc)], [("qT", i)])
                c.op16("pool", lambda E: E.indirect_dma_start(
                    out=kT[:, sl], out_offset=None, in_=k_g[pc],
                    in_offset=bass.IndirectOffsetOnAxis(ap=idx[:, i:i + 1], axis=0)),
                    ["idx", ("g_k", pc)], [("kT", i)])
            b0 = 0
            for vp in range(3):
                c.op16("pool", lambda E: E.indirect_dma_start(
                    out=V[:, i * 16 + b0:i * 16 + b0 + VPW[vp], :].rearrange("p b d -> p (b d)"), out_offset=None,
                    in_=v_g[vp], in_offset=bass.IndirectOffsetOnAxis(ap=idx[:, i:i + 1], axis=0)),
                    ["idx", ("g_v", vp)], [("V", i)])
                b0 += VPW[vp]
        lv = c.sb("lv", [128, 256], F32)
        c.dma("sp", lv[:], lam_in.partition_broadcast(128), writes=["lv"])
        lp = c.sb("lp", [128, 2, 64], F32)
        l2 = c.sb("l2", [128, 8], F32)
        lv4 = lv[:].rearrange("p (a d) -> p a d", a=4)
        c.tt("dve", lp[:, 0, :], lv4[:, 0, :], lv4[:, 1, :], ALU.mult, ["lv"], ["lp"])
        c.tt("dve", lp[:, 1, :], lv4[:, 2, :], lv4[:, 3, :], ALU.mult, ["lv", "lp"], ["lp"])
        c.op("dve", lambda E: E.tensor_reduce(out=l2[:, 0:2], in_=lp[:], axis=AX.X, op=ALU.add), ["lp"], ["l2"])
        c.act(l2[:, 2:4], l2[:, 0:2], AF.Exp, ["l2"], ["l2"])
        c.tt("dve", l2[:, 4:5], l2[:, 3:4], l2[:, 2:3], ALU.subtract, ["l2"], ["l2"])
        c.ts("dve", l2[:, 5:6], l2[:, 4:5], -lam_init, None, ALU.add, None, ["l2"], ["l2"])
        sgc = c.sb("sgc", [128, 1], F32)
        c.dma("sp", sgc[:], io["sgc"], writes=["sgc"])
        c.ts("dve", sgc[:], sgc[:], 1.0 - lam_init, None, ALU.mult, None, ["sgc"], ["sgc"])
        ones = c.sb("ones_f", [128, 128], F32)
        c.op("pool", lambda E: E.memset(ones[:], 1.0), writes=["ones_f"])
        ones_b = c.sb("ones_b", [128, 128], BF16)
        c.op("pool", lambda E: E.memset(ones_b[:], 1.0), writes=["ones_b"])
        tri = c.sb("tri", [128, 128], BF16)
        c.op("pool", lambda E: E.memset(tri[:], 1.0), writes=["tri"])
        c.op("pool", lambda E: E.affine_select(out=tri[:], in_=tri[:], pattern=[[1, 128]], compare_op=ALU.is_ge,
                                               fill=0.0, base=0, channel_multiplier=-1), ["tri"], ["tri"])
        PT = [c.sb(f"PT{i}", [128, 2, 512], BF16) for i in range(2)]
        accS = [c.sb(f"accS{i}", [128, 2, 512], F32) for i in range(2)]
        rL = c.sb("rL", [128, 2, 512], F32)
        o_t = c.sb("o_t", [128, 512], F32)
        o_u = c.sb("o_u", [128, 512], F32)
        yb = [c.sb(f"ybo{i}", [128, 512], BF16) for i in range(2)]
        units = [(qg, kb) for qg in range(S // 512) for kb in range(4 * qg + 4)]

        def emit_scores(u):
            qg, kb = units[u]
            r = kb - 4 * qg
            c0 = max(r, 0) * 128
            b2 = u % 2
            for cp in range(2):
                ps_ = slice(cp * 64, (cp + 1) * 64)
                c.mm(sc[b2][:, cp * 512 + c0:(cp + 1) * 512], kT[ps_, kb * 128:(kb + 1) * 128],
                     qT[ps_, qg * 512 + c0:(qg + 1) * 512], True, True,
                     [("qT", qg // 4), ("kT", kb // 16)], [("sc", b2)])
            c.act(PT[b2][:, :, c0:512], sc[b2][:].rearrange("p (a q) -> p a q", a=2)[:, :, c0:512], AF.Exp,
                  [("sc", b2)], [("PT", b2)])
            if r >= 0:
                c.tt("dve", PT[b2][:, :, c0:c0 + 128], PT[b2][:, :, c0:c0 + 128],
                     tri[:, :].unsqueeze(1).to_broadcast([128, 2, 128]), ALU.mult,
                     [("PT", b2), "tri"], [("PT", b2)])

        def emit_pv(u):
            qg, kb = units[u]
            r = kb - 4 * qg
            c0 = max(r, 0) * 128
            b2 = u % 2
            g2 = qg % 2
            for cp in range(2):
                c.mm(accs[:, cp * 512 + c0:(cp + 1) * 512], V[:, kb, 0:128], PT[b2][:, cp, c0:512],
                     kb == 0, False, [("PT", b2), ("V", kb // 16)], [("acc", cp)], skip_group_check=True)
            c.mm(accs[:, 1536 + c0:2048], ones_b[:], PT[b2][:, 1, c0:512], kb == 0, False,
                 [("PT", b2), "ones_b"], [("acc", 3)], skip_group_check=True)
            if kb == 0:
                c.copy("dve", accS[g2][:, 0, :], PT[b2][:, 0, :], [("PT", b2)], [("accS", g2)])
            else:
                c.tt("dve", accS[g2][:, 0, c0:512], accS[g2][:, 0, c0:512], PT[b2][:, 0, c0:512], ALU.add,
                     [("PT", b2), ("accS", g2)], [("accS", g2)])

        o0s = [c.sb(f"o0s{i}", [128, 512], F32) for i in range(2)]
        o1s = [c.sb(f"o1s{i}", [128, 512], F32) for i in range(2)]
        pending = []

        def finalize(qg, u_now, gap):
            g2 = qg % 2

            def f0():
                c.copy("act", o0s[g2][:], accs[:, 0:512], [("acc", 0)], [("o0s", g2)])
                c.copy("act", o1s[g2][:], accs[:, 512:1024], [("acc", 1)], [("o1s", g2)])
                c.mm(accs[:, 1024:1536], ones[:], accS[g2][:, 0, :], True, True, ["ones_f", ("accS", g2)], [("acc", 2)])
                c.act(rL[:, 1, :], accs[:, 1536:2048], AF.Ln, [("acc", 3)], [("rL", 1)])

            def f1():
                c.act(rL[:, 0, :], accs[:, 1024:1536], AF.Ln, [("acc", 2)], [("rL", 0)])
                c.act(rL[:, 0, :], rL[:, 0, :], AF.Exp, [("rL", 0)], [("rL", 0)], scale=-1.0)
                c.act(rL[:, 1, :], rL[:, 1, :], AF.Exp, [("rL", 1)], [("rL", 1)], scale=-1.0)

            def f2():
                c.tt("dve", o_t[:], o0s[g2][:], rL[:, 0, :], ALU.mult, [("o0s", g2), ("rL", 0)], ["o_t"])
                c.tt("dve", o_u[:], o1s[g2][:], rL[:, 1, :], ALU.mult, [("o1s", g2), ("rL", 1)], ["o_u"])

            def f3():
                c.stt(o_t[:], o_u[:], l2[:, 5:6], o_t[:], ALU.mult, ALU.add, ["o_u", "o_t", "l2"], ["o_t"])
                c.tt("dve", o_u[:], o_t[:], o_t[:], ALU.mult, ["o_t", "o_u"], ["o_u"])
                c.mm(accs[:, 1024:1536], ones[:], o_u[:], True, True, ["ones_f", "o_u"], [("acc", 2)])

            def f4():
                c.act(o_u[:], accs[:, 1024:1536], AF.Ln, [("acc", 2), "eps", "o_u"], ["o_u"], scale=1.0 / 128,
                      bias=EPS_AP[0])
                c.act(o_u[:], o_u[:], AF.Exp, ["o_u"], ["o_u"], scale=-0.5)
                c.stt(yb[g2][:], o_t[:], sgc[:, 0:1], o_u[:], ALU.mult, ALU.mult, ["o_t", "o_u", "sgc"], [("ybo", g2)])
                j_, col0 = qg // 4, (qg % 4) * 512
                c.dma("sp", y_send[col0 // 1024][j_ * 128:(j_ + 1) * 128, (col0 % 1024):(col0 % 1024) + 512],
                      yb[g2][:], reads=[("ybo", g2)], writes=[("d_ys", col0 // 1024)])
            for k, f in enumerate((f0, f1, f2, f3, f4)):
                pending.append((u_now + k * gap, f))

        emit_scores(0)
        nu = len(units)
        for u in range(nu):
            if u + 1 < nu:
                emit_scores(u + 1)
            emit_pv(u)
            qg, kb = units[u]
            if kb == 4 * qg + 3:
                finalize(qg, u, 2 if u + 1 < nu else 0)
            while pending and pending[0][0] <= u:
                pending.pop(0)[1]()
        while pending:
            pending.pop(0)[1]()


def emit_back(c, nc, io):
    (x_ext, yaT_in, zs_in, yloc_in, eacg_in, CT_in, st_g, mlt_in, mk_in, y_g, gT_in, ng_in, w_a, w_b, w_c, w_o,
     idx_in, xm_o, halo_out) = (io[k] for k in (
        "x_ext", "yaT", "zs", "yloc", "eacg", "CT", "st_g", "mlt", "mk", "y_g", "gT", "ng", "w_a", "w_b", "w_c",
        "w_o", "idx", "x_mid", "halo_out"))
    x_in = x_ext[128:, :]

    def wview(w):
        return w.rearrange("(k p) n -> p k n", p=128)
    with Section(c):
        ident = setup_consts(c)
        banks = [c.ps(f"bank{i}", [128, 512], F32) for i in range(8)]
        BK = lambda i: ("bank", i)
        H_bf = c.sb("H_bf", [128, 1024], BF16)
        idx = c.sb("idx", [128, 4], mybir.dt.int32)
        c.dma("sp", idx[:], idx_in, writes=["idx"])
        ybT = c.sb("ybT", [128, 8, T], BF16)
        ycT = c.sb("ycT", [128, 4, T], BF16)
        yaT = c.sb("yaT", [128, 4, T], BF16)
        c.dma("sp", yaT[:], yaT_in.rearrange("(g p) t -> p g t", p=128), writes=["yaT"])
        wa = c.sb("wa", [128, 4, D], BF16)
        wb = c.sb("wb", [128, 8, D], BF16)
        wc = c.sb("wc", [128, 4, D], BF16)
        wo = c.sb("wo", [128, 8, D], BF16)
        for t_, w_, k_ in ((wa, w_a, "wa"), (wb, w_b, "wb"), (wc, w_c, "wc"), (wo, w_o, "wo")):
            c.dma("pool", t_[:], wview(w_), writes=[k_])
        with Section(c, keep=True):
            sf = c.sb("sf", [128, 4, 1024], F32)
            ld = c.sb("ld", [128, 4, 16], F32)
            mlt = c.sb("mlt", [128, 4], F32)
            mk = c.sb("mk", [128, 16], F32)
            stv = st_g.rearrange("(r p) n -> p r n", p=128)
            c.dma("sp", sf[:], stv[:, :, 0:1024], reads=["g_st"], writes=["sf"])
            c.dma("sp", ld[:], stv[:, :, 1024:1040], reads=["g_st"], writes=["ld"])
            c.dma("sp", mlt[:], mlt_in, writes=["mlt"])
            c.dma("sp", mk[:], mk_in, writes=["mk"])
            H = c.sb("H", [128, 1024], F32)
            Ht = c.sb("Ht", [128, 1024], F32)
            e = c.sb("e", [128, 16], F32)
            c.op("pool", lambda E: E.memset(H[:], 0.0), writes=["H"])
            for i in range(4):
                c.op("pool", lambda E: E.memset(e[:], 0.0), writes=["e"])
                for k in range(4):
                    c.stt(e[:], ld[:, k, :], mk[:, i * 4 + k:i * 4 + k + 1], e[:], ALU.mult, ALU.add,
                          ["ld", "mk", "e"], ["e"])
                c.act(e[:], e[:], AF.Exp, ["e"], ["e"])
                c.ts("dve", e[:], e[:], mlt[:, i:i + 1], None, ALU.mult, None, ["e", "mlt"], ["e"])
                c.tt("dve", Ht[:].rearrange("p (h d) -> p h d", h=16), sf[:, i, :].rearrange("p (h d) -> p h d", h=16),
                     e[:, :].unsqueeze(2).to_broadcast([128, 16, 64]), ALU.mult, ["sf", "e"], ["Ht"])
                c.tt("dve", H[:], H[:], Ht[:], ALU.add, ["H", "Ht"], ["H"])
            c.copy("act", H_bf[:], H[:], ["H"], ["H_bf"])
        for h in range(4):
            for pc in range(2):
                c.op16("pool", lambda E: E.indirect_dma_start(
                    out=ycT[:, h, pc * 1024:(pc + 1) * 1024], out_offset=None,
                    in_=y_g[pc], in_offset=bass.IndirectOffsetOnAxis(ap=idx[:, h:h + 1], axis=0)),
                    ["idx", ("g_y", pc)], [("ycT", h)])
        with Section(c):
            ng = c.sb("ng", [128, 8], F32)
            c.dma("sp", ng[:], ng_in, writes=["ng"])
            CT = c.sb("CT", [128, 2, T], BF16)
            c.dma("sp", CT[:], CT_in.rearrange("(g p) t -> p g t", p=128), writes=["CT"])
            eacg = c.sb("eacg", [128, NCH, 16], F32)
            c.dma("sp", eacg[:], eacg_in.rearrange("(c p) h -> p c h", p=128), writes=["eacg"])
            yl = [c.sb(f"yl{i}", [128, 1024], F32) for i in range(2)]
            zt = [c.sb(f"zt{i}", [128, 1024], BF16) for i in range(2)]
            yt = [c.sb(f"yt{i}", [128, 1024], F32) for i in range(2)]
            yb = [c.sb(f"yb{i}", [128, 1024], BF16) for i in range(2)]
            yj = c.sb("yj", [128, 1024], BF16)
            yss = [c.sb(f"yss{i}", [128, 4], F32) for i in range(2)]
            pT = banks[7][:].bitcast(BF16)
            pT2 = banks[6][:].bitcast(BF16)
            def b_stage1(ch):
                s = ch % 2
                rs = slice(ch * 128, (ch + 1) * 128)
                c.dma("sp", yl[s][:], yloc_in[rs, :], writes=[("yl", s)])
                c.dma("sp", zt[s][:], zs_in[rs, :], writes=[("zt", s)])
                for g in range(2):
                    c.mm(banks[g][:], CT[:, g, rs], H_bf[:, g * 512:(g + 1) * 512], True, True, ["CT", "H_bf"], [BK(g)])
                    hs = slice(g * 512, (g + 1) * 512)
                    c.tt("dve", yt[s][:, hs].rearrange("p (h d) -> p h d", h=8),
                         banks[g][:].rearrange("p (h d) -> p h d", h=8),
                         eacg[:, ch, g * 8:(g + 1) * 8].unsqueeze(2).to_broadcast([128, 8, 64]), ALU.mult,
                         [BK(g), "eacg"], [("yt", s, g)])
                ytk = [("yt", s, 0), ("yt", s, 1)]
                c.tt("dve", yt[s][:], yt[s][:], yl[s][:], ALU.add, ytk + [("yl", s)], ytk)
                c.tt("dve", yt[s][:], yt[s][:], zt[s][:], ALU.mult, ytk + [("zt", s)], ytk)
                c.act(yj[:], yt[s][:], AF.Square, ytk, ["yj", ("yss", s)], accum_out=yss[s][:, 0:1])
                rstd_from_ss(c, yss[s], ("yss", s), 1.0 / 1024)

            def b_stage2(ch):
                s = ch % 2
                rs = slice(ch * 128, (ch + 1) * 128)
                ytk = [("yt", s, 0), ("yt", s, 1)]
                c.act(yb[s][:], yt[s][:], AF.Copy, ytk + [("yss", s)], [("yb", s)], scale=yss[s][:, 2:3])
                for k in range(8):
                    c.tr(pT[:, k * 128:(k + 1) * 128], yb[s][:, k * 128:(k + 1) * 128], ident[:],
                         [("yb", s), "ident"], [BK(7)])
                c.tt("dve", ybT[:, :, rs], pT.rearrange("p (k t) -> p k t", k=8),
                     ng[:, :].unsqueeze(2).to_broadcast([128, 8, 128]), ALU.mult, [BK(7), "ng"], [("ybT", ch)])

            b_stage1(0)
            for ch in range(NCH):
                if ch + 1 < NCH:
                    b_stage1(ch + 1)
                b_stage2(ch)
        release_kept(c)
        with Section(c):
            ally = [("ybT", ch) for ch in range(NCH)] + [("ycT", h) for h in range(4)] + ["yaT"]
            gts = [c.sb(f"gts{i}", [128, 3, 512], BF16) for i in range(2)]
            m1 = [c.sb(f"m1{i}", [128, 512], F32) for i in range(2)]
            m2 = [c.sb(f"m2{i}", [128, 512], F32) for i in range(2)]
            m3 = [c.sb(f"m3{i}", [128, 512], F32) for i in range(2)]
            mT = [c.sb(f"mT{i}", [128, 8, 512], BF16) for i in range(2)]
            xr = [c.sb(f"xr{i}", [128, D], F32) for i in range(2)]
            xo = [c.sb(f"xo{i}", [128, D], F32) for i in range(2)]
            gview = gT_in.rearrange("(b o p) t -> p b o t", b=3, p=128)
            un = 0
            for tl in range(4):
                ts_ = slice(tl * 512, (tl + 1) * 512)
                mt = mT[tl % 2]
                for ob in range(8):
                    s = un % 2
                    un += 1
                    os_ = slice(ob * 128, (ob + 1) * 128)
                    c.dma("sp", gts[s][:], gview[:, :, ob, ts_], writes=[("gts", s)])
                    for k in range(4):
                        c.mm(banks[0][:], wa[:, k, os_], yaT[:, k, ts_], k == 0, k == 3, ["wa"] + ally, [BK(0)])
                    for k in range(8):
                        c.mm(banks[1][:], wb[:, k, os_], ybT[:, k, ts_], k == 0, k == 7, ["wb"] + ally, [BK(1)])
                    for k in range(4):
                        c.mm(banks[2][:], wc[:, k, os_], ycT[:, k, ts_], k == 0, k == 3, ["wc"] + ally, [BK(2)])
                    c.tt("dve", m1[s][:], banks[0][:], gts[s][:, 0, :], ALU.mult, [BK(0), ("gts", s)], [("m1", s)])
                    c.tt("dve", m2[s][:], banks[1][:], gts[s][:, 1, :], ALU.mult, [BK(1), ("gts", s)], [("m2", s)])
                    c.tt("dve", m3[s][:], banks[2][:], gts[s][:, 2, :], ALU.mult, [BK(2), ("gts", s)], [("m3", s)])
                    c.tt("dve", m1[s][:], m1[s][:], m2[s][:], ALU.add, [("m1", s), ("m2", s)], [("m1", s)])
                    c.tt("dve", mt[:, ob, :], m1[s][:], m3[s][:], ALU.add, [("m1", s), ("m3", s)], [("mT", tl % 2, ob)])
                allm = [("mT", tl % 2, ob) for ob in range(8)]
                for cc in range(4):
                    gch = tl * 4 + cc
                    s2 = gch % 2
                    c.dma("sp", xr[s2][:], x_in[gch * 128:(gch + 1) * 128, :], writes=[("xr", s2)])
                    for hf in range(2):
                        pd = banks[4 + hf]
                        for k in range(8):
                            c.mm(pd[:], mt[:, k, cc * 128:(cc + 1) * 128], wo[:, k, hf * 512:(hf + 1) * 512],
                                 k == 0, k == 7, allm + ["wo"], [BK(4 + hf)])
                        c.tt("dve", xo[s2][:, hf * 512:(hf + 1) * 512], pd[:], xr[s2][:, hf * 512:(hf + 1) * 512],
                             ALU.add, [BK(4 + hf), ("xr", s2)], [("xo", s2)])
                    c.dma("sp", xm_o[gch * 128:(gch + 1) * 128, :], xo[s2][:], reads=[("xo", s2)], writes=["d_xm"])
                    if gch == NCH - 1:
                        c.dma("sp", halo_out, xo[s2][120:128, :], reads=[("xo", s2)], writes=["d_halo"])


def halo_select(c, nc, halo_g, selp_in, dst_rows, dkey):
    with Section(c):
        hg = c.sb("hg", [8, 4, D], F32)
        sel = c.sb("sel", [8, 4], F32)
        acc = c.sb("hacc", [8, D], F32)
        c.dma("sp", hg[:], halo_g.rearrange("(r p) n -> p r n", p=8), reads=["g_halo"], writes=["hg"])
        c.dma("sp", sel[:], selp_in[0:8, :], writes=["sel"])
        c.ts("dve", acc[:], hg[:, 0, :], sel[:, 0:1], None, ALU.mult, None, ["hg", "sel"], ["hacc"])
        for r in range(1, 4):
            c.stt(acc[:], hg[:, r, :], sel[:, r:r + 1], acc[:], ALU.mult, ALU.add, ["hg", "sel", "hacc"], ["hacc"])
        c.dma("sp", dst_rows, acc[:], reads=["hacc"], writes=[dkey])


LAYER_IN = [
    ("g_a", [128, 8]), ("w_u", [D, 512]), ("w_v", [D, 512]), ("w_z", [D, 1024]), ("w_xbc", [D, 1536]),
    ("w_dt", [D, 16]), ("w_q", [D, 512]), ("w_k", [D, 512]), ("w_v2", [D, 512]), ("w_g", [D, 3072]),
    ("vgain", [1, 512]), ("wsT", [128, 4, 128]), ("bs", [1, 512]), ("scw", [128, 12, 4]), ("scb", [128, 12]),
    ("dtb", [1, 16]), ("alog", [1, 16]), ("dsk", [1, 16]), ("qg", [1, 64]), ("kg", [1, 64]),
    ("lam", [1, 256]), ("sgc", [128, 1]), ("ng", [128, 8]),
    ("w_a", [512, D]), ("w_b", [D, D]), ("w_c", [512, D]), ("w_o", [D, D]),
    ("g_f", [128, 8]), ("w_up", [D, 2 * FFN]), ("fcw", [128, 44, 3]), ("fcb", [128, 44]), ("w_dn", [FFN, D]),
]


def build_fused(nl=2):
    nc = bass.Bass("TRN2", target_bir_lowering=False)
    I32 = mybir.dt.int32

    def din(name, shape, dt=F32):
        return nc.dram_tensor(name, shape, dt, kind="ExternalInput").ap()

    def scr(name, shape, dt):
        return nc.dram_tensor(name, shape, dt, kind="Internal").ap()
    x_ext0 = din("x_ext0", [(NCH + 1) * 128, D])
    L = [{k: din(f"{k}_{l}", sh) for k, sh in LAYER_IN} for l in range(nl)]
    cos_in, sin_in = din("cos", [T, 8]), din("sin", [T, 8])
    idx_in = din("idx", [128, 4], I32)
    mlt_in, mk_in, selp_in = din("mlt", [128, 4]), din("mk", [128, 16]), din("selp", [128, 4])
    x_out = nc.dram_tensor("x_out", [T, D], F32, kind="ExternalOutput").ap()
    yaT, zs, yloc = scr("s_yaT", [512, T], BF16), scr("s_zs", [T, 1024], BF16), scr("s_yloc", [T, 1024], F32)
    eacg, CT, gT = scr("s_eacg", [T, 16], F32), scr("s_CT", [256, T], BF16), scr("s_gT", [3072, T], BF16)
    st_send, st_g = scr("s_st", [128, 1040], F32), scr("g_st", [512, 1040], F32)
    q_send = [scr(f"s_q{i}", [512, 1024], BF16) for i in range(2)]
    k_send = [scr(f"s_k{i}", [512, 1024], BF16) for i in range(2)]
    v_send = [scr(f"s_v{i}", [512, VPW[i] * 130], BF16) for i in range(3)]
    q_g = [scr(f"g_q{i}", [2048, 1024], BF16) for i in range(2)]
    k_g = [scr(f"g_k{i}", [2048, 1024], BF16) for i in range(2)]
    v_g = [scr(f"g_v{i}", [2048, VPW[i] * 130], BF16) for i in range(3)]
    y_send = [scr(f"s_y{i}", [512, 1024], BF16) for i in range(2)]
    y_g = [scr(f"g_y{i}", [2048, 1024], BF16) for i in range(2)]
    halo_send, halo_g = scr("s_halo", [8, D], F32), scr("g_halo", [32, D], F32)
    xm_ext = scr("xm_ext", [(NCH + 1) * 128, D], F32)
    x1_ext = scr("x1_ext", [(NCH + 1) * 128, D], F32)
    with ExitStack() as es:
        c = Ctx(nc, es)
        with Section(c):
            zt = c.sb("zt", [128, D], F32)
            c.op("pool", lambda E: E.memset(zt[:], 0.0), writes=["zt"])
            c.dma("sp", xm_ext[0:128, :], zt[:], reads=["zt"], writes=["d_xm"])
            c.dma("sp", x1_ext[0:128, :], zt[:], reads=["zt"], writes=["d_xout"])
        for l in range(nl):
            W = L[l]
            x_ext = x_ext0 if l == 0 else x1_ext
            last = l == nl - 1
            def ag_qkv():
                for pc in range(2):
                    c.allgather(q_send[pc], q_g[pc], [("d_qs", pc)], [("g_q", pc)])
                    c.allgather(k_send[pc], k_g[pc], [("d_ks", pc)], [("g_k", pc)])
                for vp in range(3):
                    c.allgather(v_send[vp], v_g[vp], [("d_vs", vp)], [("g_v", vp)])
            emit_front(c, nc, {
                "x_ext": x_ext, "g": W["g_a"], "w_u": W["w_u"], "w_v": W["w_v"], "w_z": W["w_z"],
                "w_xbc": W["w_xbc"], "w_dt": W["w_dt"], "w_q": W["w_q"], "w_k": W["w_k"], "w_v2": W["w_v2"],
                "w_g": W["w_g"], "vgain": W["vgain"], "wsT": W["wsT"], "bs": W["bs"], "cw": W["scw"],
                "cb": W["scb"], "dtb": W["dtb"], "alog": W["alog"], "dsk": W["dsk"], "qg": W["qg"], "kg": W["kg"],
                "cos": cos_in, "sin": sin_in, "yaT": yaT, "zs": zs, "yloc": yloc, "eacg": eacg, "CT": CT, "gT": gT,
                "st_send": st_send, "q_send": q_send, "k_send": k_send, "v_send": v_send,
                "after_qkv": ag_qkv})
            c.new_epoch()
            c.allgather(st_send, st_g, ["d_st"], ["g_st"])
            emit_attn(c, nc, {"lam": W["lam"], "sgc": W["sgc"], "idx": idx_in, "q_g": q_g, "k_g": k_g, "v_g": v_g,
                              "y_send": y_send}, l)
            c.new_epoch()
            for pc in range(2):
                c.allgather(y_send[pc], y_g[pc], [("d_ys", pc)], [("g_y", pc)])
            emit_back(c, nc, {"x_ext": x_ext, "yaT": yaT, "zs": zs, "yloc": yloc, "eacg": eacg, "CT": CT,
                              "st_g": st_g, "mlt": mlt_in, "mk": mk_in, "y_g": y_g, "gT": gT, "ng": W["ng"],
                              "w_a": W["w_a"], "w_b": W["w_b"], "w_c": W["w_c"], "w_o": W["w_o"], "idx": idx_in,
                              "x_mid": xm_ext[128:, :], "halo_out": halo_send})
            c.new_epoch()
            c.allgather(halo_send, halo_g, ["d_halo"], ["g_halo"])
            halo_select(c, nc, halo_g, selp_in, xm_ext[120:128, :], "d_xm")
            emit_ffn(c, nc, {"x_ext": xm_ext, "g": W["g_f"], "w_up": W["w_up"], "cw": W["fcw"], "cb": W["fcb"],
                             "w_dn": W["w_dn"], "x_out": x_out if last else x1_ext[128:, :],
                             "halo_out": None if last else halo_send})
            c.new_epoch()
            if not last:
                c.allgather(halo_send, halo_g, ["d_halo"], ["g_halo"])
                halo_select(c, nc, halo_g, selp_in, x1_ext[120:128, :], "d_xout")
        c.barrier()
        c.finish()
    return nc


def fused_inputs(inp, core, cos, sin, nl=2):
    b, j = core // 4, core % 4
    ca = np.ascontiguousarray
    m = {"x_ext0": ext_rows(inp["x"], b, j),
         "cos": ca(cos[j * T:(j + 1) * T]), "sin": ca(sin[j * T:(j + 1) * T]),
         "idx": ca(np.array([[r * 512 + j * 128 + p for r in range(4)] for p in range(128)], np.int32))}
    mlt = np.zeros((128, 4), np.float32)
    mk = np.zeros((128, 4, 4), np.float32)
    selp = np.zeros((128, 4), np.float32)
    for i in range(4):
        if i < j:
            mlt[:, i] = 1.0
        if i == j - 1:
            selp[:, i] = 1.0
        for k in range(4):
            if i < k < j:
                mk[:, i, k] = 1.0
    m["mlt"], m["mk"], m["selp"] = mlt, ca(mk.reshape(128, 16)), selp
    for l in range(nl):
        w_in = inp["w_in"][l]
        sec = [ca(w_in[:, IN_OFFS[i]:IN_OFFS[i + 1]]) for i in range(9)]
        d = {
            "g_a": ca(inp["attn_norm_g"][l].reshape(8, 128).T),
            "w_u": sec[0], "w_v": sec[1], "w_z": sec[2], "w_xbc": sec[3], "w_dt": sec[4],
            "w_q": sec[5], "w_k": sec[6], "w_v2": sec[7], "w_g": sec[8],
            "vgain": ca(inp["gm_v_norm_g"][l].reshape(1, 512)),
            "wsT": ca(inp["gm_w_s"][l].transpose(2, 0, 1)),
            "bs": ca(inp["gm_b_s"][l].reshape(1, 512)),
            "scw": ca(inp["ssm_conv_w"][l].reshape(12, 128, 4).transpose(1, 0, 2)),
            "scb": ca(inp["ssm_conv_b"][l].reshape(12, 128).T),
            "dtb": ca(inp["ssm_dt_bias"][l].reshape(1, 16)),
            "alog": ca(inp["ssm_a_log"][l].reshape(1, 16)),
            "dsk": ca(inp["ssm_d"][l].reshape(1, 16)),
            "qg": ca(inp["da_q_norm_g"][l].reshape(1, 64)),
            "kg": ca(inp["da_k_norm_g"][l].reshape(1, 64)),
            "lam": ca(inp["da_lambda"][l].reshape(1, 256)),
            "sgc": ca(inp["da_subln_g"][l].reshape(128, 1)),
            "ng": ca(inp["ssm_norm_g"][l].reshape(8, 128).T),
            "w_a": ca(inp["w_branch_a"][l]), "w_b": ca(inp["w_branch_b"][l]),
            "w_c": ca(inp["w_branch_c"][l]), "w_o": ca(inp["w_out"][l]),
            "g_f": ca(inp["ffn_norm_g"][l].reshape(8, 128).T),
            "w_up": ca(inp["ffn_w_up"][l]),
            "fcw": ca(inp["ffn_conv_w"][l].reshape(44, 128, 3).transpose(1, 0, 2)),
            "fcb": ca(inp["ffn_conv_b"][l].reshape(44, 128).T),
            "w_dn": ca(inp["ffn_w_down"][l]),
        }
        for k, v in d.items():
            m[f"{k}_{l}"] = v
    return m


_NC_CACHE = {}


def kernel(**inp):
    inp = {k: np.asarray(v, dtype=np.float32) for k, v in inp.items()}
    cores = list(range(NCORE))
    cos, sin = rope_tables_np()
    if "nc" not in _NC_CACHE:
        _NC_CACHE["nc"] = build_fused(2)
    maps = [fused_inputs(inp, cc, cos, sin) for cc in cores]
    res = run_bass_kernel_spmd(_NC_CACHE["nc"], maps, core_ids=cores).results
    x = np.stack([np.concatenate([res[b * 4 + j]["x_out"] for j in range(4)], 0) for b in range(2)])
    return x.astype(np.float32)
```
